# Optimizing a Trainium2 kernel written in Bass

```python
import math
import jax, jax.numpy as jnp
from jax import lax
import numpy as np

D_MODEL = 4096
BATCH = 2
SEQ = 8192
DEPTH = 1

CHUNK = 64
N_META = 16
GLA_DK = 128
GLA_DV = 256
GLA_HEADS = D_MODEL // (2 * GLA_DK)
GLA_KEY_WIDTH = GLA_HEADS * GLA_DK
GLA_VAL_WIDTH = GLA_HEADS * GLA_DV
GATE_RANK = 16
GATE_TAU = 16.0
CONV_CH = D_MODEL
CONV_WIDTH = 31
N_BRANCH = 2
PEER_HEADS = 8
PEER_KEYS = 128
PEER_EXPERTS = PEER_KEYS * PEER_KEYS
PEER_QDIM = 256
PEER_HALF = PEER_QDIM // 2
PEER_TOPK = 16
PEER_BLOCK = 64
LN_EPS = 1e-5
RMS_EPS = 1e-6
DEEPNORM_ALPHA = (2.0 * DEPTH) ** 0.25
DEEPNORM_BETA = (8.0 * DEPTH) ** -0.25
OFF_Q = 0
OFF_K = OFF_Q + GLA_KEY_WIDTH
OFF_V = OFF_K + GLA_KEY_WIDTH
OFF_G = OFF_V + GLA_VAL_WIDTH
OFF_A = OFF_G + GLA_VAL_WIDTH
OFF_C = OFF_A + GATE_RANK
OFF_GATE = OFF_C + 2 * CONV_CH
IN_WIDTH = OFF_GATE + N_BRANCH * D_MODEL

kernel_name = 'hybrid_gla_conformer_peer_deepnorm'


def _layer_norm(x, g, b):
    xf = x.astype(jnp.float32)
    mu = jnp.mean(xf, axis=-1, keepdims=True)
    var = jnp.mean(jnp.square(xf - mu), axis=-1, keepdims=True)
    y = (xf - mu) * lax.rsqrt(var + LN_EPS)
    return (y * g.astype(jnp.float32) + b.astype(jnp.float32)).astype(x.dtype)


def _rms_norm(x, g):
    xf = x.astype(jnp.float32)
    y = xf * lax.rsqrt(jnp.mean(jnp.square(xf), axis=-1, keepdims=True) + RMS_EPS)
    return (y * g.astype(jnp.float32)).astype(x.dtype)


def _gla_chunk_causal(q, k, v, log_a):
    bsz, seqlen = q.shape[0], q.shape[1]
    pad = (-seqlen) % CHUNK
    n_chunks = (seqlen + pad) // CHUNK

    def to_chunks(t):
        t = jnp.pad(t.astype(jnp.float32), ((0, 0), (pad, 0), (0, 0), (0, 0)))
        t = t.reshape(bsz, n_chunks, CHUNK, t.shape[2], t.shape[3])
        return jnp.transpose(t, (1, 0, 3, 2, 4))

    qc, kc, vc, lac = (to_chunks(t) for t in (q, k, v, log_a))
    cum = jnp.cumsum(lac, axis=3)
    tot = cum[:, :, :, -1:, :]
    k_dec = kc * jnp.exp(tot - cum)
    chunk_decay = jnp.exp(tot[:, :, :, 0, :])

    def step(state, inp):
        q_i, k_i, v_i, d_i = inp
        state = d_i[..., None] * state + jnp.einsum('bhck,bhcv->bhkv', k_i, v_i)
        return state, jnp.einsum('bhck,bhkv->bhcv', q_i, state)

    s0 = jnp.zeros((bsz, GLA_HEADS, GLA_DK, GLA_DV), jnp.float32)
    _, o = lax.scan(step, s0, (qc, k_dec, vc, chunk_decay))
    o = jnp.transpose(o, (1, 0, 3, 2, 4)).reshape(bsz, n_chunks * CHUNK, GLA_HEADS, GLA_DV)
    return o[:, pad:]


def _causal_depthwise_conv(c, w, b):
    rhs = w.astype(c.dtype)[:, None, :]
    y = lax.conv_general_dilated(c, rhs, window_strides=(1,), padding=[(CONV_WIDTH - 1, 0)],
                                 dimension_numbers=('NWC', 'WIO', 'NWC'),
                                 feature_group_count=c.shape[-1])
    return y + b.astype(c.dtype)


def _mixer(h, w_in, w_a2, b_a, gla_norm_g, w_gla_o, conv_w, conv_b, conv_ln_g, conv_ln_b,
           w_conv_o, b_conv_o, w_out):
    bsz, seqlen, _ = h.shape

    def proj(off, size):
        return h @ w_in[:, off:off + size]

    q = proj(OFF_Q, GLA_KEY_WIDTH).reshape(bsz, seqlen, GLA_HEADS, GLA_DK) * (GLA_DK ** -0.5)
    k = proj(OFF_K, GLA_KEY_WIDTH).reshape(bsz, seqlen, GLA_HEADS, GLA_DK)
    v = proj(OFF_V, GLA_VAL_WIDTH).reshape(bsz, seqlen, GLA_HEADS, GLA_DV)
    g = proj(OFF_G, GLA_VAL_WIDTH)
    a_lr = proj(OFF_A, GATE_RANK)
    log_a = jax.nn.log_sigmoid((a_lr @ w_a2 + b_a).astype(jnp.float32)) / GATE_TAU
    log_a = log_a.reshape(bsz, seqlen, GLA_HEADS, GLA_DK)
    o = _gla_chunk_causal(q, k, v, log_a)
    o = _rms_norm(o, gla_norm_g).reshape(bsz, seqlen, GLA_VAL_WIDTH).astype(h.dtype)
    y_gla = (o * jax.nn.silu(g)) @ w_gla_o

    c = proj(OFF_C, CONV_CH) * jax.nn.sigmoid(proj(OFF_C + CONV_CH, CONV_CH))
    c = _causal_depthwise_conv(c, conv_w, conv_b)
    c = jax.nn.silu(_layer_norm(c, conv_ln_g, conv_ln_b))
    y_conv = c @ w_conv_o + b_conv_o

    gate_gla = jax.nn.sigmoid(proj(OFF_GATE, D_MODEL))
    gate_conv = jax.nn.sigmoid(proj(OFF_GATE + D_MODEL, D_MODEL))
    return (gate_gla * y_gla + gate_conv * y_conv) @ w_out


def _peer(h, wq, keys, u_tab, v_tab):
    bsz, seqlen, d = h.shape
    n_tok = bsz * seqlen
    xt = h.reshape(n_tok, d)
    q = (xt @ wq).reshape(n_tok, PEER_HEADS, 2, PEER_HALF)
    scores = jnp.einsum('tphd,phnd->tphn', q, keys).astype(jnp.float32)
    sub_val, sub_idx = lax.top_k(scores, PEER_TOPK)
    cand = (sub_val[:, :, 0, :, None] + sub_val[:, :, 1, None, :]).reshape(n_tok, PEER_HEADS, PEER_TOPK * PEER_TOPK)
    cand_idx = (sub_idx[:, :, 0, :, None] * PEER_KEYS + sub_idx[:, :, 1, None, :]).reshape(n_tok, PEER_HEADS, PEER_TOPK * PEER_TOPK)
    top_val, top_pos = lax.top_k(cand, PEER_TOPK)
    expert_idx = jnp.take_along_axis(cand_idx, top_pos, axis=-1).reshape(n_tok, PEER_HEADS * PEER_TOPK)
    gate = jax.nn.softmax(top_val, axis=-1).reshape(n_tok, PEER_HEADS * PEER_TOPK).astype(h.dtype)

    pad = (-n_tok) % PEER_BLOCK
    xt_p = jnp.pad(xt, ((0, pad), (0, 0))).reshape(-1, PEER_BLOCK, d)
    idx_p = jnp.pad(expert_idx, ((0, pad), (0, 0))).reshape(-1, PEER_BLOCK, PEER_HEADS * PEER_TOPK)
    gate_p = jnp.pad(gate, ((0, pad), (0, 0))).reshape(-1, PEER_BLOCK, PEER_HEADS * PEER_TOPK)

    def block(args):
        xb, ib, gb = args
        ub = jnp.take(u_tab, ib, axis=0)
        act = jax.nn.gelu(jnp.einsum('td,ted->te', xb, ub).astype(jnp.float32), approximate=False).astype(xb.dtype)
        vb = jnp.take(v_tab, ib, axis=0)
        return jnp.einsum('te,ted->td', gb * act, vb)

    out = lax.map(block, (xt_p, idx_p, gate_p)).reshape(-1, d)[:n_tok]
    return out.reshape(bsz, seqlen, d)


def setup_inputs(seed: int = 0) -> dict:
    key = jax.random.key(seed)
    ks = jax.random.split(key, 32)
    f32 = jnp.float32

    def nrm(k, shape, scale):
        return jax.random.normal(k, shape, f32) * scale

    beta = DEEPNORM_BETA
    col_scale = jnp.concatenate([
        jnp.ones((2 * GLA_KEY_WIDTH,), f32),
        jnp.full((GLA_VAL_WIDTH,), beta, f32),
        jnp.ones((IN_WIDTH - OFF_G,), f32)])
    return {
        'x': nrm(ks[0], (BATCH, SEQ, D_MODEL), 1.0),
        'meta': nrm(ks[1], (N_META, D_MODEL), 1.0),
        'ln0_g': 1.0 + nrm(ks[2], (D_MODEL,), 0.02),
        'ln0_b': nrm(ks[3], (D_MODEL,), 0.02),
        'w_in': nrm(ks[4], (DEPTH, D_MODEL, IN_WIDTH), D_MODEL ** -0.5) * col_scale,
        'w_a2': nrm(ks[5], (DEPTH, GATE_RANK, GLA_KEY_WIDTH), GATE_RANK ** -0.5),
        'b_a': nrm(ks[6], (DEPTH, GLA_KEY_WIDTH), 0.1),
        'gla_norm_g': 1.0 + nrm(ks[7], (DEPTH, GLA_DV), 0.02),
        'w_gla_o': nrm(ks[8], (DEPTH, GLA_VAL_WIDTH, D_MODEL), beta * GLA_VAL_WIDTH ** -0.5),
        'conv_w': nrm(ks[9], (DEPTH, CONV_WIDTH, CONV_CH), CONV_WIDTH ** -0.5),
        'conv_b': nrm(ks[10], (DEPTH, CONV_CH), 0.02),
        'conv_ln_g': 1.0 + nrm(ks[11], (DEPTH, CONV_CH), 0.02),
        'conv_ln_b': nrm(ks[12], (DEPTH, CONV_CH), 0.02),
        'w_conv_o': nrm(ks[13], (DEPTH, CONV_CH, D_MODEL), beta * CONV_CH ** -0.5),
        'b_conv_o': nrm(ks[14], (DEPTH, D_MODEL), 0.02),
        'w_out': nrm(ks[15], (DEPTH, D_MODEL, D_MODEL), beta * D_MODEL ** -0.5),
        'ln1_g': 1.0 + nrm(ks[16], (DEPTH, D_MODEL), 0.02),
        'ln1_b': nrm(ks[17], (DEPTH, D_MODEL), 0.02),
        'peer_wq': nrm(ks[18], (DEPTH, D_MODEL, PEER_HEADS * PEER_QDIM), D_MODEL ** -0.5),
        'peer_keys': nrm(ks[19], (DEPTH, PEER_HEADS, 2, PEER_KEYS, PEER_HALF), PEER_HALF ** -0.5),
        'peer_u': nrm(ks[20], (DEPTH, PEER_EXPERTS, D_MODEL), D_MODEL ** -0.5),
        'peer_v': nrm(ks[21], (DEPTH, PEER_EXPERTS, D_MODEL), beta * PEER_HEADS ** -0.5),
        'ln2_g': 1.0 + nrm(ks[22], (DEPTH, D_MODEL), 0.02),
        'ln2_b': nrm(ks[23], (DEPTH, D_MODEL), 0.02),
    }


def reference(x, meta, ln0_g, ln0_b, w_in, w_a2, b_a, gla_norm_g, w_gla_o, conv_w, conv_b,
              conv_ln_g, conv_ln_b, w_conv_o, b_conv_o, w_out, ln1_g, ln1_b, peer_wq, peer_keys,
              peer_u, peer_v, ln2_g, ln2_b):
    bsz = x.shape[0]
    meta_b = jnp.broadcast_to(meta[None].astype(x.dtype), (bsz, N_META, D_MODEL))
    h = _layer_norm(jnp.concatenate([meta_b, x], axis=1), ln0_g, ln0_b)
    for l in range(DEPTH):
        mix = _mixer(h, w_in[l], w_a2[l], b_a[l], gla_norm_g[l], w_gla_o[l], conv_w[l], conv_b[l],
                     conv_ln_g[l], conv_ln_b[l], w_conv_o[l], b_conv_o[l], w_out[l])
        h = _layer_norm(DEEPNORM_ALPHA * h + mix, ln1_g[l], ln1_b[l])
        ffn = _peer(h, peer_wq[l], peer_keys[l], peer_u[l], peer_v[l])
        h = _layer_norm(DEEPNORM_ALPHA * h + ffn, ln2_g[l], ln2_b[l])
    return h[:, N_META:]
```

```python
import numpy as np
from contextlib import ExitStack
import concourse.bass as bass
import concourse.mybir as mybir
from concourse.bass_utils import run_bass_kernel_spmd

F32 = mybir.dt.float32
BF16 = mybir.dt.bfloat16
AF = mybir.ActivationFunctionType
ALU = mybir.AluOpType
AX = mybir.AxisListType

D = 4096
KC = 32
T = 256
NSUB = 2
SEG = 2048
NPRE_TILES = 25
NOWN_TILES = 8
OFF_Q, OFF_K, OFF_V, OFF_G, OFF_A, OFF_C, OFF_GATE = 0, 2048, 4096, 8192, 12288, 12304, 20496
IN_WIDTH = 28688
ALPHA = 2.0 ** 0.25
LN_EPS = 1e-5
RMS_EPS = 1e-6
NEXP = 16384
TOPK_EPS = 4e-6
NEG = -1.0e30

ENGS = ["pe", "act", "dve", "pool", "sp"]


class Op:
    __slots__ = ("eng", "fn", "deps", "dma", "idx", "sig", "sigcnt", "dcnt")

    def __init__(self, eng, fn, dma):
        self.eng, self.fn, self.dma = eng, fn, dma
        self.deps = []
        self.sig = False
        self.sigcnt = 0
        self.dcnt = 0


class Sched:
    def __init__(self):
        self.ops = {e: [] for e in ENGS}
        self.lastw = {}
        self.readers = {}
        self.dma_keys = {}

    def add(self, eng, fn, reads=(), writes=(), dma=None):
        op = Op(eng, fn, dma)
        deps = {}
        for r in reads:
            w = self.lastw.get(r)
            if w is not None:
                deps[id(w)] = w
        for wr in writes:
            w = self.lastw.get(wr)
            if w is not None:
                deps[id(w)] = w
            for rd in self.readers.get(wr, ()):
                deps[id(rd)] = rd
        deps.pop(id(op), None)
        op.deps = [d for d in deps.values()
                   if not (d.eng == "pe" and eng == "pe" and d.dma is None)]
        for r in reads:
            self.readers.setdefault(r, []).append(op)
        for wr in writes:
            self.lastw[wr] = op
            self.readers[wr] = []
        if dma is not None:
            self.dma_keys[dma] = self.dma_keys.get(dma, 0) + 1
            op.dcnt = self.dma_keys[dma]
        op.idx = len(self.ops[eng])
        self.ops[eng].append(op)
        return op

    def emit(self, nc, es, final_waits_eng="sp"):
        for e in ENGS:
            for op in self.ops[e]:
                for d in op.deps:
                    if d.dma is None:
                        d.sig = True
        esem = {e: es.enter_context(nc.semaphore("sem_" + e)) for e in ENGS}
        dsem = {k: es.enter_context(nc.semaphore("dsem_%d" % i)) for i, k in enumerate(self.dma_keys)}
        for e in ENGS:
            c = 0
            for op in self.ops[e]:
                if op.sig:
                    c += 1
                op.sigcnt = c
        block = es.enter_context(nc.Block())
        ops = self.ops
        final = [(dsem[k], 16 * n) for k, n in self.dma_keys.items() if isinstance(k, tuple) and k[0] == "out"]

        def body_for(e):
            def body(eng):
                waited = {}
                for op in ops[e]:
                    for d in op.deps:
                        if d.dma is not None:
                            sem, val = dsem[d.dma], 16 * d.dcnt
                        else:
                            sem, val = esem[d.eng], d.sigcnt
                        key = id(sem)
                        if waited.get(key, 0) >= val:
                            continue
                        waited[key] = val
                        eng.wait_ge(sem, val)
                    ins = op.fn(eng)
                    if op.dma is not None:
                        ins.then_inc(dsem[op.dma], 16)
                    elif op.sig:
                        ins.then_inc(esem[e], 1)
                if e == final_waits_eng:
                    for sem, val in final:
                        eng.wait_ge(sem, val)
            return body

        block.tensor(body_for("pe"))
        block.scalar(body_for("act"))
        block.vector(body_for("dve"))
        block.gpsimd(body_for("pool"))
        block.sync(body_for("sp"))


def build(n_pre=NPRE_TILES, n_own=NOWN_TILES, dbg=None, stage=9):
    nc = bass.Bass("TRN2", target_bir_lowering=False)
    ntiles = n_pre + n_own
    S = Sched()
    es = ExitStack()

    def dram(name, shape, dt=F32, kind="ExternalInput"):
        return nc.dram_tensor(name, list(shape), dt, kind=kind).ap()

    xs = dram("xs", [ntiles * T, D])
    maskt = dram("maskt", [128, ntiles * NSUB])
    maskrep = dram("maskrep", [128, T])
    w_in = dram("w_in", [D, IN_WIDTH])
    wa2 = dram("wa2", [17, 2048])
    gng = dram("gng", [128, 2])
    w_gla_o = dram("w_gla_o", [D, D])
    conv_w = dram("conv_w", [128, KC, 31])
    w_conv_o = dram("w_conv_o", [D, D])
    w_out = dram("w_out", [D, D])
    wq = dram("wq", [D, 2048])
    keysT = dram("keysT", [128, 16, 128])
    uT = dram("uT", [D, NEXP])
    vtab = dram("vtab", [NEXP, D])
    vecs = dram("vecs", [128, 11, KC])
    consts = dram("consts", [128, 4, 128])
    out = dram("out", [n_own * T, D], F32, kind="ExternalOutput")
    scrs = [nc.dram_tensor("scr%d" % i, [200, 128, 4096], BF16, kind="Internal").ap() for i in range(3)]

    class _Scr:
        def __getitem__(self, bid):
            return scrs[bid // 200][bid % 200]
    scr = _Scr()
    scr_ids = {}
    dbg_out = None
    if dbg:
        dbg_out = dram("dbg", [128, dbg["cols"]], F32, kind="ExternalOutput")

    def sb(name, shape, dt=F32):
        return es.enter_context(nc.sbuf_tensor(name, list(shape), dt))

    hT = sb("hT", [128, KC, T], BF16)
    B2f = sb("B2f", [128, D], F32)
    B2 = B2f[:].bitcast(BF16).rearrange("p (k t) -> p k t", k=KC)
    B3 = sb("B3", [128, KC, T], BF16)
    B4 = sb("B4", [128, KC, T], F32)
    xbuf = sb("xbuf", [128, D], F32)
    NSLOT = 3
    wsl = [sb("wsl%d" % i, [128, 16, 256], BF16) for i in range(NSLOT)]
    Sst = sb("Sst", [128, 16, 256], F32)
    Sb = sb("Sb", [128, 2, 256], BF16)
    vec = sb("vec", [128, 11, KC], F32)
    cst = sb("cst", [128, 4, 128], F32)
    identf = cst[:, 0, :]
    cstb = sb("cstb", [128, 4, 128], BF16)
    identb, onesb, ublk, cind = cstb[:, 0, :], cstb[:, 1, :], cstb[:, 2, :], cstb[:, 3, 0:4]
    onesf = sb("onesf", [128, 128], F32)
    mk = sb("mk", [128, ntiles * NSUB], F32)
    mkrep = sb("mkrep", [128, T], F32)
    wA = sb("wA", [128, KC, 16], BF16)
    wa2b = sb("wa2b", [32, 2048], BF16)
    gn = sb("gn", [128, 2], F32)
    cw = sb("cw", [128, KC, 31], F32)
    keyb = sb("keyb", [128, 16, 128], BF16)
    halo = sb("halo", [128, KC, 32], F32)
    stats = sb("stats", [128, 8, 6], F32)
    mv = sb("mv", [128, 2], F32)
    rs = sb("rs", [128, 2], F32)
    a_aug = sb("a_aug", [32, T], BF16)
    e1 = sb("e1", [128, 512], F32)
    lbuf = sb("lbuf", [128, NSUB, 256], BF16)
    erev = sb("erev", [128, NSUB, 256], F32)
    dTt = sb("dTt", [128, 16], F32)
    kdec = [sb("kdec%d" % i, [128, NSUB, 256], BF16) for i in range(2)]
    vbuf = [[sb("vbuf%d_%d" % (i, h), [128, NSUB, 256], BF16) for h in range(2)] for i in range(2)]
    qTb = [sb("qTb%d" % i, [128, 2, T], BF16) for i in range(2)]
    sqb = sb("sqb", [128, 2, T], BF16)
    lnv = sb("lnv", [128, T], F32)
    rstd = sb("rstd", [128, T], F32)
    mean = sb("mean", [128, T], F32)
    msq = sb("msq", [128, T], F32)
    tmpa = [sb("tmpa%d" % i, [128, T], F32) for i in range(2)]
    tmpb = [sb("tmpb%d" % i, [128, T], F32) for i in range(2)]
    cgb = [sb("cgb%d" % i, [128, 32 + T], F32) for i in range(2)]
    sc = xbuf[:].rearrange("p (s a h n) -> p s a h n", s=NSUB, a=2, h=8)
    scw = B2f[:, 0:2048].rearrange("p (a n) -> p a n", a=16)
    v16 = sb("v16", [128, 2, 8, 16], F32)
    cand = B2f[:, 2048:4096].rearrange("p (a n) -> p a n", a=8)
    t16 = sb("t16", [128, 8, 16], F32)
    e16 = sb("e16", [128, 8, 16], F32)
    zz = sb("zz", [128, 8], F32)
    mz = sb("mz", [128, 8], F32)
    taue = sb("taue", [128, 8], F32)
    thr = sb("thr", [128, NSUB, 8, 128], F32)
    etmp = [sb("etmp%d" % i, [128, 128], F32) for i in range(4)]
    wpb = [sb("wpb%d" % i, [128, 8, 128], BF16) for i in range(2)]
    pqT = [sb("pqT%d" % i, [128, 2, T], BF16) for i in range(2)]

    ps = [es.enter_context(nc.psum_tensor("ps%d" % i, [128, 512], F32)) for i in range(8)]

    def wview(w):
        return w.rearrange("(kc p) n -> p kc n", p=128)
    w_in_v, wgo_v, wco_v, wout_v, wq_v, uT_v = (wview(w) for w in (w_in, w_gla_o, w_conv_o, w_out, wq, uT))
    vt_v = vtab.rearrange("(q kc p) n -> q p kc n", p=128, kc=32)

    cnt = {"slot": 0, "acc": 0, "x": 0}

    def proj(wv, col0, rhs_t, mode, wname="w_in"):
        rhs_ap, rhs_key = rhs_t
        half = cnt["acc"] % 2
        cnt["acc"] += 1
        accs = [ps[2 * half][:, 0:256], ps[2 * half + 1][:, 0:256]]
        akeys = [("ps", 2 * half), ("ps", 2 * half + 1)]
        for kb in range(2):
            slot = cnt["slot"] % NSLOT
            cnt["slot"] += 1
            wt = wsl[slot]
            bkey = (wname, col0, kb)
            if bkey in scr_ids:
                bid = scr_ids[bkey]
                src = scr[bid].rearrange("p (k n) -> p k n", k=16)
                S.add("pool", (lambda eng, wt=wt, src=src: eng.dma_start(out=wt[:], in_=src)),
                      reads=[("scr", bid)], writes=[("wsl", slot)], dma=("wsl", slot))
            else:
                bid = len(scr_ids)
                scr_ids[bkey] = bid
                src = wv[:, kb * 16:(kb + 1) * 16, col0:col0 + 256]
                S.add("pool", (lambda eng, wt=wt, src=src: eng.dma_start(out=wt[:], in_=src)),
                      writes=[("wsl", slot)], dma=("wsl", slot))
                dst = scr[bid].rearrange("p (k n) -> p k n", k=16)
                S.add("sp", (lambda eng, wt=wt, dst=dst: eng.dma_start(out=dst, in_=wt[:])),
                      reads=[("wsl", slot)], writes=[("scr", bid)], dma=("scrst", bid % 16))

            def mm(eng, wt=wt, kb=kb):
                ins = None
                for a in range(2):
                    for k in range(16):
                        kc = kb * 16 + k
                        st = (kb == 0 and k == 0)
                        sp_ = (kb == 1 and k == 15)
                        if mode == "fm":
                            ins = eng.matmul(accs[a], lhsT=wt[:, k, a * 128:(a + 1) * 128], rhs=rhs_ap[:, kc, :],
                                             start=st, stop=sp_)
                        else:
                            ins = eng.matmul(accs[a], lhsT=rhs_ap[:, kc, a * 128:(a + 1) * 128], rhs=wt[:, k, :],
                                             start=st, stop=sp_)
                return ins
            S.add("pe", mm, reads=[("wsl", slot), rhs_key], writes=akeys)
        return accs, akeys

    dbg_col = [0]

    def dump(ap_f32, key, ncol):
        if dbg_out is None:
            return
        c0 = dbg_col[0]
        dbg_col[0] += ncol
        S.add("sp", (lambda eng: eng.dma_start(out=dbg_out[:, c0:c0 + ncol], in_=ap_f32)),
              reads=[key], dma=("out", "dbg"))

    S.add("sp", lambda e: e.dma_start(out=cst[:], in_=consts), writes=["cst"], dma=("ld", 0))
    S.add("sp", lambda e: e.dma_start(out=vec[:], in_=vecs), writes=["vec"], dma=("ld", 1))
    S.add("sp", lambda e: e.dma_start(out=mk[:], in_=maskt), writes=["mk"], dma=("ld", 2))
    S.add("sp", lambda e: e.dma_start(out=mkrep[:], in_=maskrep), writes=["mkrep"], dma=("ld", 3))
    S.add("sp", lambda e: e.dma_start(out=gn[:], in_=gng), writes=["gn"], dma=("ld", 4))
    S.add("sp", lambda e: e.dma_start(out=cw[:], in_=conv_w), writes=["cw"], dma=("ld", 5))
    S.add("pool", lambda e: e.dma_start(out=wA[:], in_=w_in_v[:, :, OFF_A:OFF_A + 16]), writes=["wA"], dma=("ld", 6))
    S.add("pool", lambda e: e.dma_start(out=wa2b[0:17, :], in_=wa2), writes=["wa2b"], dma=("ld", 7))
    S.add("pool", lambda e: e.dma_start(out=keyb[:], in_=keysT), writes=["keyb"], dma=("ld", 8))
    S.add("pool", lambda e: e.dma_start(out=cstb[:], in_=consts), writes=["cstb"], dma=("ld", 9))
    S.add("dve", lambda e: e.memset(Sst[:], 0.0), writes=[("S", h) for h in range(16)])
    S.add("dve", lambda e: e.memset(halo[:], 0.0), writes=["halo"])
    S.add("dve", lambda e: e.memset(a_aug[:], 1.0), writes=["a_aug"])
    S.add("dve", lambda e: e.memset(onesf[:], 1.0), writes=["onesf"])

    VG0, VB0, VCB, VCG, VCBB, VBCO, VG1, VB1, VG2, VB2 = range(10)
    alt = [0]

    def evac_engine():
        alt[0] ^= 1
        return "act" if alt[0] else "dve"

    def affine_evac(eng_name, out_ap, in_ap, scale_ap, bias_ap, reads, writes, func=None):
        if eng_name == "act" or func is not None:
            f = func if func is not None else AF.Identity
            S.add("act", lambda e: e.activation(out=out_ap, in_=in_ap, func=f, bias=bias_ap, scale=scale_ap),
                  reads=reads, writes=writes)
        else:
            S.add("dve", lambda e: e.tensor_scalar(out=out_ap, in0=in_ap, scalar1=scale_ap, scalar2=bias_ap,
                                                   op0=ALU.mult, op1=ALU.add), reads=reads, writes=writes)

    def ln_stats(src, src_key):
        def mm1(eng):
            ins = None
            for kc in range(KC):
                ins = eng.matmul(ps[5][:, 0:T], lhsT=onesf[:], rhs=src[:, kc, :], start=(kc == 0), stop=(kc == KC - 1))
            return ins
        S.add("pe", mm1, reads=[src_key, "onesf"], writes=[("ps", 5)])
        for kc in range(KC):
            tb = tmpa[kc % 2]
            S.add("act", lambda e, tb=tb, kc=kc: e.activation(out=tb[:], in_=src[:, kc, :], func=AF.Square),
                  reads=[src_key], writes=[("tmpa", kc % 2)])
            S.add("pe", lambda e, tb=tb, kc=kc: e.matmul(ps[5][:, T:2 * T], lhsT=onesf[:], rhs=tb[:],
                                                         start=(kc == 0), stop=(kc == KC - 1)),
                  reads=[("tmpa", kc % 2), "onesf"], writes=[("ps", 5)])
        S.add("dve", lambda e: e.tensor_scalar(out=mean[:], in0=ps[5][:, 0:T], scalar1=1.0 / D, scalar2=None,
                                               op0=ALU.mult), reads=[("ps", 5)], writes=["mean"])
        S.add("dve", lambda e: e.tensor_tensor(out=msq[:], in0=mean[:], in1=mean[:], op=ALU.mult),
              reads=["mean"], writes=["msq"])
        S.add("dve", lambda e: e.scalar_tensor_tensor(out=lnv[:], in0=ps[5][:, T:2 * T], scalar=1.0 / D, in1=msq[:],
                                                      op0=ALU.mult, op1=ALU.subtract),
              reads=[("ps", 5), "msq"], writes=["lnv"])
        S.add("dve", lambda e: e.tensor_scalar(out=lnv[:], in0=lnv[:], scalar1=LN_EPS, scalar2=None, op0=ALU.add),
              reads=["lnv"], writes=["lnv"])
        S.add("act", lambda e: e.activation(out=lnv[:], in_=lnv[:], func=AF.Ln), reads=["lnv"], writes=["lnv"])
        S.add("act", lambda e: e.activation(out=rstd[:], in_=lnv[:], func=AF.Exp, scale=-0.5),
              reads=["lnv"], writes=["rstd"])

    def ln_apply(src, src_key, kc, out_ap, out_key, gi, bi, func=None):
        tb = tmpb[kc % 2]
        S.add("dve", lambda e: e.tensor_tensor(out=tb[:], in0=src[:, kc, :], in1=mean[:], op=ALU.subtract),
              reads=[src_key, "mean"], writes=[("tmpb", kc % 2)])
        S.add("dve", lambda e: e.tensor_tensor(out=tb[:], in0=tb[:], in1=rstd[:], op=ALU.mult),
              reads=[("tmpb", kc % 2), "rstd"], writes=[("tmpb", kc % 2)])
        affine_evac("act", out_ap, tb[:], vec[:, gi, kc:kc + 1], vec[:, bi, kc:kc + 1],
                    reads=[("tmpb", kc % 2), "vec"], writes=[out_key], func=func)

    for ti in range(ntiles):
        own = ti >= n_pre
        last_pre = (ti == n_pre - 1)
        for s in range(NSUB):
            r0 = ti * T + s * 128
            xb, xk = (xbuf, "xbuf") if s == 0 else (B2f, "B2")
            S.add("sp", lambda e, r0=r0, xb=xb: e.dma_start(out=xb[:], in_=xs[r0:r0 + 128, :]),
                  writes=[xk], dma=("xbuf", s))

            def bst(e, xb=xb):
                ins = None
                for j in range(8):
                    ins = e.bn_stats(out=stats[:, j, :], in_=xb[:, j * 512:(j + 1) * 512])
                return ins
            S.add("dve", bst, reads=[xk], writes=["stats"])
            S.add("dve", lambda e: e.bn_aggr(out=mv[:], in_=stats[:].rearrange("p a b -> p (a b)")),
                  reads=["stats"], writes=["mv"])
            S.add("dve", lambda e: e.tensor_scalar(out=rs[:, 0:1], in0=mv[:, 1:2], scalar1=LN_EPS, scalar2=None,
                                                   op0=ALU.add), reads=["mv"], writes=["rs0"])
            S.add("act", lambda e: e.activation(out=rs[:, 0:1], in_=rs[:, 0:1], func=AF.Sqrt),
                  reads=["rs0"], writes=["rs0"])
            S.add("dve", lambda e: e.reciprocal(out=rs[:, 1:2], in_=rs[:, 0:1]), reads=["rs0"], writes=["rs1"])
            S.add("dve", lambda e, xb=xb: e.tensor_scalar(out=xb[:], in0=xb[:], scalar1=mv[:, 0:1], scalar2=rs[:, 1:2],
                                                   op0=ALU.subtract, op1=ALU.mult),
                  reads=[xk, "mv", "rs1"], writes=[xk])
            for g4 in range(8):
                bank = 4 + (g4 % 2)

                def tr(e, g4=g4, bank=bank, xb=xb):
                    ins = None
                    for q in range(4):
                        kc = g4 * 4 + q
                        ins = e.transpose(out=ps[bank][:, q * 128:(q + 1) * 128], in_=xb[:, kc * 128:(kc + 1) * 128],
                                          identity=identf)
                    return ins
                S.add("pe", tr, reads=[xk, "cst"], writes=[("ps", bank)])
                for q in range(4):
                    kc = g4 * 4 + q
                    affine_evac(evac_engine(), hT[:, kc, s * 128:(s + 1) * 128], ps[bank][:, q * 128:(q + 1) * 128],
                                vec[:, VG0, kc:kc + 1], vec[:, VB0, kc:kc + 1],
                                reads=[("ps", bank), "vec"], writes=["hT"])

        if stage == 1:
            if own:
                for kc in (0, 31):
                    S.add("dve", lambda e, kc=kc: e.tensor_copy(out=tmpa[0][:], in_=hT[:, kc, :]), reads=["hT"], writes=[("tmpa", 0)])
                    dump(tmpa[0][:], ("tmpa", 0), T)
            continue
        def mma(e):
            ins = None
            for kc in range(KC):
                ins = e.matmul(ps[4][0:16, 0:T], lhsT=wA[:, kc, :], rhs=hT[:, kc, :], start=(kc == 0), stop=(kc == KC - 1))
            return ins
        S.add("pe", mma, reads=["wA", "hT"], writes=[("ps", 4)])
        S.add("act", lambda e: e.activation(out=a_aug[0:16, :], in_=ps[4][0:16, 0:T], func=AF.Identity),
              reads=[("ps", 4)], writes=["a_aug"])

        def stA1(hp):
            def mmz(e, hp=hp):
                ins = None
                for s in range(NSUB):
                    ins = e.matmul(ps[4][:, s * 256:(s + 1) * 256], lhsT=a_aug[0:17, s * 128:(s + 1) * 128],
                                   rhs=wa2b[0:17, hp * 256:(hp + 1) * 256], start=True, stop=True)
                return ins
            S.add("pe", mmz, reads=["a_aug", "wa2b"], writes=[("ps", 4)])
            S.add("act", lambda e: e.activation(out=e1[:], in_=ps[4][:, :], func=AF.Exp, scale=-1.0),
                  reads=[("ps", 4)], writes=["e1"])
            S.add("act", lambda e: e.activation(out=lbuf[:].rearrange("p s c -> p (s c)"), in_=e1[:], func=AF.Ln, bias=1.0),
                  reads=["e1"], writes=["lbuf"])

        def stA2(hp):
            par = hp % 2

            def mmrev(e):
                ins = None
                for s in range(NSUB):
                    ins = e.matmul(ps[5][:, s * 256:(s + 1) * 256], lhsT=ublk, rhs=lbuf[:, s, :], start=True, stop=True)
                return ins
            S.add("pe", mmrev, reads=["lbuf", "cstb"], writes=[("ps", 5)])
            S.add("act", lambda e: e.activation(out=erev[:].rearrange("p s c -> p (s c)"), in_=ps[5][:, :], func=AF.Exp,
                                                scale=-1.0 / 16.0),
                  reads=[("ps", 5)], writes=["erev"])

            def mmtot(e):
                ins = None
                for h in range(2):
                    for s in range(NSUB):
                        c0 = (h * NSUB + s) * 2
                        ins = e.matmul(ps[4][:, c0:c0 + 2], lhsT=lbuf[:, s, h * 128:(h + 1) * 128],
                                       rhs=cind[:, 0:2], start=True, stop=True)
                return ins
            S.add("pe", mmtot, reads=["lbuf", "cstb"], writes=[("ps", 4)])
            S.add("act", lambda e: e.activation(out=dTt[:, par * 8:par * 8 + 8], in_=ps[4][:, 0:8], func=AF.Exp, scale=-1.0 / 16.0),
                  reads=[("ps", 4)], writes=[("dTt", par)])

        def stB(hp):
            par = hp % 2
            accs, akeys = proj(w_in_v, OFF_K + hp * 256, (hT, "hT"), "tm")
            for s in range(NSUB):
                S.add("dve", lambda e, s=s, a=accs[s]: e.tensor_tensor(out=kdec[par][:, s, :], in0=a, in1=erev[:, s, :], op=ALU.mult),
                      reads=[akeys[s], "erev"], writes=[("kdec", par, s)])
            for h in range(2):
                accs, akeys = proj(w_in_v, OFF_V + (hp * 2 + h) * 256, (hT, "hT"), "tm")
                for s in range(NSUB):
                    mcol = ti * NSUB + s
                    S.add("act", lambda e, s=s, h=h, a=accs[s], mcol=mcol: e.activation(
                        out=vbuf[par][h][:, s, :], in_=a, func=AF.Identity, scale=mk[:, mcol:mcol + 1]),
                        reads=[akeys[s], "mk"], writes=[("vbuf", par, h, s)])
            if own:
                accs, akeys = proj(w_in_v, OFF_Q + hp * 256, (hT, "hT"), "fm")
                for h in range(2):
                    S.add("dve", lambda e, h=h, a=accs[h]: e.tensor_scalar(out=qTb[par][:, h, :], in0=a, scalar1=128.0 ** -0.5,
                                                                         scalar2=None, op0=ALU.mult),
                          reads=[akeys[h]], writes=[("qTb", par, h)])

        def stC(hp):
            par = hp % 2
            order = [(h, c) for h in range(2) for c in range(4)] if own else [(h, c) for c in range(4) for h in range(2)]
            for (h, c) in order:
                s, hf = c // 2, c % 2
                rows = slice(hf * 64, hf * 64 + 64)
                hg = hp * 2 + h
                kvb = 6 if (own or h == 0) else 7
                kvp = ps[kvb][:, 0:256]
                S.add("pe", lambda e, h=h, s=s, rows=rows, kvp=kvp: e.matmul(
                    kvp, lhsT=kdec[par][rows, s, h * 128:(h + 1) * 128], rhs=vbuf[par][h][rows, s, :], start=True, stop=True),
                    reads=[("kdec", par, s), ("vbuf", par, h, s)], writes=[("ps", kvb)])
                S.add("dve", lambda e, h=h, hg=hg, c=c, kvp=kvp: e.scalar_tensor_tensor(
                    out=Sst[:, hg, :], in0=Sst[:, hg, :], scalar=dTt[:, par * 8 + h * 4 + c:par * 8 + h * 4 + c + 1], in1=kvp,
                    op0=ALU.mult, op1=ALU.add),
                    reads=[("S", hg), ("dTt", par), ("ps", kvb)], writes=[("S", hg)])
                if own:
                    S.add("act", lambda e, h=h, hg=hg: e.activation(out=Sb[:, h, :], in_=Sst[:, hg, :], func=AF.Identity),
                          reads=[("S", hg)], writes=[("Sb", h)])

                    def mmo(e, h=h, c=c):
                        ins = None
                        for dvs in range(2):
                            ins = e.matmul(ps[7][:, dvs * 256 + c * 64: dvs * 256 + c * 64 + 64],
                                           lhsT=Sb[:, h, dvs * 128:(dvs + 1) * 128], rhs=qTb[par][:, h, c * 64:(c + 1) * 64],
                                           start=True, stop=True)
                        return ins
                    S.add("pe", mmo, reads=[("Sb", h), ("qTb", par, h)], writes=[("ps", 7)])
                if own and c == 3:
                    S.add("act", lambda e, h=h: e.activation(out=sqb[:].rearrange("p a t -> p (a t)"), in_=ps[7][:, :],
                                                            func=AF.Square), reads=[("ps", 7)], writes=["sqb"])

                    def mms(e):
                        ins = None
                        for dvs in range(2):
                            ins = e.matmul(ps[5][:, 256:512], lhsT=onesb, rhs=sqb[:, dvs, :], start=(dvs == 0), stop=(dvs == 1))
                        return ins
                    S.add("pe", mms, reads=["sqb", "cstb"], writes=[("ps", 5)])
                    S.add("dve", lambda e: e.tensor_scalar(out=lnv[:], in0=ps[5][:, 256:512], scalar1=1.0 / 256.0,
                                                           scalar2=RMS_EPS, op0=ALU.mult, op1=ALU.add),
                          reads=[("ps", 5)], writes=["lnv"])
                    S.add("act", lambda e: e.activation(out=lnv[:], in_=lnv[:], func=AF.Ln), reads=["lnv"], writes=["lnv"])
                    S.add("act", lambda e: e.activation(out=rstd[:], in_=lnv[:], func=AF.Exp, scale=-0.5),
                          reads=["lnv"], writes=["rstd"])
                    accs, akeys = proj(w_in_v, OFF_G + hg * 256, (hT, "hT"), "fm")
                    for dvs in range(2):
                        S.add("act", lambda e, dvs=dvs, a=accs[dvs]: e.activation(out=tmpa[dvs][:], in_=a, func=AF.Silu),
                              reads=[akeys[dvs]], writes=[("tmpa", dvs)])
                        S.add("dve", lambda e, h=h, dvs=dvs: e.scalar_tensor_tensor(
                            out=tmpb[dvs][:], in0=ps[7][:, dvs * 256:(dvs + 1) * 256], scalar=gn[:, dvs:dvs + 1],
                            in1=rstd[:], op0=ALU.mult, op1=ALU.mult),
                            reads=[("ps", 7), "gn", "rstd"], writes=[("tmpb", dvs)])
                        S.add("dve", lambda e, hg=hg, dvs=dvs: e.tensor_tensor(
                            out=B2[:, hg * 2 + dvs, :], in0=tmpb[dvs][:], in1=tmpa[dvs][:], op=ALU.mult),
                            reads=[("tmpb", dvs), ("tmpa", dvs)], writes=["B2"])

        for hp in range(9):
            if hp < 8:
                stA1(hp)
            if hp >= 1:
                stC(hp - 1)
            if hp < 8:
                stA2(hp)
                stB(hp)

        if stage == 2:
            if own:
                for kc in (0, 1, 31):
                    S.add("dve", lambda e, kc=kc: e.tensor_copy(out=tmpa[0][:], in_=B2[:, kc, :]), reads=["B2"], writes=[("tmpa", 0)])
                    dump(tmpa[0][:], ("tmpa", 0), T)
            continue
        if not own and not last_pre:
            continue

        if own:
            for d2 in range(16):
                accy, ky = proj(wgo_v, d2 * 256, (B2, "B2"), "fm", "wgo")
                accg, kg = proj(w_in_v, OFF_GATE + d2 * 256, (hT, "hT"), "fm")
                for a in range(2):
                    dc = d2 * 2 + a
                    S.add("act", lambda e, a=a, ag=accg[a]: e.activation(out=tmpa[a][:], in_=ag, func=AF.Sigmoid),
                          reads=[kg[a]], writes=[("tmpa", a)])
                    S.add("dve", lambda e, a=a, dc=dc, ay=accy[a]: e.tensor_tensor(out=B3[:, dc, :], in0=ay, in1=tmpa[a][:],
                                                                                 op=ALU.mult),
                          reads=[ky[a], ("tmpa", a)], writes=["B3"])
        for cp in range(16):
            acca, ka = proj(w_in_v, OFF_C + cp * 256, (hT, "hT"), "fm")
            for a in range(2):
                S.add("act", lambda e, a=a, aa=acca[a]: e.activation(out=tmpa[a][:], in_=aa, func=AF.Identity),
                      reads=[ka[a]], writes=[("tmpa", a)])
            accb, kb_ = proj(w_in_v, OFF_C + D + cp * 256, (hT, "hT"), "fm")
            for a in range(2):
                kc = cp * 2 + a
                S.add("act", lambda e, a=a, ab=accb[a]: e.activation(out=tmpb[a][:], in_=ab, func=AF.Sigmoid),
                      reads=[kb_[a]], writes=[("tmpb", a)])
                S.add("dve", lambda e, a=a, kc=kc: e.tensor_copy(out=cgb[a][:, 0:32], in_=halo[:, kc, :]),
                      reads=["halo"], writes=[("cgb", a)])
                S.add("dve", lambda e, a=a: e.tensor_tensor(out=cgb[a][:, 32:32 + T], in0=tmpa[a][:], in1=tmpb[a][:], op=ALU.mult),
                      reads=[("tmpa", a), ("tmpb", a)], writes=[("cgb", a)])
                if last_pre:
                    S.add("dve", lambda e, a=a: e.tensor_tensor(out=cgb[a][:, 32:32 + T], in0=cgb[a][:, 32:32 + T], in1=mkrep[:],
                                                                op=ALU.mult), reads=[("cgb", a), "mkrep"], writes=[("cgb", a)])
                S.add("act", lambda e, a=a, kc=kc: e.activation(out=halo[:, kc, :], in_=cgb[a][:, T:T + 32], func=AF.Identity),
                      reads=[("cgb", a)], writes=["halo"])
            if own:
                for k in range(31):
                    for a in range(2):
                        kc = cp * 2 + a
                        if k == 0:
                            S.add("dve", lambda e, a=a, kc=kc: e.tensor_scalar(
                                out=B4[:, kc, :], in0=cgb[a][:, 2:2 + T], scalar1=cw[:, kc, 0:1], scalar2=vec[:, VCB, kc:kc + 1],
                                op0=ALU.mult, op1=ALU.add), reads=[("cgb", a), "cw", "vec"], writes=[("B4", kc)])
                        else:
                            S.add("dve", lambda e, a=a, kc=kc, k=k: e.scalar_tensor_tensor(
                                out=B4[:, kc, :], in0=cgb[a][:, 2 + k:2 + k + T], scalar=cw[:, kc, k:k + 1], in1=B4[:, kc, :],
                                op0=ALU.mult, op1=ALU.add), reads=[("cgb", a), "cw", ("B4", kc)], writes=[("B4", kc)])
        if not own:
            continue
        b4keys = [("B4", kc) for kc in range(KC)]
        S.add("dve", lambda e: e.engine_nop(), reads=b4keys, writes=["B4"])
        ln_stats(B4, "B4")
        for kc in range(KC):
            ln_apply(B4, "B4", kc, B2[:, kc, :], "B2", VCG, VCBB, func=AF.Silu)
        for d2 in range(16):
            accy, ky = proj(wco_v, d2 * 256, (B2, "B2"), "fm", "wco")
            accg, kg = proj(w_in_v, OFF_GATE + D + d2 * 256, (hT, "hT"), "fm")
            for a in range(2):
                dc = d2 * 2 + a
                S.add("act", lambda e, a=a, ag=accg[a]: e.activation(out=tmpa[a][:], in_=ag, func=AF.Sigmoid),
                      reads=[kg[a]], writes=[("tmpa", a)])
                S.add("dve", lambda e, a=a, dc=dc, ay=accy[a]: e.scalar_tensor_tensor(
                    out=tmpb[a][:], in0=ay, scalar=vec[:, VBCO, dc:dc + 1], in1=tmpa[a][:], op0=ALU.add, op1=ALU.mult),
                    reads=[ky[a], ("tmpa", a), "vec"], writes=[("tmpb", a)])
                S.add("dve", lambda e, a=a, dc=dc: e.tensor_tensor(out=B3[:, dc, :], in0=B3[:, dc, :], in1=tmpb[a][:], op=ALU.add),
                      reads=["B3", ("tmpb", a)], writes=["B3"])
        for d2 in range(16):
            accy, ky = proj(wout_v, d2 * 256, (B3, "B3"), "fm", "wout")
            for a in range(2):
                dc = d2 * 2 + a
                S.add("dve", lambda e, a=a, dc=dc, ay=accy[a]: e.scalar_tensor_tensor(
                    out=B4[:, dc, :], in0=hT[:, dc, :], scalar=ALPHA, in1=ay, op0=ALU.mult, op1=ALU.add),
                    reads=["hT", ky[a]], writes=["B4"])
        ln_stats(B4, "B4")
        for kc in range(KC):
            ln_apply(B4, "B4", kc, hT[:, kc, :], "hT", VG1, VB1)

        if stage == 3:
            for kc in (0, 31):
                S.add("dve", lambda e, kc=kc: e.tensor_copy(out=tmpa[0][:], in_=hT[:, kc, :]), reads=["hT"], writes=[("tmpa", 0)])
                dump(tmpa[0][:], ("tmpa", 0), T)
            continue
        for p in range(8):
            accq, kq = proj(wq_v, p * 256, (hT, "hT"), "fm", "wq")
            pq = pqT[p % 2]
            for a in range(2):
                S.add(evac_engine() if False else "act", lambda e, a=a, aq=accq[a], pq=pq: e.activation(out=pq[:, a, :], in_=aq, func=AF.Identity),
                      reads=[kq[a]], writes=[("pqT", p % 2, a)])

            def mmsc(e, p=p, pq=pq):
                ins = None
                for s in range(NSUB):
                    for hfq in range(2):
                        c0 = (s * 2 + hfq) * 128
                        ins = e.matmul(ps[4][:, c0:c0 + 128], lhsT=pq[:, hfq, s * 128:(s + 1) * 128],
                                       rhs=keyb[:, p * 2 + hfq, :], start=True, stop=True)
                return ins
            S.add("pe", mmsc, reads=[("pqT", p % 2, 0), ("pqT", p % 2, 1), "keyb"], writes=[("ps", 4)])
            for s in range(NSUB):
                S.add("act" if s == 0 else "dve", (lambda e, s=s, p=p: e.activation(
                    out=sc[:, s, :, p, :], in_=ps[4][:, s * 256:(s + 1) * 256].rearrange("q (a n) -> q a n", a=2), func=AF.Identity))
                    if s == 0 else (lambda e, s=s, p=p: e.tensor_copy(
                        out=sc[:, s, :, p, :], in_=ps[4][:, s * 256:(s + 1) * 256].rearrange("q (a n) -> q a n", a=2))),
                    reads=[("ps", 4)], writes=["xbuf"])
        for s in range(NSUB):
            def top_a(e, s=s):
                ins = None
                for hfq in range(2):
                    for p in range(8):
                        ins = e.max(out=v16[:, hfq, p, 0:8], in_=sc[:, s, hfq, p, :])
                return ins
            S.add("dve", top_a, reads=["xbuf"], writes=["v16a"])

            def top_b(e, s=s):
                ins = None
                for hfq in range(2):
                    for p in range(8):
                        ins = e.match_replace(out=scw[:, hfq * 8 + p, :], in_to_replace=v16[:, hfq, p, 0:8],
                                              in_values=sc[:, s, hfq, p, :], imm_value=NEG)
                return ins
            S.add("dve", top_b, reads=["xbuf", "v16a"], writes=["B2"])

            def top_c(e):
                ins = None
                for hfq in range(2):
                    for p in range(8):
                        ins = e.max(out=v16[:, hfq, p, 8:16], in_=scw[:, hfq * 8 + p, :])
                return ins
            S.add("dve", top_c, reads=["B2"], writes=["v16b"])

            def mkcand(e):
                ins = None
                for p in range(8):
                    ins = e.tensor_tensor(out=cand[:, p, :].rearrange("q (a b) -> q a b", a=16),
                                          in0=v16[:, 0, p, :].unsqueeze(2).broadcast_to([128, 16, 16]),
                                          in1=v16[:, 1, p, :].unsqueeze(1).broadcast_to([128, 16, 16]), op=ALU.add)
                return ins
            S.add("dve", mkcand, reads=["v16a", "v16b"], writes=["B2"])

            def ctop_a(e):
                ins = None
                for p in range(8):
                    ins = e.max(out=t16[:, p, 0:8], in_=cand[:, p, :])
                return ins
            S.add("dve", ctop_a, reads=["B2"], writes=["t16a"])

            def ctop_b(e):
                ins = None
                for p in range(8):
                    ins = e.match_replace(out=cand[:, p, :], in_to_replace=t16[:, p, 0:8], in_values=cand[:, p, :], imm_value=NEG)
                return ins
            S.add("dve", ctop_b, reads=["B2", "t16a"], writes=["B2"])

            def ctop_c(e):
                ins = None
                for p in range(8):
                    ins = e.max(out=t16[:, p, 8:16], in_=cand[:, p, :])
                return ins
            S.add("dve", ctop_c, reads=["B2"], writes=["t16b"])
            S.add("dve", lambda e: e.tensor_tensor(out=e16[:], in0=t16[:], in1=t16[:, :, 0:1].broadcast_to([128, 8, 16]),
                                                   op=ALU.subtract), reads=["t16a", "t16b"], writes=["e16"])
            S.add("act", lambda e: e.activation(out=e16[:], in_=e16[:], func=AF.Exp), reads=["e16"], writes=["e16"])
            S.add("dve", lambda e: e.tensor_reduce(out=zz[:], in_=e16[:], axis=AX.X, op=ALU.add), reads=["e16"], writes=["zz"])
            S.add("act", lambda e: e.activation(out=zz[:], in_=zz[:], func=AF.Ln), reads=["zz"], writes=["zz"])
            S.add("dve", lambda e: e.tensor_tensor(out=mz[:], in0=zz[:], in1=t16[:, :, 0], op=ALU.add),
                  reads=["zz", "t16a"], writes=["mz"])
            S.add("dve", lambda e: e.tensor_scalar(out=taue[:], in0=t16[:, :, 15], scalar1=-TOPK_EPS, scalar2=None, op0=ALU.add),
                  reads=["t16b"], writes=["taue"])
            S.add("dve", lambda e: e.tensor_tensor(out=taue[:], in0=taue[:], in1=mz[:], op=ALU.subtract),
                  reads=["taue", "mz"], writes=["taue"])
            S.add("dve", lambda e, s=s: e.tensor_tensor(out=thr[:, s, :, :], in0=taue[:].unsqueeze(2).broadcast_to([128, 8, 128]),
                                                        in1=sc[:, s, 0, :, :], op=ALU.subtract),
                  reads=["taue", "xbuf"], writes=[("thr", s)])
            S.add("dve", lambda e, s=s: e.tensor_tensor(out=sc[:, s, 1, :, :], in0=sc[:, s, 1, :, :],
                                                        in1=mz[:].unsqueeze(2).broadcast_to([128, 8, 128]), op=ALU.subtract),
                  reads=["mz", "xbuf"], writes=["xbuf"])
        ei = 0
        for qtr in range(4):
            for i2 in range(16):
                accu, ku = proj(uT_v, (qtr * 16 + i2) * 256, (hT, "hT"), "fm", "uT")
                for a in range(2):
                    i = (qtr * 16 + i2) * 2 + a
                    il = i2 * 2 + a
                    for s in range(NSUB):
                        wb = wpb[(i * NSUB + s) % 2]
                        wk = ("wpb", (i * NSUB + s) % 2)
                        for p in range(8):
                            et = etmp[ei % 4]
                            ek = ("etmp", ei % 4)
                            ei += 1
                            S.add("act", lambda e, et=et, s=s, p=p, i=i: e.activation(
                                out=et[:], in_=sc[:, s, 1, p, :], func=AF.Exp, bias=sc[:, s, 0, p, i:i + 1], scale=1.0),
                                reads=["xbuf"], writes=[ek])
                            S.add("dve", lambda e, et=et, wb=wb, s=s, p=p, i=i: e.scalar_tensor_tensor(
                                out=wb[:, p, :], in0=sc[:, s, 1, p, :], scalar=thr[:, s, p, i:i + 1], in1=et[:],
                                op0=ALU.is_ge, op1=ALU.mult), reads=["xbuf", ("thr", s), ek], writes=[wk])

                        def mmw(e, wb=wb, s=s):
                            ins = None
                            for p in range(8):
                                ins = e.matmul(ps[6][:, s * 128:(s + 1) * 128], lhsT=wb[:, p, :], rhs=identb,
                                               start=(p == 0), stop=(p == 7))
                            return ins
                        S.add("pe", mmw, reads=[wk, "cstb"], writes=[("ps", 6)])
                    S.add("act", lambda e, a=a, au=accu[a]: e.activation(out=tmpa[a][:], in_=au, func=AF.Gelu),
                          reads=[ku[a]], writes=[("tmpa", a)])
                    S.add("dve", lambda e, a=a, il=il: e.tensor_tensor(out=B3[:, il, :], in0=tmpa[a][:], in1=ps[6][:, 0:T], op=ALU.mult),
                          reads=[("tmpa", a), ("ps", 6)], writes=["B3"])
            for d2 in range(16):
                accv, kv_ = proj(vt_v[qtr], d2 * 256, (B3, "B3"), "fm", ("v", qtr))
                for a in range(2):
                    dc = d2 * 2 + a
                    if qtr == 0:
                        S.add("dve", lambda e, dc=dc, av=accv[a]: e.scalar_tensor_tensor(
                            out=B4[:, dc, :], in0=hT[:, dc, :], scalar=ALPHA, in1=av, op0=ALU.mult, op1=ALU.add),
                            reads=["hT", kv_[a]], writes=["B4"])
                    else:
                        S.add("dve", lambda e, dc=dc, av=accv[a]: e.tensor_tensor(out=B4[:, dc, :], in0=B4[:, dc, :], in1=av, op=ALU.add),
                              reads=["B4", kv_[a]], writes=["B4"])
        ln_stats(B4, "B4")
        for kc in range(KC):
            ln_apply(B4, "B4", kc, B4[:, kc, :], "B4", VG2, VB2)
        for s in range(NSUB):
            for g4 in range(8):
                bank = 4 + (g4 % 2)

                def tr2(e, g4=g4, bank=bank, s=s):
                    ins = None
                    for q in range(4):
                        kc = g4 * 4 + q
                        ins = e.transpose(out=ps[bank][:, q * 128:(q + 1) * 128], in_=B4[:, kc, s * 128:(s + 1) * 128],
                                          identity=identf)
                    return ins
                S.add("pe", tr2, reads=["B4", "cst"], writes=[("ps", bank)])
                if g4 % 2 == 0:
                    S.add("act", lambda e, g4=g4, bank=bank: e.activation(out=xbuf[:, g4 * 512:(g4 + 1) * 512], in_=ps[bank][:, :], func=AF.Identity),
                          reads=[("ps", bank)], writes=["xbuf"])
                else:
                    S.add("dve", lambda e, g4=g4, bank=bank: e.tensor_copy(out=xbuf[:, g4 * 512:(g4 + 1) * 512], in_=ps[bank][:, :]),
                          reads=[("ps", bank)], writes=["xbuf"])
            r0 = (ti - n_pre) * T + s * 128
            S.add("sp", lambda e, r0=r0: e.dma_start(out=out[r0:r0 + 128, :], in_=xbuf[:]), reads=["xbuf"], dma=("out", 0))

    S.emit(nc, es)
    es.close()
    return nc


def _consts():
    c = np.zeros((128, 4, 128), np.float32)
    c[:, 0, :] = np.eye(128, dtype=np.float32)
    c[:, 1, :] = 1.0
    j = np.arange(128)[:, None]
    cc = np.arange(128)[None, :]
    c[:, 2, :] = ((j // 64 == cc // 64) & (j > cc)).astype(np.float32)
    c[:, 3, 0] = (np.arange(128) < 64)
    c[:, 3, 1] = (np.arange(128) >= 64)
    return c


def _chunkvec(v):
    return np.ascontiguousarray(np.asarray(v, np.float32).reshape(KC, 128).T)


def prepare(inputs, n_pre=NPRE_TILES, n_own=NOWN_TILES, cores=range(8)):
    x = np.asarray(inputs["x"], np.float32)
    meta = np.asarray(inputs["meta"], np.float32)
    f = lambda k: np.asarray(inputs[k], np.float32)
    shared = {
        "w_in": f("w_in")[0],
        "wa2": np.ascontiguousarray(np.concatenate([f("w_a2")[0], f("b_a")[0][None, :]], axis=0)),
        "gng": np.ascontiguousarray(f("gla_norm_g")[0].reshape(2, 128).T),
        "w_gla_o": f("w_gla_o")[0],
        "conv_w": np.ascontiguousarray(f("conv_w")[0].T.reshape(KC, 128, 31).transpose(1, 0, 2)),
        "w_conv_o": f("w_conv_o")[0],
        "w_out": f("w_out")[0],
        "wq": f("peer_wq")[0],
        "keysT": np.ascontiguousarray(f("peer_keys")[0].reshape(16, 128, 128).transpose(2, 0, 1)),
        "uT": np.ascontiguousarray(f("peer_u")[0].T),
        "vtab": f("peer_v")[0],
        "consts": _consts(),
    }
    vecs = np.zeros((128, 11, KC), np.float32)
    for i, v in enumerate([f("ln0_g"), f("ln0_b"), f("conv_b")[0], f("conv_ln_g")[0], f("conv_ln_b")[0], f("b_conv_o")[0],
                           f("ln1_g")[0], f("ln1_b")[0], f("ln2_g")[0], f("ln2_b")[0]]):
        vecs[:, i, :] = _chunkvec(v)
    shared["vecs"] = vecs
    npre_tok = n_pre * T
    ntok = (n_pre + n_own) * T
    maps = []
    for c in cores:
        b, j = c // 4, c % 4
        xs = np.zeros((ntok, D), np.float32)
        mask = np.zeros((ntok,), np.float32)
        nvalid = j * SEG
        own0 = npre_tok
        if nvalid > 0:
            xs[own0 - nvalid:own0] = x[b, 0:nvalid]
            mask[own0 - nvalid:own0] = 1.0
        m0 = own0 - nvalid - 16
        xs[m0:m0 + 16] = meta
        mask[m0:m0 + 16] = 1.0
        xs[own0:own0 + n_own * T] = x[b, j * SEG:j * SEG + n_own * T]
        mask[own0:] = 1.0
        mt = np.ascontiguousarray(mask.reshape(-1, 128).T)
        lp = n_pre - 1
        mrep = np.ascontiguousarray(np.broadcast_to(mask[lp * T:(lp + 1) * T][None, :], (128, T)))
        d = dict(shared)
        d.update({"xs": xs, "maskt": mt, "maskrep": mrep})
        maps.append(d)
    return maps


def kernel(**inputs):
    nc = build()
    maps = prepare(inputs)
    res = run_bass_kernel_spmd(nc, maps, core_ids=list(range(8)))
    outp = np.zeros((2, 8192, D), np.float32)
    for c in range(8):
        b, j = c // 4, c % 4
        outp[b, j * SEG:(j + 1) * SEG] = res.results[c]["out"]
    return outp
```

```python
import numpy as np
from contextlib import ExitStack
import concourse.bass as bass
import concourse.mybir as mybir
from concourse.bass_utils import run_bass_kernel_spmd

F32 = mybir.dt.float32
BF16 = mybir.dt.bfloat16
AF = mybir.ActivationFunctionType
ALU = mybir.AluOpType
AX = mybir.AxisListType

D = 4096
KC = 32
T = 256
NSUB = 2
SEG = 2048
NPRE_TILES = 25
NOWN_TILES = 8
OFF_Q, OFF_K, OFF_V, OFF_G, OFF_A, OFF_C, OFF_GATE = 0, 2048, 4096, 8192, 12288, 12304, 20496
IN_WIDTH = 28688
ALPHA = 2.0 ** 0.25
LN_EPS = 1e-5
RMS_EPS = 1e-6
NEXP = 16384
TOPK_EPS = 4e-6
NEG = -1.0e30

ENGS = ["pe", "act", "dve", "pool", "sp"]


class Op:
    __slots__ = ("eng", "fn", "deps", "dma", "idx", "sig", "sigcnt", "dcnt")

    def __init__(self, eng, fn, dma):
        self.eng, self.fn, self.dma = eng, fn, dma
        self.deps = []
        self.sig = False
        self.sigcnt = 0
        self.dcnt = 0


class Sched:
    def __init__(self):
        self.ops = {e: [] for e in ENGS}
        self.lastw = {}
        self.readers = {}
        self.dma_keys = {}

    def add(self, eng, fn, reads=(), writes=(), dma=None):
        op = Op(eng, fn, dma)
        deps = {}
        for r in reads:
            w = self.lastw.get(r)
            if w is not None:
                deps[id(w)] = w
        for wr in writes:
            w = self.lastw.get(wr)
            if w is not None:
                deps[id(w)] = w
            for rd in self.readers.get(wr, ()):
                deps[id(rd)] = rd
        deps.pop(id(op), None)
        op.deps = [d for d in deps.values()
                   if not (d.eng == "pe" and eng == "pe" and d.dma is None)]
        for r in reads:
            self.readers.setdefault(r, []).append(op)
        for wr in writes:
            self.lastw[wr] = op
            self.readers[wr] = []
        if dma is not None:
            self.dma_keys[dma] = self.dma_keys.get(dma, 0) + 1
            op.dcnt = self.dma_keys[dma]
        op.idx = len(self.ops[eng])
        self.ops[eng].append(op)
        return op

    def emit(self, nc, es, final_waits_eng="sp"):
        for e in ENGS:
            for op in self.ops[e]:
                for d in op.deps:
                    if d.dma is None:
                        d.sig = True
        esem = {e: es.enter_context(nc.semaphore("sem_" + e)) for e in ENGS}
        dsem = {k: es.enter_context(nc.semaphore("dsem_%d" % i)) for i, k in enumerate(self.dma_keys)}
        for e in ENGS:
            c = 0
            for op in self.ops[e]:
                if op.sig:
                    c += 1
                op.sigcnt = c
        block = es.enter_context(nc.Block())
        ops = self.ops
        final = [(dsem[k], 16 * n) for k, n in self.dma_keys.items() if isinstance(k, tuple) and k[0] == "out"]

        def body_for(e):
            def body(eng):
                waited = {}
                for op in ops[e]:
                    for d in op.deps:
                        if d.dma is not None:
                            sem, val = dsem[d.dma], 16 * d.dcnt
                        else:
                            sem, val = esem[d.eng], d.sigcnt
                        key = id(sem)
                        if waited.get(key, 0) >= val:
                            continue
                        waited[key] = val
                        eng.wait_ge(sem, val)
                    ins = op.fn(eng)
                    if op.dma is not None:
                        ins.then_inc(dsem[op.dma], 16)
                    elif op.sig:
                        ins.then_inc(esem[e], 1)
                if e == final_waits_eng:
                    for sem, val in final:
                        eng.wait_ge(sem, val)
            return body

        block.tensor(body_for("pe"))
        block.scalar(body_for("act"))
        block.vector(body_for("dve"))
        block.gpsimd(body_for("pool"))
        block.sync(body_for("sp"))


def build(n_pre=NPRE_TILES, n_own=NOWN_TILES, dbg=None, stage=9):
    nc = bass.Bass("TRN2", target_bir_lowering=False)
    ntiles = n_pre + n_own
    S = Sched()
    es = ExitStack()

    def dram(name, shape, dt=F32, kind="ExternalInput"):
        return nc.dram_tensor(name, list(shape), dt, kind=kind).ap()

    xs = dram("xs", [ntiles * T, D])
    maskt = dram("maskt", [128, ntiles * NSUB])
    maskrep = dram("maskrep", [128, T])
    w_in = dram("w_in", [D, IN_WIDTH])
    wa2 = dram("wa2", [17, 2048])
    gng = dram("gng", [128, 2])
    w_gla_o = dram("w_gla_o", [D, D])
    conv_w = dram("conv_w", [128, KC, 31])
    w_conv_o = dram("w_conv_o", [D, D])
    w_out = dram("w_out", [D, D])
    wq = dram("wq", [D, 2048])
    keysT = dram("keysT", [128, 16, 128])
    uT = dram("uT", [D, NEXP])
    vtab = dram("vtab", [NEXP, D])
    vecs = dram("vecs", [128, 11, KC])
    consts = dram("consts", [128, 4, 128])
    out = dram("out", [n_own * T, D], F32, kind="ExternalOutput")
    scrs = [nc.dram_tensor("scr%d" % i, [200, 128, 4096], BF16, kind="Internal").ap() for i in range(3)]

    class _Scr:
        def __getitem__(self, bid):
            return scrs[bid // 200][bid % 200]
    scr = _Scr()
    scr_ids = {}
    dbg_out = None
    if dbg:
        dbg_out = dram("dbg", [128, dbg["cols"]], F32, kind="ExternalOutput")

    def sb(name, shape, dt=F32):
        return es.enter_context(nc.sbuf_tensor(name, list(shape), dt))

    hT = sb("hT", [128, KC, T], BF16)
    B2f = sb("B2f", [128, D], F32)
    B2 = B2f[:].bitcast(BF16).rearrange("p (k t) -> p k t", k=KC)
    B3 = sb("B3", [128, KC, T], BF16)
    B4 = sb("B4", [128, KC, T], F32)
    xbuf = sb("xbuf", [128, D], F32)
    NSLOT = 3
    wsl = [sb("wsl%d" % i, [128, 16, 256], BF16) for i in range(NSLOT)]
    Sst = sb("Sst", [128, 16, 256], F32)
    Sb = sb("Sb", [128, 2, 256], BF16)
    vec = sb("vec", [128, 11, KC], F32)
    cst = sb("cst", [128, 4, 128], F32)
    identf = cst[:, 0, :]
    cstb = sb("cstb", [128, 4, 128], BF16)
    identb, onesb, ublk, cind = cstb[:, 0, :], cstb[:, 1, :], cstb[:, 2, :], cstb[:, 3, 0:4]
    onesf = sb("onesf", [128, 128], F32)
    mk = sb("mk", [128, ntiles * NSUB], F32)
    mkrep = sb("mkrep", [128, T], F32)
    wA = sb("wA", [128, KC, 16], BF16)
    wa2b = sb("wa2b", [32, 2048], BF16)
    gn = sb("gn", [128, 2], F32)
    cw = sb("cw", [128, KC, 31], F32)
    keyb = sb("keyb", [128, 16, 128], BF16)
    halo = sb("halo", [128, KC, 32], F32)
    stats = sb("stats", [128, 8, 6], F32)
    mv = sb("mv", [128, 2], F32)
    rs = sb("rs", [128, 2], F32)
    a_aug = sb("a_aug", [32, T], BF16)
    e1 = sb("e1", [128, 512], F32)
    lbuf = sb("lbuf", [128, NSUB, 256], BF16)
    erev = sb("erev", [128, NSUB, 256], F32)
    dTt = sb("dTt", [128, 16], F32)
    kdec = [sb("kdec%d" % i, [128, NSUB, 256], BF16) for i in range(2)]
    vbuf = [[sb("vbuf%d_%d" % (i, h), [128, NSUB, 256], BF16) for h in range(2)] for i in range(2)]
    qTb = [sb("qTb%d" % i, [128, 2, T], BF16) for i in range(2)]
    sqb = sb("sqb", [128, 2, T], BF16)
    lnv = sb("lnv", [128, T], F32)
    rstd = sb("rstd", [128, T], F32)
    mean = sb("mean", [128, T], F32)
    msq = sb("msq", [128, T], F32)
    tmpa = [sb("tmpa%d" % i, [128, T], F32) for i in range(2)]
    tmpb = [sb("tmpb%d" % i, [128, T], F32) for i in range(2)]
    cgb = [sb("cgb%d" % i, [128, 32 + T], F32) for i in range(2)]
    sc = xbuf[:].rearrange("p (s a h n) -> p s a h n", s=NSUB, a=2, h=8)
    scw = B2f[:, 0:2048].rearrange("p (a n) -> p a n", a=16)
    v16 = sb("v16", [128, 2, 8, 16], F32)
    cand = B2f[:, 2048:4096].rearrange("p (a n) -> p a n", a=8)
    t16 = sb("t16", [128, 8, 16], F32)
    e16 = sb("e16", [128, 8, 16], F32)
    zz = sb("zz", [128, 8], F32)
    mz = sb("mz", [128, 8], F32)
    taue = sb("taue", [128, 8], F32)
    thr = sb("thr", [128, NSUB, 8, 128], F32)
    etmp = [sb("etmp%d" % i, [128, 128], F32) for i in range(4)]
    wpb = [sb("wpb%d" % i, [128, 8, 128], BF16) for i in range(4)]
    pqT = [sb("pqT%d" % i, [128, 2, T], BF16) for i in range(2)]

    ps = [es.enter_context(nc.psum_tensor("ps%d" % i, [128, 512], F32)) for i in range(8)]

    def wview(w):
        return w.rearrange("(kc p) n -> p kc n", p=128)
    w_in_v, wgo_v, wco_v, wout_v, wq_v, uT_v = (wview(w) for w in (w_in, w_gla_o, w_conv_o, w_out, wq, uT))
    vt_v = vtab.rearrange("(q kc p) n -> q p kc n", p=128, kc=32)

    cnt = {"slot": 0, "acc": 0, "x": 0}

    def proj(wv, col0, rhs_t, mode, wname="w_in"):
        rhs_ap, rhs_key = rhs_t
        half = cnt["acc"] % 2
        cnt["acc"] += 1
        accs = [ps[2 * half][:, 0:256], ps[2 * half + 1][:, 0:256]]
        akeys = [("ps", 2 * half), ("ps", 2 * half + 1)]
        for kb in range(2):
            slot = cnt["slot"] % NSLOT
            cnt["slot"] += 1
            wt = wsl[slot]
            bkey = (wname, col0, kb)
            if bkey in scr_ids:
                bid = scr_ids[bkey]
                src = scr[bid].rearrange("p (k n) -> p k n", k=16)
                S.add("pool", (lambda eng, wt=wt, src=src: eng.dma_start(out=wt[:], in_=src)),
                      reads=[("scr", bid)], writes=[("wsl", slot)], dma=("wsl", slot))
            else:
                bid = len(scr_ids)
                scr_ids[bkey] = bid
                src = wv[:, kb * 16:(kb + 1) * 16, col0:col0 + 256]
                S.add("pool", (lambda eng, wt=wt, src=src: eng.dma_start(out=wt[:], in_=src)),
                      writes=[("wsl", slot)], dma=("wsl", slot))
                dst = scr[bid].rearrange("p (k n) -> p k n", k=16)
                S.add("sp", (lambda eng, wt=wt, dst=dst: eng.dma_start(out=dst, in_=wt[:])),
                      reads=[("wsl", slot)], writes=[("scr", bid)], dma=("scrst", bid % 16))

            def mm(eng, wt=wt, kb=kb):
                ins = None
                for a in range(2):
                    for k in range(16):
                        kc = kb * 16 + k
                        st = (kb == 0 and k == 0)
                        sp_ = (kb == 1 and k == 15)
                        if mode == "fm":
                            ins = eng.matmul(accs[a], lhsT=wt[:, k, a * 128:(a + 1) * 128], rhs=rhs_ap[:, kc, :],
                                             start=st, stop=sp_)
                        else:
                            ins = eng.matmul(accs[a], lhsT=rhs_ap[:, kc, a * 128:(a + 1) * 128], rhs=wt[:, k, :],
                                             start=st, stop=sp_)
                return ins
            S.add("pe", mm, reads=[("wsl", slot), rhs_key], writes=akeys)
        return accs, akeys

    dbg_col = [0]

    def dump(ap_f32, key, ncol):
        if dbg_out is None:
            return
        c0 = dbg_col[0]
        dbg_col[0] += ncol
        S.add("sp", (lambda eng: eng.dma_start(out=dbg_out[:, c0:c0 + ncol], in_=ap_f32)),
              reads=[key], dma=("out", "dbg"))

    S.add("sp", lambda e: e.dma_start(out=cst[:], in_=consts), writes=["cst"], dma=("ld", 0))
    S.add("sp", lambda e: e.dma_start(out=vec[:], in_=vecs), writes=["vec"], dma=("ld", 1))
    S.add("sp", lambda e: e.dma_start(out=mk[:], in_=maskt), writes=["mk"], dma=("ld", 2))
    S.add("sp", lambda e: e.dma_start(out=mkrep[:], in_=maskrep), writes=["mkrep"], dma=("ld", 3))
    S.add("sp", lambda e: e.dma_start(out=gn[:], in_=gng), writes=["gn"], dma=("ld", 4))
    S.add("sp", lambda e: e.dma_start(out=cw[:], in_=conv_w), writes=["cw"], dma=("ld", 5))
    S.add("pool", lambda e: e.dma_start(out=wA[:], in_=w_in_v[:, :, OFF_A:OFF_A + 16]), writes=["wA"], dma=("ld", 6))
    S.add("pool", lambda e: e.dma_start(out=wa2b[0:17, :], in_=wa2), writes=["wa2b"], dma=("ld", 7))
    S.add("pool", lambda e: e.dma_start(out=keyb[:], in_=keysT), writes=["keyb"], dma=("ld", 8))
    S.add("pool", lambda e: e.dma_start(out=cstb[:], in_=consts), writes=["cstb"], dma=("ld", 9))
    S.add("dve", lambda e: e.memset(Sst[:], 0.0), writes=[("S", h) for h in range(16)])
    S.add("dve", lambda e: e.memset(halo[:], 0.0), writes=["halo"])
    S.add("dve", lambda e: e.memset(a_aug[:], 1.0), writes=["a_aug"])
    S.add("dve", lambda e: e.memset(onesf[:], 1.0), writes=["onesf"])

    VG0, VB0, VCB, VCG, VCBB, VBCO, VG1, VB1, VG2, VB2 = range(10)
    alt = [0]

    def evac_engine():
        alt[0] ^= 1
        return "act" if alt[0] else "dve"

    def affine_evac(eng_name, out_ap, in_ap, scale_ap, bias_ap, reads, writes, func=None):
        if eng_name == "act" or func is not None:
            f = func if func is not None else AF.Identity
            S.add("act", lambda e: e.activation(out=out_ap, in_=in_ap, func=f, bias=bias_ap, scale=scale_ap),
                  reads=reads, writes=writes)
        else:
            S.add("dve", lambda e: e.tensor_scalar(out=out_ap, in0=in_ap, scalar1=scale_ap, scalar2=bias_ap,
                                                   op0=ALU.mult, op1=ALU.add), reads=reads, writes=writes)

    def ln_stats(src, src_key):
        def mm1(eng):
            ins = None
            for kc in range(KC):
                ins = eng.matmul(ps[5][:, 0:T], lhsT=onesf[:], rhs=src[:, kc, :], start=(kc == 0), stop=(kc == KC - 1))
            return ins
        S.add("pe", mm1, reads=[src_key, "onesf"], writes=[("ps", 5)])
        for kc in range(KC):
            tb = tmpa[kc % 2]
            S.add("act", lambda e, tb=tb, kc=kc: e.activation(out=tb[:], in_=src[:, kc, :], func=AF.Square),
                  reads=[src_key], writes=[("tmpa", kc % 2)])
            S.add("pe", lambda e, tb=tb, kc=kc: e.matmul(ps[5][:, T:2 * T], lhsT=onesf[:], rhs=tb[:],
                                                         start=(kc == 0), stop=(kc == KC - 1)),
                  reads=[("tmpa", kc % 2), "onesf"], writes=[("ps", 5)])
        S.add("dve", lambda e: e.tensor_scalar(out=mean[:], in0=ps[5][:, 0:T], scalar1=1.0 / D, scalar2=None,
                                               op0=ALU.mult), reads=[("ps", 5)], writes=["mean"])
        S.add("dve", lambda e: e.tensor_tensor(out=msq[:], in0=mean[:], in1=mean[:], op=ALU.mult),
              reads=["mean"], writes=["msq"])
        S.add("dve", lambda e: e.scalar_tensor_tensor(out=lnv[:], in0=ps[5][:, T:2 * T], scalar=1.0 / D, in1=msq[:],
                                                      op0=ALU.mult, op1=ALU.subtract),
              reads=[("ps", 5), "msq"], writes=["lnv"])
        S.add("dve", lambda e: e.tensor_scalar(out=lnv[:], in0=lnv[:], scalar1=LN_EPS, scalar2=None, op0=ALU.add),
              reads=["lnv"], writes=["lnv"])
        S.add("act", lambda e: e.activation(out=lnv[:], in_=lnv[:], func=AF.Ln), reads=["lnv"], writes=["lnv"])
        S.add("act", lambda e: e.activation(out=rstd[:], in_=lnv[:], func=AF.Exp, scale=-0.5),
              reads=["lnv"], writes=["rstd"])

    def ln_apply(src, src_key, kc, out_ap, out_key, gi, bi, func=None):
        tb = tmpb[kc % 2]
        S.add("dve", lambda e: e.tensor_tensor(out=tb[:], in0=src[:, kc, :], in1=mean[:], op=ALU.subtract),
              reads=[src_key, "mean"], writes=[("tmpb", kc % 2)])
        S.add("dve", lambda e: e.tensor_tensor(out=tb[:], in0=tb[:], in1=rstd[:], op=ALU.mult),
              reads=[("tmpb", kc % 2), "rstd"], writes=[("tmpb", kc % 2)])
        affine_evac("act", out_ap, tb[:], vec[:, gi, kc:kc + 1], vec[:, bi, kc:kc + 1],
                    reads=[("tmpb", kc % 2), "vec"], writes=[out_key], func=func)

    for ti in range(ntiles):
        own = ti >= n_pre
        last_pre = (ti == n_pre - 1)
        for s in range(NSUB):
            r0 = ti * T + s * 128
            xb, xk = (xbuf, "xbuf") if s == 0 else (B2f, "B2")
            S.add("sp", lambda e, r0=r0, xb=xb: e.dma_start(out=xb[:], in_=xs[r0:r0 + 128, :]),
                  writes=[xk], dma=("xbuf", s))

            def bst(e, xb=xb):
                ins = None
                for j in range(8):
                    ins = e.bn_stats(out=stats[:, j, :], in_=xb[:, j * 512:(j + 1) * 512])
                return ins
            S.add("dve", bst, reads=[xk], writes=["stats"])
            S.add("dve", lambda e: e.bn_aggr(out=mv[:], in_=stats[:].rearrange("p a b -> p (a b)")),
                  reads=["stats"], writes=["mv"])
            S.add("dve", lambda e: e.tensor_scalar(out=rs[:, 0:1], in0=mv[:, 1:2], scalar1=LN_EPS, scalar2=None,
                                                   op0=ALU.add), reads=["mv"], writes=["rs0"])
            S.add("act", lambda e: e.activation(out=rs[:, 0:1], in_=rs[:, 0:1], func=AF.Sqrt),
                  reads=["rs0"], writes=["rs0"])
            S.add("dve", lambda e: e.reciprocal(out=rs[:, 1:2], in_=rs[:, 0:1]), reads=["rs0"], writes=["rs1"])
            S.add("dve", lambda e, xb=xb: e.tensor_scalar(out=xb[:], in0=xb[:], scalar1=mv[:, 0:1], scalar2=rs[:, 1:2],
                                                   op0=ALU.subtract, op1=ALU.mult),
                  reads=[xk, "mv", "rs1"], writes=[xk])
            for g4 in range(8):
                bank = 4 + (g4 % 2)

                def tr(e, g4=g4, bank=bank, xb=xb):
                    ins = None
                    for q in range(4):
                        kc = g4 * 4 + q
                        ins = e.transpose(out=ps[bank][:, q * 128:(q + 1) * 128], in_=xb[:, kc * 128:(kc + 1) * 128],
                                          identity=identf)
                    return ins
                S.add("pe", tr, reads=[xk, "cst"], writes=[("ps", bank)])
                for q in range(4):
                    kc = g4 * 4 + q
                    affine_evac(evac_engine(), hT[:, kc, s * 128:(s + 1) * 128], ps[bank][:, q * 128:(q + 1) * 128],
                                vec[:, VG0, kc:kc + 1], vec[:, VB0, kc:kc + 1],
                                reads=[("ps", bank), "vec"], writes=["hT"])

        if stage == 1:
            if own:
                for kc in (0, 31):
                    S.add("dve", lambda e, kc=kc: e.tensor_copy(out=tmpa[0][:], in_=hT[:, kc, :]), reads=["hT"], writes=[("tmpa", 0)])
                    dump(tmpa[0][:], ("tmpa", 0), T)
            continue
        def mma(e):
            ins = None
            for kc in range(KC):
                ins = e.matmul(ps[4][0:16, 0:T], lhsT=wA[:, kc, :], rhs=hT[:, kc, :], start=(kc == 0), stop=(kc == KC - 1))
            return ins
        S.add("pe", mma, reads=["wA", "hT"], writes=[("ps", 4)])
        S.add("act", lambda e: e.activation(out=a_aug[0:16, :], in_=ps[4][0:16, 0:T], func=AF.Identity),
              reads=[("ps", 4)], writes=["a_aug"])

        def stA1(hp):
            def mmz(e, hp=hp):
                ins = None
                for s in range(NSUB):
                    ins = e.matmul(ps[4][:, s * 256:(s + 1) * 256], lhsT=a_aug[0:17, s * 128:(s + 1) * 128],
                                   rhs=wa2b[0:17, hp * 256:(hp + 1) * 256], start=True, stop=True)
                return ins
            S.add("pe", mmz, reads=["a_aug", "wa2b"], writes=[("ps", 4)])
            S.add("act", lambda e: e.activation(out=e1[:], in_=ps[4][:, :], func=AF.Exp, scale=-1.0),
                  reads=[("ps", 4)], writes=["e1"])
            S.add("act", lambda e: e.activation(out=lbuf[:].rearrange("p s c -> p (s c)"), in_=e1[:], func=AF.Ln, bias=1.0),
                  reads=["e1"], writes=["lbuf"])

        def stA2(hp):
            par = hp % 2

            def mmrev(e):
                ins = None
                for s in range(NSUB):
                    ins = e.matmul(ps[5][:, s * 256:(s + 1) * 256], lhsT=ublk, rhs=lbuf[:, s, :], start=True, stop=True)
                return ins
            S.add("pe", mmrev, reads=["lbuf", "cstb"], writes=[("ps", 5)])
            S.add("act", lambda e: e.activation(out=erev[:].rearrange("p s c -> p (s c)"), in_=ps[5][:, :], func=AF.Exp,
                                                scale=-1.0 / 16.0),
                  reads=[("ps", 5)], writes=["erev"])

            def mmtot(e):
                ins = None
                for h in range(2):
                    for s in range(NSUB):
                        c0 = (h * NSUB + s) * 2
                        ins = e.matmul(ps[4][:, c0:c0 + 2], lhsT=lbuf[:, s, h * 128:(h + 1) * 128],
                                       rhs=cind[:, 0:2], start=True, stop=True)
                return ins
            S.add("pe", mmtot, reads=["lbuf", "cstb"], writes=[("ps", 4)])
            S.add("act", lambda e: e.activation(out=dTt[:, par * 8:par * 8 + 8], in_=ps[4][:, 0:8], func=AF.Exp, scale=-1.0 / 16.0),
                  reads=[("ps", 4)], writes=[("dTt", par)])

        def stB(hp):
            par = hp % 2
            accs, akeys = proj(w_in_v, OFF_K + hp * 256, (hT, "hT"), "tm")
            for s in range(NSUB):
                S.add("dve", lambda e, s=s, a=accs[s]: e.tensor_tensor(out=kdec[par][:, s, :], in0=a, in1=erev[:, s, :], op=ALU.mult),
                      reads=[akeys[s], "erev"], writes=[("kdec", par, s)])
            for h in range(2):
                accs, akeys = proj(w_in_v, OFF_V + (hp * 2 + h) * 256, (hT, "hT"), "tm")
                for s in range(NSUB):
                    mcol = ti * NSUB + s
                    S.add("act", lambda e, s=s, h=h, a=accs[s], mcol=mcol: e.activation(
                        out=vbuf[par][h][:, s, :], in_=a, func=AF.Identity, scale=mk[:, mcol:mcol + 1]),
                        reads=[akeys[s], "mk"], writes=[("vbuf", par, h, s)])
            if own:
                accs, akeys = proj(w_in_v, OFF_Q + hp * 256, (hT, "hT"), "fm")
                for h in range(2):
                    S.add("dve", lambda e, h=h, a=accs[h]: e.tensor_scalar(out=qTb[par][:, h, :], in0=a, scalar1=128.0 ** -0.5,
                                                                         scalar2=None, op0=ALU.mult),
                          reads=[akeys[h]], writes=[("qTb", par, h)])

        def stC(hp):
            par = hp % 2
            order = [(h, c) for h in range(2) for c in range(4)] if own else [(h, c) for c in range(4) for h in range(2)]
            for (h, c) in order:
                s, hf = c // 2, c % 2
                rows = slice(hf * 64, hf * 64 + 64)
                hg = hp * 2 + h
                kvb = 6 if (own or h == 0) else 7
                kvp = ps[kvb][:, 0:256]
                S.add("pe", lambda e, h=h, s=s, rows=rows, kvp=kvp: e.matmul(
                    kvp, lhsT=kdec[par][rows, s, h * 128:(h + 1) * 128], rhs=vbuf[par][h][rows, s, :], start=True, stop=True),
                    reads=[("kdec", par, s), ("vbuf", par, h, s)], writes=[("ps", kvb)])
                S.add("dve", lambda e, h=h, hg=hg, c=c, kvp=kvp: e.scalar_tensor_tensor(
                    out=Sst[:, hg, :], in0=Sst[:, hg, :], scalar=dTt[:, par * 8 + h * 4 + c:par * 8 + h * 4 + c + 1], in1=kvp,
                    op0=ALU.mult, op1=ALU.add),
                    reads=[("S", hg), ("dTt", par), ("ps", kvb)], writes=[("S", hg)])
                if own:
                    S.add("act", lambda e, h=h, hg=hg: e.activation(out=Sb[:, h, :], in_=Sst[:, hg, :], func=AF.Identity),
                          reads=[("S", hg)], writes=[("Sb", h)])

                    def mmo(e, h=h, c=c):
                        ins = None
                        for dvs in range(2):
                            ins = e.matmul(ps[7][:, dvs * 256 + c * 64: dvs * 256 + c * 64 + 64],
                                           lhsT=Sb[:, h, dvs * 128:(dvs + 1) * 128], rhs=qTb[par][:, h, c * 64:(c + 1) * 64],
                                           start=True, stop=True)
                        return ins
                    S.add("pe", mmo, reads=[("Sb", h), ("qTb", par, h)], writes=[("ps", 7)])
                if own and c == 3:
                    S.add("act", lambda e, h=h: e.activation(out=sqb[:].rearrange("p a t -> p (a t)"), in_=ps[7][:, :],
                                                            func=AF.Square), reads=[("ps", 7)], writes=["sqb"])

                    def mms(e):
                        ins = None
                        for dvs in range(2):
                            ins = e.matmul(ps[5][:, 256:512], lhsT=onesb, rhs=sqb[:, dvs, :], start=(dvs == 0), stop=(dvs == 1))
                        return ins
                    S.add("pe", mms, reads=["sqb", "cstb"], writes=[("ps", 5)])
                    S.add("dve", lambda e: e.tensor_scalar(out=lnv[:], in0=ps[5][:, 256:512], scalar1=1.0 / 256.0,
                                                           scalar2=RMS_EPS, op0=ALU.mult, op1=ALU.add),
                          reads=[("ps", 5)], writes=["lnv"])
                    S.add("act", lambda e: e.activation(out=lnv[:], in_=lnv[:], func=AF.Ln), reads=["lnv"], writes=["lnv"])
                    S.add("act", lambda e: e.activation(out=rstd[:], in_=lnv[:], func=AF.Exp, scale=-0.5),
                          reads=["lnv"], writes=["rstd"])
                    accs, akeys = proj(w_in_v, OFF_G + hg * 256, (hT, "hT"), "fm")
                    for dvs in range(2):
                        S.add("act", lambda e, dvs=dvs, a=accs[dvs]: e.activation(out=tmpa[dvs][:], in_=a, func=AF.Silu),
                              reads=[akeys[dvs]], writes=[("tmpa", dvs)])
                        S.add("dve", lambda e, h=h, dvs=dvs: e.scalar_tensor_tensor(
                            out=tmpb[dvs][:], in0=ps[7][:, dvs * 256:(dvs + 1) * 256], scalar=gn[:, dvs:dvs + 1],
                            in1=rstd[:], op0=ALU.mult, op1=ALU.mult),
                            reads=[("ps", 7), "gn", "rstd"], writes=[("tmpb", dvs)])
                        S.add("dve", lambda e, hg=hg, dvs=dvs: e.tensor_tensor(
                            out=B2[:, hg * 2 + dvs, :], in0=tmpb[dvs][:], in1=tmpa[dvs][:], op=ALU.mult),
                            reads=[("tmpb", dvs), ("tmpa", dvs)], writes=["B2"])

        for hp in range(9):
            if hp < 8:
                stA1(hp)
            if hp >= 1:
                stC(hp - 1)
            if hp < 8:
                stA2(hp)
                stB(hp)

        if stage == 2:
            if own:
                for kc in (0, 1, 31):
                    S.add("dve", lambda e, kc=kc: e.tensor_copy(out=tmpa[0][:], in_=B2[:, kc, :]), reads=["B2"], writes=[("tmpa", 0)])
                    dump(tmpa[0][:], ("tmpa", 0), T)
            continue
        if not own and not last_pre:
            continue

        if own:
            for d2 in range(16):
                accy, ky = proj(wgo_v, d2 * 256, (B2, "B2"), "fm", "wgo")
                accg, kg = proj(w_in_v, OFF_GATE + d2 * 256, (hT, "hT"), "fm")
                for a in range(2):
                    dc = d2 * 2 + a
                    S.add("act", lambda e, a=a, ag=accg[a]: e.activation(out=tmpa[a][:], in_=ag, func=AF.Sigmoid),
                          reads=[kg[a]], writes=[("tmpa", a)])
                    S.add("dve", lambda e, a=a, dc=dc, ay=accy[a]: e.tensor_tensor(out=B3[:, dc, :], in0=ay, in1=tmpa[a][:],
                                                                                 op=ALU.mult),
                          reads=[ky[a], ("tmpa", a)], writes=["B3"])
        for cp in range(16):
            acca, ka = proj(w_in_v, OFF_C + cp * 256, (hT, "hT"), "fm")
            for a in range(2):
                S.add("act", lambda e, a=a, aa=acca[a]: e.activation(out=tmpa[a][:], in_=aa, func=AF.Identity),
                      reads=[ka[a]], writes=[("tmpa", a)])
            accb, kb_ = proj(w_in_v, OFF_C + D + cp * 256, (hT, "hT"), "fm")
            for a in range(2):
                kc = cp * 2 + a
                S.add("act", lambda e, a=a, ab=accb[a]: e.activation(out=tmpb[a][:], in_=ab, func=AF.Sigmoid),
                      reads=[kb_[a]], writes=[("tmpb", a)])
                S.add("dve", lambda e, a=a, kc=kc: e.tensor_copy(out=cgb[a][:, 0:32], in_=halo[:, kc, :]),
                      reads=["halo"], writes=[("cgb", a)])
                S.add("dve", lambda e, a=a: e.tensor_tensor(out=cgb[a][:, 32:32 + T], in0=tmpa[a][:], in1=tmpb[a][:], op=ALU.mult),
                      reads=[("tmpa", a), ("tmpb", a)], writes=[("cgb", a)])
                if last_pre:
                    S.add("dve", lambda e, a=a: e.tensor_tensor(out=cgb[a][:, 32:32 + T], in0=cgb[a][:, 32:32 + T], in1=mkrep[:],
                                                                op=ALU.mult), reads=[("cgb", a), "mkrep"], writes=[("cgb", a)])
                S.add("act", lambda e, a=a, kc=kc: e.activation(out=halo[:, kc, :], in_=cgb[a][:, T:T + 32], func=AF.Identity),
                      reads=[("cgb", a)], writes=["halo"])
            if own:
                for k in range(31):
                    for a in range(2):
                        kc = cp * 2 + a
                        if k == 0:
                            S.add("dve", lambda e, a=a, kc=kc: e.tensor_scalar(
                                out=B4[:, kc, :], in0=cgb[a][:, 2:2 + T], scalar1=cw[:, kc, 0:1], scalar2=vec[:, VCB, kc:kc + 1],
                                op0=ALU.mult, op1=ALU.add), reads=[("cgb", a), "cw", "vec"], writes=[("B4", kc)])
                        else:
                            S.add("dve", lambda e, a=a, kc=kc, k=k: e.scalar_tensor_tensor(
                                out=B4[:, kc, :], in0=cgb[a][:, 2 + k:2 + k + T], scalar=cw[:, kc, k:k + 1], in1=B4[:, kc, :],
                                op0=ALU.mult, op1=ALU.add), reads=[("cgb", a), "cw", ("B4", kc)], writes=[("B4", kc)])
        if not own:
            continue
        b4keys = [("B4", kc) for kc in range(KC)]
        S.add("dve", lambda e: e.engine_nop(), reads=b4keys, writes=["B4"])
        ln_stats(B4, "B4")
        for kc in range(KC):
            ln_apply(B4, "B4", kc, B2[:, kc, :], "B2", VCG, VCBB, func=AF.Silu)
        for d2 in range(16):
            accy, ky = proj(wco_v, d2 * 256, (B2, "B2"), "fm", "wco")
            accg, kg = proj(w_in_v, OFF_GATE + D + d2 * 256, (hT, "hT"), "fm")
            for a in range(2):
                dc = d2 * 2 + a
                S.add("act", lambda e, a=a, ag=accg[a]: e.activation(out=tmpa[a][:], in_=ag, func=AF.Sigmoid),
                      reads=[kg[a]], writes=[("tmpa", a)])
                S.add("dve", lambda e, a=a, dc=dc, ay=accy[a]: e.scalar_tensor_tensor(
                    out=tmpb[a][:], in0=ay, scalar=vec[:, VBCO, dc:dc + 1], in1=tmpa[a][:], op0=ALU.add, op1=ALU.mult),
                    reads=[ky[a], ("tmpa", a), "vec"], writes=[("tmpb", a)])
                S.add("dve", lambda e, a=a, dc=dc: e.tensor_tensor(out=B3[:, dc, :], in0=B3[:, dc, :], in1=tmpb[a][:], op=ALU.add),
                      reads=["B3", ("tmpb", a)], writes=["B3"])
        for d2 in range(16):
            accy, ky = proj(wout_v, d2 * 256, (B3, "B3"), "fm", "wout")
            for a in range(2):
                dc = d2 * 2 + a
                S.add("dve", lambda e, a=a, dc=dc, ay=accy[a]: e.scalar_tensor_tensor(
                    out=B4[:, dc, :], in0=hT[:, dc, :], scalar=ALPHA, in1=ay, op0=ALU.mult, op1=ALU.add),
                    reads=["hT", ky[a]], writes=["B4"])
        ln_stats(B4, "B4")
        for kc in range(KC):
            ln_apply(B4, "B4", kc, hT[:, kc, :], "hT", VG1, VB1)

        if stage == 3:
            for kc in (0, 31):
                S.add("dve", lambda e, kc=kc: e.tensor_copy(out=tmpa[0][:], in_=hT[:, kc, :]), reads=["hT"], writes=[("tmpa", 0)])
                dump(tmpa[0][:], ("tmpa", 0), T)
            continue
        for p in range(8):
            accq, kq = proj(wq_v, p * 256, (hT, "hT"), "fm", "wq")
            pq = pqT[p % 2]
            for a in range(2):
                S.add(evac_engine() if False else "act", lambda e, a=a, aq=accq[a], pq=pq: e.activation(out=pq[:, a, :], in_=aq, func=AF.Identity),
                      reads=[kq[a]], writes=[("pqT", p % 2, a)])

            def mmsc(e, p=p, pq=pq):
                ins = None
                for s in range(NSUB):
                    for hfq in range(2):
                        c0 = (s * 2 + hfq) * 128
                        ins = e.matmul(ps[4][:, c0:c0 + 128], lhsT=pq[:, hfq, s * 128:(s + 1) * 128],
                                       rhs=keyb[:, p * 2 + hfq, :], start=True, stop=True)
                return ins
            S.add("pe", mmsc, reads=[("pqT", p % 2, 0), ("pqT", p % 2, 1), "keyb"], writes=[("ps", 4)])
            for s in range(NSUB):
                S.add("act" if s == 0 else "dve", (lambda e, s=s, p=p: e.activation(
                    out=sc[:, s, :, p, :], in_=ps[4][:, s * 256:(s + 1) * 256].rearrange("q (a n) -> q a n", a=2), func=AF.Identity))
                    if s == 0 else (lambda e, s=s, p=p: e.tensor_copy(
                        out=sc[:, s, :, p, :], in_=ps[4][:, s * 256:(s + 1) * 256].rearrange("q (a n) -> q a n", a=2))),
                    reads=[("ps", 4)], writes=["xbuf"])
        for s in range(NSUB):
            def top_a(e, s=s):
                ins = None
                for hfq in range(2):
                    for p in range(8):
                        ins = e.max(out=v16[:, hfq, p, 0:8], in_=sc[:, s, hfq, p, :])
                return ins
            S.add("dve", top_a, reads=["xbuf"], writes=["v16a"])

            def top_b(e, s=s):
                ins = None
                for hfq in range(2):
                    for p in range(8):
                        ins = e.match_replace(out=scw[:, hfq * 8 + p, :], in_to_replace=v16[:, hfq, p, 0:8],
                                              in_values=sc[:, s, hfq, p, :], imm_value=NEG)
                return ins
            S.add("dve", top_b, reads=["xbuf", "v16a"], writes=["B2"])

            def top_c(e):
                ins = None
                for hfq in range(2):
                    for p in range(8):
                        ins = e.max(out=v16[:, hfq, p, 8:16], in_=scw[:, hfq * 8 + p, :])
                return ins
            S.add("dve", top_c, reads=["B2"], writes=["v16b"])

            def mkcand(e):
                ins = None
                for p in range(8):
                    ins = e.tensor_tensor(out=cand[:, p, :].rearrange("q (a b) -> q a b", a=16),
                                          in0=v16[:, 0, p, :].unsqueeze(2).broadcast_to([128, 16, 16]),
                                          in1=v16[:, 1, p, :].unsqueeze(1).broadcast_to([128, 16, 16]), op=ALU.add)
                return ins
            S.add("dve", mkcand, reads=["v16a", "v16b"], writes=["B2"])

            def ctop_a(e):
                ins = None
                for p in range(8):
                    ins = e.max(out=t16[:, p, 0:8], in_=cand[:, p, :])
                return ins
            S.add("dve", ctop_a, reads=["B2"], writes=["t16a"])

            def ctop_b(e):
                ins = None
                for p in range(8):
                    ins = e.match_replace(out=cand[:, p, :], in_to_replace=t16[:, p, 0:8], in_values=cand[:, p, :], imm_value=NEG)
                return ins
            S.add("dve", ctop_b, reads=["B2", "t16a"], writes=["B2"])

            def ctop_c(e):
                ins = None
                for p in range(8):
                    ins = e.max(out=t16[:, p, 8:16], in_=cand[:, p, :])
                return ins
            S.add("dve", ctop_c, reads=["B2"], writes=["t16b"])
            S.add("dve", lambda e: e.tensor_tensor(out=e16[:], in0=t16[:], in1=t16[:, :, 0:1].broadcast_to([128, 8, 16]),
                                                   op=ALU.subtract), reads=["t16a", "t16b"], writes=["e16"])
            S.add("act", lambda e: e.activation(out=e16[:], in_=e16[:], func=AF.Exp), reads=["e16"], writes=["e16"])
            S.add("dve", lambda e: e.tensor_reduce(out=zz[:], in_=e16[:], axis=AX.X, op=ALU.add), reads=["e16"], writes=["zz"])
            S.add("act", lambda e: e.activation(out=zz[:], in_=zz[:], func=AF.Ln), reads=["zz"], writes=["zz"])
            S.add("dve", lambda e: e.tensor_tensor(out=mz[:], in0=zz[:], in1=t16[:, :, 0], op=ALU.add),
                  reads=["zz", "t16a"], writes=["mz"])
            S.add("dve", lambda e: e.tensor_scalar(out=taue[:], in0=t16[:, :, 15], scalar1=-TOPK_EPS, scalar2=None, op0=ALU.add),
                  reads=["t16b"], writes=["taue"])
            S.add("dve", lambda e: e.tensor_tensor(out=taue[:], in0=taue[:], in1=mz[:], op=ALU.subtract),
                  reads=["taue", "mz"], writes=["taue"])
            S.add("dve", lambda e, s=s: e.tensor_tensor(out=thr[:, s, :, :], in0=taue[:].unsqueeze(2).broadcast_to([128, 8, 128]),
                                                        in1=sc[:, s, 0, :, :], op=ALU.subtract),
                  reads=["taue", "xbuf"], writes=[("thr", s)])
            S.add("dve", lambda e, s=s: e.tensor_tensor(out=sc[:, s, 1, :, :], in0=sc[:, s, 1, :, :],
                                                        in1=mz[:].unsqueeze(2).broadcast_to([128, 8, 128]), op=ALU.subtract),
                  reads=["mz", "xbuf"], writes=["xbuf"])
        ei = 0
        for qtr in range(4):
            pend = proj(uT_v, (qtr * 16) * 256, (hT, "hT"), "fm", "uT")
            for i2 in range(16):
                accu, ku = pend
                if i2 + 1 < 16:
                    pend = proj(uT_v, (qtr * 16 + i2 + 1) * 256, (hT, "hT"), "fm", "uT")
                for a in range(2):
                    i = (qtr * 16 + i2) * 2 + a
                    il = i2 * 2 + a
                    wtb = 6 + (i % 2)
                    for s in range(NSUB):
                        wsel = (i * NSUB + s) % 4
                        wb = wpb[wsel]
                        wk = ("wpb", wsel)
                        for p in range(8):
                            et = etmp[ei % 4]
                            ek = ("etmp", ei % 4)
                            ei += 1
                            S.add("act", lambda e, et=et, s=s, p=p, i=i: e.activation(
                                out=et[:], in_=sc[:, s, 1, p, :], func=AF.Exp, bias=sc[:, s, 0, p, i:i + 1], scale=1.0),
                                reads=["xbuf"], writes=[ek])
                            S.add("dve", lambda e, et=et, wb=wb, s=s, p=p, i=i: e.scalar_tensor_tensor(
                                out=wb[:, p, :], in0=sc[:, s, 1, p, :], scalar=thr[:, s, p, i:i + 1], in1=et[:],
                                op0=ALU.is_ge, op1=ALU.mult), reads=["xbuf", ("thr", s), ek], writes=[wk])

                        def mmw(e, wb=wb, s=s, wtb=wtb):
                            ins = None
                            for p in range(8):
                                ins = e.matmul(ps[wtb][:, s * 128:(s + 1) * 128], lhsT=wb[:, p, :], rhs=identb,
                                               start=(p == 0), stop=(p == 7))
                            return ins
                        S.add("pe", mmw, reads=[wk, "cstb"], writes=[("ps", wtb)])
                    S.add("act", lambda e, a=a, au=accu[a]: e.activation(out=tmpa[a][:], in_=au, func=AF.Gelu),
                          reads=[ku[a]], writes=[("tmpa", a)])
                    S.add("dve", lambda e, a=a, il=il, wtb=wtb: e.tensor_tensor(out=B3[:, il, :], in0=tmpa[a][:], in1=ps[wtb][:, 0:T], op=ALU.mult),
                          reads=[("tmpa", a), ("ps", wtb)], writes=["B3"])
            for d2 in range(16):
                accv, kv_ = proj(vt_v[qtr], d2 * 256, (B3, "B3"), "fm", ("v", qtr))
                for a in range(2):
                    dc = d2 * 2 + a
                    if qtr == 0:
                        S.add("dve", lambda e, dc=dc, av=accv[a]: e.scalar_tensor_tensor(
                            out=B4[:, dc, :], in0=hT[:, dc, :], scalar=ALPHA, in1=av, op0=ALU.mult, op1=ALU.add),
                            reads=["hT", kv_[a]], writes=["B4"])
                    else:
                        S.add("dve", lambda e, dc=dc, av=accv[a]: e.tensor_tensor(out=B4[:, dc, :], in0=B4[:, dc, :], in1=av, op=ALU.add),
                              reads=["B4", kv_[a]], writes=["B4"])
        ln_stats(B4, "B4")
        for kc in range(KC):
            ln_apply(B4, "B4", kc, B4[:, kc, :], "B4", VG2, VB2)
        for s in range(NSUB):
            for g4 in range(8):
                bank = 4 + (g4 % 2)

                def tr2(e, g4=g4, bank=bank, s=s):
                    ins = None
                    for q in range(4):
                        kc = g4 * 4 + q
                        ins = e.transpose(out=ps[bank][:, q * 128:(q + 1) * 128], in_=B4[:, kc, s * 128:(s + 1) * 128],
                                          identity=identf)
                    return ins
                S.add("pe", tr2, reads=["B4", "cst"], writes=[("ps", bank)])
                if g4 % 2 == 0:
                    S.add("act", lambda e, g4=g4, bank=bank: e.activation(out=xbuf[:, g4 * 512:(g4 + 1) * 512], in_=ps[bank][:, :], func=AF.Identity),
                          reads=[("ps", bank)], writes=["xbuf"])
                else:
                    S.add("dve", lambda e, g4=g4, bank=bank: e.tensor_copy(out=xbuf[:, g4 * 512:(g4 + 1) * 512], in_=ps[bank][:, :]),
                          reads=[("ps", bank)], writes=["xbuf"])
            r0 = (ti - n_pre) * T + s * 128
            S.add("sp", lambda e, r0=r0: e.dma_start(out=out[r0:r0 + 128, :], in_=xbuf[:]), reads=["xbuf"], dma=("out", 0))

    S.emit(nc, es)
    es.close()
    return nc


def _consts():
    c = np.zeros((128, 4, 128), np.float32)
    c[:, 0, :] = np.eye(128, dtype=np.float32)
    c[:, 1, :] = 1.0
    j = np.arange(128)[:, None]
    cc = np.arange(128)[None, :]
    c[:, 2, :] = ((j // 64 == cc // 64) & (j > cc)).astype(np.float32)
    c[:, 3, 0] = (np.arange(128) < 64)
    c[:, 3, 1] = (np.arange(128) >= 64)
    return c


def _chunkvec(v):
    return np.ascontiguousarray(np.asarray(v, np.float32).reshape(KC, 128).T)


def prepare(inputs, n_pre=NPRE_TILES, n_own=NOWN_TILES, cores=range(8)):
    x = np.asarray(inputs["x"], np.float32)
    meta = np.asarray(inputs["meta"], np.float32)
    f = lambda k: np.asarray(inputs[k], np.float32)
    shared = {
        "w_in": f("w_in")[0],
        "wa2": np.ascontiguousarray(np.concatenate([f("w_a2")[0], f("b_a")[0][None, :]], axis=0)),
        "gng": np.ascontiguousarray(f("gla_norm_g")[0].reshape(2, 128).T),
        "w_gla_o": f("w_gla_o")[0],
        "conv_w": np.ascontiguousarray(f("conv_w")[0].T.reshape(KC, 128, 31).transpose(1, 0, 2)),
        "w_conv_o": f("w_conv_o")[0],
        "w_out": f("w_out")[0],
        "wq": f("peer_wq")[0],
        "keysT": np.ascontiguousarray(f("peer_keys")[0].reshape(16, 128, 128).transpose(2, 0, 1)),
        "uT": np.ascontiguousarray(f("peer_u")[0].T),
        "vtab": f("peer_v")[0],
        "consts": _consts(),
    }
    vecs = np.zeros((128, 11, KC), np.float32)
    for i, v in enumerate([f("ln0_g"), f("ln0_b"), f("conv_b")[0], f("conv_ln_g")[0], f("conv_ln_b")[0], f("b_conv_o")[0],
                           f("ln1_g")[0], f("ln1_b")[0], f("ln2_g")[0], f("ln2_b")[0]]):
        vecs[:, i, :] = _chunkvec(v)
    shared["vecs"] = vecs
    npre_tok = n_pre * T
    ntok = (n_pre + n_own) * T
    maps = []
    for c in cores:
        b, j = c // 4, c % 4
        xs = np.zeros((ntok, D), np.float32)
        mask = np.zeros((ntok,), np.float32)
        nvalid = j * SEG
        own0 = npre_tok
        if nvalid > 0:
            xs[own0 - nvalid:own0] = x[b, 0:nvalid]
            mask[own0 - nvalid:own0] = 1.0
        m0 = own0 - nvalid - 16
        xs[m0:m0 + 16] = meta
        mask[m0:m0 + 16] = 1.0
        xs[own0:own0 + n_own * T] = x[b, j * SEG:j * SEG + n_own * T]
        mask[own0:] = 1.0
        mt = np.ascontiguousarray(mask.reshape(-1, 128).T)
        lp = n_pre - 1
        mrep = np.ascontiguousarray(np.broadcast_to(mask[lp * T:(lp + 1) * T][None, :], (128, T)))
        d = dict(shared)
        d.update({"xs": xs, "maskt": mt, "maskrep": mrep})
        maps.append(d)
    return maps


def kernel(**inputs):
    nc = build()
    maps = prepare(inputs)
    res = run_bass_kernel_spmd(nc, maps, core_ids=list(range(8)))
    outp = np.zeros((2, 8192, D), np.float32)
    for c in range(8):
        b, j = c // 4, c % 4
        outp[b, j * SEG:(j + 1) * SEG] = res.results[c]["out"]
    return outp
```

```python
import numpy as np
from contextlib import ExitStack
import concourse.bass as bass
import concourse.mybir as mybir
from concourse.bass_utils import run_bass_kernel_spmd

F32 = mybir.dt.float32
BF16 = mybir.dt.bfloat16
AF = mybir.ActivationFunctionType
ALU = mybir.AluOpType
AX = mybir.AxisListType

D = 4096
KC = 32
T = 256
NSUB = 2
SEG = 2048
NPRE_TILES = 25
NOWN_TILES = 8
OFF_Q, OFF_K, OFF_V, OFF_G, OFF_A, OFF_C, OFF_GATE = 0, 2048, 4096, 8192, 12288, 12304, 20496
IN_WIDTH = 28688
ALPHA = 2.0 ** 0.25
LN_EPS = 1e-5
RMS_EPS = 1e-6
NEXP = 16384
TOPK_EPS = 4e-6
NEG = -1.0e30

ENGS = ["pe", "act", "dve", "pool", "sp"]


class Op:
    __slots__ = ("eng", "fn", "deps", "dma", "idx", "sig", "sigcnt", "dcnt")

    def __init__(self, eng, fn, dma):
        self.eng, self.fn, self.dma = eng, fn, dma
        self.deps = []
        self.sig = False
        self.sigcnt = 0
        self.dcnt = 0


class Sched:
    def __init__(self):
        self.ops = {e: [] for e in ENGS}
        self.lastw = {}
        self.readers = {}
        self.dma_keys = {}

    def add(self, eng, fn, reads=(), writes=(), dma=None):
        op = Op(eng, fn, dma)
        deps = {}
        for r in reads:
            w = self.lastw.get(r)
            if w is not None:
                deps[id(w)] = w
        for wr in writes:
            w = self.lastw.get(wr)
            if w is not None:
                deps[id(w)] = w
            for rd in self.readers.get(wr, ()):
                deps[id(rd)] = rd
        deps.pop(id(op), None)
        op.deps = [d for d in deps.values()
                   if not (d.eng == "pe" and eng == "pe" and d.dma is None)]
        for r in reads:
            self.readers.setdefault(r, []).append(op)
        for wr in writes:
            self.lastw[wr] = op
            self.readers[wr] = []
        if dma is not None:
            self.dma_keys[dma] = self.dma_keys.get(dma, 0) + 1
            op.dcnt = self.dma_keys[dma]
        op.idx = len(self.ops[eng])
        self.ops[eng].append(op)
        return op

    def emit(self, nc, es, final_waits_eng="sp"):
        for e in ENGS:
            for op in self.ops[e]:
                for d in op.deps:
                    if d.dma is None:
                        d.sig = True
        esem = {e: es.enter_context(nc.semaphore("sem_" + e)) for e in ENGS}
        dsem = {k: es.enter_context(nc.semaphore("dsem_%d" % i)) for i, k in enumerate(self.dma_keys)}
        for e in ENGS:
            c = 0
            for op in self.ops[e]:
                if op.sig:
                    c += 1
                op.sigcnt = c
        block = es.enter_context(nc.Block())
        ops = self.ops
        final = [(dsem[k], 16 * n) for k, n in self.dma_keys.items() if isinstance(k, tuple) and k[0] == "out"]

        def body_for(e):
            def body(eng):
                waited = {}
                for op in ops[e]:
                    for d in op.deps:
                        if d.dma is not None:
                            sem, val = dsem[d.dma], 16 * d.dcnt
                        else:
                            sem, val = esem[d.eng], d.sigcnt
                        key = id(sem)
                        if waited.get(key, 0) >= val:
                            continue
                        waited[key] = val
                        eng.wait_ge(sem, val)
                    ins = op.fn(eng)
                    if op.dma is not None:
                        ins.then_inc(dsem[op.dma], 16)
                    elif op.sig:
                        ins.then_inc(esem[e], 1)
                if e == final_waits_eng:
                    for sem, val in final:
                        eng.wait_ge(sem, val)
            return body

        block.tensor(body_for("pe"))
        block.scalar(body_for("act"))
        block.vector(body_for("dve"))
        block.gpsimd(body_for("pool"))
        block.sync(body_for("sp"))


def build(n_pre=NPRE_TILES, n_own=NOWN_TILES, dbg=None, stage=9):
    nc = bass.Bass("TRN2", target_bir_lowering=False)
    ntiles = n_pre + n_own
    S = Sched()
    es = ExitStack()

    def dram(name, shape, dt=F32, kind="ExternalInput"):
        return nc.dram_tensor(name, list(shape), dt, kind=kind).ap()

    xs = dram("xs", [ntiles * T, D])
    maskt = dram("maskt", [128, ntiles * NSUB])
    maskrep = dram("maskrep", [128, T])
    w_in = dram("w_in", [D, IN_WIDTH])
    wa2 = dram("wa2", [17, 2048])
    gng = dram("gng", [128, 2])
    w_gla_o = dram("w_gla_o", [D, D])
    conv_w = dram("conv_w", [128, KC, 31])
    w_conv_o = dram("w_conv_o", [D, D])
    w_out = dram("w_out", [D, D])
    wq = dram("wq", [D, 2048])
    keysT = dram("keysT", [128, 16, 128])
    uT = dram("uT", [D, NEXP])
    vtab = dram("vtab", [NEXP, D])
    vecs = dram("vecs", [128, 11, KC])
    consts = dram("consts", [128, 4, 128])
    out = dram("out", [n_own * T, D], F32, kind="ExternalOutput")
    scrs = [nc.dram_tensor("scr%d" % i, [200, 128, 4096], BF16, kind="Internal").ap() for i in range(3)]

    class _Scr:
        def __getitem__(self, bid):
            return scrs[bid // 200][bid % 200]
    scr = _Scr()
    scr_ids = {}
    dbg_out = None
    if dbg:
        dbg_out = dram("dbg", [128, dbg["cols"]], F32, kind="ExternalOutput")

    def sb(name, shape, dt=F32):
        return es.enter_context(nc.sbuf_tensor(name, list(shape), dt))

    hT = sb("hT", [128, KC, T], BF16)
    B2f = sb("B2f", [128, D], F32)
    B2 = B2f[:].bitcast(BF16).rearrange("p (k t) -> p k t", k=KC)
    B3 = sb("B3", [128, KC, T], BF16)
    B4 = sb("B4", [128, KC, T], F32)
    xbuf = sb("xbuf", [128, D], F32)
    NSLOT = 3
    wsl = [sb("wsl%d" % i, [128, 16, 256], BF16) for i in range(NSLOT)]
    Sst = sb("Sst", [128, 16, 256], F32)
    Sb = sb("Sb", [128, 2, 256], BF16)
    vec = sb("vec", [128, 11, KC], F32)
    cst = sb("cst", [128, 4, 128], F32)
    identf = cst[:, 0, :]
    cstb = sb("cstb", [128, 4, 128], BF16)
    identb, onesb, ublk, cind = cstb[:, 0, :], cstb[:, 1, :], cstb[:, 2, :], cstb[:, 3, 0:4]
    onesf = sb("onesf", [128, 128], F32)
    mk = sb("mk", [128, ntiles * NSUB], F32)
    mkrep = sb("mkrep", [128, T], F32)
    wA = sb("wA", [128, KC, 16], BF16)
    wa2b = sb("wa2b", [32, 2048], BF16)
    gn = sb("gn", [128, 2], F32)
    cw = sb("cw", [128, KC, 31], F32)
    keyb = sb("keyb", [128, 16, 128], BF16)
    halo = sb("halo", [128, KC, 32], F32)
    stats = sb("stats", [128, 8, 6], F32)
    mv = sb("mv", [128, 2], F32)
    rs = sb("rs", [128, 2], F32)
    a_aug = sb("a_aug", [32, T], BF16)
    e1 = sb("e1", [128, 512], F32)
    lbuf = sb("lbuf", [128, NSUB, 256], BF16)
    erev = sb("erev", [128, NSUB, 256], F32)
    dTt = sb("dTt", [128, 16], F32)
    kdec = [sb("kdec%d" % i, [128, NSUB, 256], BF16) for i in range(2)]
    vbuf = [[sb("vbuf%d_%d" % (i, h), [128, NSUB, 256], BF16) for h in range(2)] for i in range(2)]
    qTb = [sb("qTb%d" % i, [128, 2, T], BF16) for i in range(2)]
    sqb = sb("sqb", [128, 2, T], BF16)
    lnv = sb("lnv", [128, T], F32)
    rstd = sb("rstd", [128, T], F32)
    mean = sb("mean", [128, T], F32)
    msq = sb("msq", [128, T], F32)
    tmpa = [sb("tmpa%d" % i, [128, T], F32) for i in range(2)]
    tmpb = [sb("tmpb%d" % i, [128, T], F32) for i in range(2)]
    cgb = [sb("cgb%d" % i, [128, 32 + T], F32) for i in range(2)]
    sc = xbuf[:].rearrange("p (s a h n) -> p s a h n", s=NSUB, a=2, h=8)
    scw = B2f[:, 0:2048].rearrange("p (a n) -> p a n", a=16)
    v16 = sb("v16", [128, 2, 8, 16], F32)
    cand = B2f[:, 2048:4096].rearrange("p (a n) -> p a n", a=8)
    t16 = sb("t16", [128, 8, 16], F32)
    e16 = sb("e16", [128, 8, 16], F32)
    zz = sb("zz", [128, 8], F32)
    mz = sb("mz", [128, 8], F32)
    taue = sb("taue", [128, 8], F32)
    thr = sb("thr", [128, NSUB, 8, 128], F32)
    etmp = [sb("etmp%d" % i, [128, 128], F32) for i in range(4)]
    wpb = [sb("wpb%d" % i, [128, 8, 128], BF16) for i in range(4)]
    pqT = [sb("pqT%d" % i, [128, 2, T], BF16) for i in range(2)]

    ps = [es.enter_context(nc.psum_tensor("ps%d" % i, [128, 512], F32)) for i in range(8)]

    def wview(w):
        return w.rearrange("(kc p) n -> p kc n", p=128)
    w_in_v, wgo_v, wco_v, wout_v, wq_v, uT_v = (wview(w) for w in (w_in, w_gla_o, w_conv_o, w_out, wq, uT))
    vt_v = vtab.rearrange("(q kc p) n -> q p kc n", p=128, kc=32)

    cnt = {"slot": 0, "acc": 0, "x": 0}

    def proj(wv, col0, rhs_t, mode, wname="w_in"):
        rhs_ap, rhs_key = rhs_t
        half = cnt["acc"] % 2
        cnt["acc"] += 1
        accs = [ps[2 * half][:, 0:256], ps[2 * half + 1][:, 0:256]]
        akeys = [("ps", 2 * half), ("ps", 2 * half + 1)]
        for kb in range(2):
            slot = cnt["slot"] % NSLOT
            cnt["slot"] += 1
            wt = wsl[slot]
            bkey = (wname, col0, kb)
            if bkey in scr_ids:
                bid = scr_ids[bkey]
                src = scr[bid].rearrange("p (k n) -> p k n", k=16)
                S.add("pool", (lambda eng, wt=wt, src=src: eng.dma_start(out=wt[:], in_=src)),
                      reads=[("scr", bid)], writes=[("wsl", slot)], dma=("wsl", slot))
            else:
                bid = len(scr_ids)
                scr_ids[bkey] = bid
                src = wv[:, kb * 16:(kb + 1) * 16, col0:col0 + 256]
                S.add("pool", (lambda eng, wt=wt, src=src: eng.dma_start(out=wt[:], in_=src)),
                      writes=[("wsl", slot)], dma=("wsl", slot))
                dst = scr[bid].rearrange("p (k n) -> p k n", k=16)
                S.add("sp", (lambda eng, wt=wt, dst=dst: eng.dma_start(out=dst, in_=wt[:])),
                      reads=[("wsl", slot)], writes=[("scr", bid)], dma=("scrst", bid % 16))

            def mm(eng, wt=wt, kb=kb):
                ins = None
                for a in range(2):
                    for k in range(16):
                        kc = kb * 16 + k
                        st = (kb == 0 and k == 0)
                        sp_ = (kb == 1 and k == 15)
                        if mode == "fm":
                            ins = eng.matmul(accs[a], lhsT=wt[:, k, a * 128:(a + 1) * 128], rhs=rhs_ap[:, kc, :],
                                             start=st, stop=sp_)
                        else:
                            ins = eng.matmul(accs[a], lhsT=rhs_ap[:, kc, a * 128:(a + 1) * 128], rhs=wt[:, k, :],
                                             start=st, stop=sp_)
                return ins
            S.add("pe", mm, reads=[("wsl", slot), rhs_key], writes=akeys)
        return accs, akeys

    def precache(wv, col0, wname):
        for kb in range(2):
            bkey = (wname, col0, kb)
            if bkey in scr_ids:
                continue
            slot = cnt["slot"] % NSLOT
            cnt["slot"] += 1
            wt = wsl[slot]
            bid = len(scr_ids)
            scr_ids[bkey] = bid
            src_ = wv[:, kb * 16:(kb + 1) * 16, col0:col0 + 256]
            S.add("pool", (lambda eng, wt=wt, src_=src_: eng.dma_start(out=wt[:], in_=src_)),
                  writes=[("wsl", slot)], dma=("wsl", slot))
            dst = scr[bid].rearrange("p (k n) -> p k n", k=16)
            S.add("sp", (lambda eng, wt=wt, dst=dst: eng.dma_start(out=dst, in_=wt[:])),
                  reads=[("wsl", slot)], writes=[("scr", bid)], dma=("scrst", bid % 16))

    pc_list = [(uT_v, c0 * 256, "uT") for c0 in range(64)] + [(vt_v[q], d2 * 256, ("v", q)) for q in range(4) for d2 in range(16)]

    dbg_col = [0]

    def dump(ap_f32, key, ncol):
        if dbg_out is None:
            return
        c0 = dbg_col[0]
        dbg_col[0] += ncol
        S.add("sp", (lambda eng: eng.dma_start(out=dbg_out[:, c0:c0 + ncol], in_=ap_f32)),
              reads=[key], dma=("out", "dbg"))

    S.add("sp", lambda e: e.dma_start(out=cst[:], in_=consts), writes=["cst"], dma=("ld", 0))
    S.add("sp", lambda e: e.dma_start(out=vec[:], in_=vecs), writes=["vec"], dma=("ld", 1))
    S.add("sp", lambda e: e.dma_start(out=mk[:], in_=maskt), writes=["mk"], dma=("ld", 2))
    S.add("sp", lambda e: e.dma_start(out=mkrep[:], in_=maskrep), writes=["mkrep"], dma=("ld", 3))
    S.add("sp", lambda e: e.dma_start(out=gn[:], in_=gng), writes=["gn"], dma=("ld", 4))
    S.add("sp", lambda e: e.dma_start(out=cw[:], in_=conv_w), writes=["cw"], dma=("ld", 5))
    S.add("pool", lambda e: e.dma_start(out=wA[:], in_=w_in_v[:, :, OFF_A:OFF_A + 16]), writes=["wA"], dma=("ld", 6))
    S.add("pool", lambda e: e.dma_start(out=wa2b[0:17, :], in_=wa2), writes=["wa2b"], dma=("ld", 7))
    S.add("pool", lambda e: e.dma_start(out=keyb[:], in_=keysT), writes=["keyb"], dma=("ld", 8))
    S.add("pool", lambda e: e.dma_start(out=cstb[:], in_=consts), writes=["cstb"], dma=("ld", 9))
    S.add("dve", lambda e: e.memset(Sst[:], 0.0), writes=[("S", h) for h in range(16)])
    S.add("dve", lambda e: e.memset(halo[:], 0.0), writes=["halo"])
    S.add("dve", lambda e: e.memset(a_aug[:], 1.0), writes=["a_aug"])
    S.add("dve", lambda e: e.memset(onesf[:], 1.0), writes=["onesf"])

    VG0, VB0, VCB, VCG, VCBB, VBCO, VG1, VB1, VG2, VB2 = range(10)
    alt = [0]

    def evac_engine():
        alt[0] ^= 1
        return "act" if alt[0] else "dve"

    def affine_evac(eng_name, out_ap, in_ap, scale_ap, bias_ap, reads, writes, func=None):
        if eng_name == "act" or func is not None:
            f = func if func is not None else AF.Identity
            S.add("act", lambda e: e.activation(out=out_ap, in_=in_ap, func=f, bias=bias_ap, scale=scale_ap),
                  reads=reads, writes=writes)
        else:
            S.add("dve", lambda e: e.tensor_scalar(out=out_ap, in0=in_ap, scalar1=scale_ap, scalar2=bias_ap,
                                                   op0=ALU.mult, op1=ALU.add), reads=reads, writes=writes)

    def ln_stats(src, src_key):
        def mm1(eng):
            ins = None
            for kc in range(KC):
                ins = eng.matmul(ps[5][:, 0:T], lhsT=onesf[:], rhs=src[:, kc, :], start=(kc == 0), stop=(kc == KC - 1))
            return ins
        S.add("pe", mm1, reads=[src_key, "onesf"], writes=[("ps", 5)])
        for kc in range(KC):
            tb = tmpa[kc % 2]
            S.add("act", lambda e, tb=tb, kc=kc: e.activation(out=tb[:], in_=src[:, kc, :], func=AF.Square),
                  reads=[src_key], writes=[("tmpa", kc % 2)])
            S.add("pe", lambda e, tb=tb, kc=kc: e.matmul(ps[5][:, T:2 * T], lhsT=onesf[:], rhs=tb[:],
                                                         start=(kc == 0), stop=(kc == KC - 1)),
                  reads=[("tmpa", kc % 2), "onesf"], writes=[("ps", 5)])
        S.add("dve", lambda e: e.tensor_scalar(out=mean[:], in0=ps[5][:, 0:T], scalar1=1.0 / D, scalar2=None,
                                               op0=ALU.mult), reads=[("ps", 5)], writes=["mean"])
        S.add("dve", lambda e: e.tensor_tensor(out=msq[:], in0=mean[:], in1=mean[:], op=ALU.mult),
              reads=["mean"], writes=["msq"])
        S.add("dve", lambda e: e.scalar_tensor_tensor(out=lnv[:], in0=ps[5][:, T:2 * T], scalar=1.0 / D, in1=msq[:],
                                                      op0=ALU.mult, op1=ALU.subtract),
              reads=[("ps", 5), "msq"], writes=["lnv"])
        S.add("dve", lambda e: e.tensor_scalar(out=lnv[:], in0=lnv[:], scalar1=LN_EPS, scalar2=None, op0=ALU.add),
              reads=["lnv"], writes=["lnv"])
        S.add("act", lambda e: e.activation(out=lnv[:], in_=lnv[:], func=AF.Ln), reads=["lnv"], writes=["lnv"])
        S.add("act", lambda e: e.activation(out=rstd[:], in_=lnv[:], func=AF.Exp, scale=-0.5),
              reads=["lnv"], writes=["rstd"])

    def ln_apply(src, src_key, kc, out_ap, out_key, gi, bi, func=None):
        tb = tmpb[kc % 2]
        S.add("dve", lambda e: e.tensor_tensor(out=tb[:], in0=src[:, kc, :], in1=mean[:], op=ALU.subtract),
              reads=[src_key, "mean"], writes=[("tmpb", kc % 2)])
        S.add("dve", lambda e: e.tensor_tensor(out=tb[:], in0=tb[:], in1=rstd[:], op=ALU.mult),
              reads=[("tmpb", kc % 2), "rstd"], writes=[("tmpb", kc % 2)])
        affine_evac("act", out_ap, tb[:], vec[:, gi, kc:kc + 1], vec[:, bi, kc:kc + 1],
                    reads=[("tmpb", kc % 2), "vec"], writes=[out_key], func=func)

    for ti in range(ntiles):
        own = ti >= n_pre
        last_pre = (ti == n_pre - 1)
        for s in range(NSUB):
            r0 = ti * T + s * 128
            xb, xk = (xbuf, "xbuf") if s == 0 else (B2f, "B2")
            S.add("sp", lambda e, r0=r0, xb=xb: e.dma_start(out=xb[:], in_=xs[r0:r0 + 128, :]),
                  writes=[xk], dma=("xbuf", s))

            def bst(e, xb=xb):
                ins = None
                for j in range(8):
                    ins = e.bn_stats(out=stats[:, j, :], in_=xb[:, j * 512:(j + 1) * 512])
                return ins
            S.add("dve", bst, reads=[xk], writes=["stats"])
            S.add("dve", lambda e: e.bn_aggr(out=mv[:], in_=stats[:].rearrange("p a b -> p (a b)")),
                  reads=["stats"], writes=["mv"])
            S.add("dve", lambda e: e.tensor_scalar(out=rs[:, 0:1], in0=mv[:, 1:2], scalar1=LN_EPS, scalar2=None,
                                                   op0=ALU.add), reads=["mv"], writes=["rs0"])
            S.add("act", lambda e: e.activation(out=rs[:, 0:1], in_=rs[:, 0:1], func=AF.Sqrt),
                  reads=["rs0"], writes=["rs0"])
            S.add("dve", lambda e: e.reciprocal(out=rs[:, 1:2], in_=rs[:, 0:1]), reads=["rs0"], writes=["rs1"])
            S.add("dve", lambda e, xb=xb: e.tensor_scalar(out=xb[:], in0=xb[:], scalar1=mv[:, 0:1], scalar2=rs[:, 1:2],
                                                   op0=ALU.subtract, op1=ALU.mult),
                  reads=[xk, "mv", "rs1"], writes=[xk])
            for g4 in range(8):
                bank = 4 + (g4 % 2)

                def tr(e, g4=g4, bank=bank, xb=xb):
                    ins = None
                    for q in range(4):
                        kc = g4 * 4 + q
                        ins = e.transpose(out=ps[bank][:, q * 128:(q + 1) * 128], in_=xb[:, kc * 128:(kc + 1) * 128],
                                          identity=identf)
                    return ins
                S.add("pe", tr, reads=[xk, "cst"], writes=[("ps", bank)])
                for q in range(4):
                    kc = g4 * 4 + q
                    affine_evac(evac_engine(), hT[:, kc, s * 128:(s + 1) * 128], ps[bank][:, q * 128:(q + 1) * 128],
                                vec[:, VG0, kc:kc + 1], vec[:, VB0, kc:kc + 1],
                                reads=[("ps", bank), "vec"], writes=["hT"])

        if stage == 1:
            if own:
                for kc in (0, 31):
                    S.add("dve", lambda e, kc=kc: e.tensor_copy(out=tmpa[0][:], in_=hT[:, kc, :]), reads=["hT"], writes=[("tmpa", 0)])
                    dump(tmpa[0][:], ("tmpa", 0), T)
            continue
        def mma(e):
            ins = None
            for kc in range(KC):
                ins = e.matmul(ps[4][0:16, 0:T], lhsT=wA[:, kc, :], rhs=hT[:, kc, :], start=(kc == 0), stop=(kc == KC - 1))
            return ins
        S.add("pe", mma, reads=["wA", "hT"], writes=[("ps", 4)])
        S.add("act", lambda e: e.activation(out=a_aug[0:16, :], in_=ps[4][0:16, 0:T], func=AF.Identity),
              reads=[("ps", 4)], writes=["a_aug"])

        def stA1(hp):
            def mmz(e, hp=hp):
                ins = None
                for s in range(NSUB):
                    ins = e.matmul(ps[4][:, s * 256:(s + 1) * 256], lhsT=a_aug[0:17, s * 128:(s + 1) * 128],
                                   rhs=wa2b[0:17, hp * 256:(hp + 1) * 256], start=True, stop=True)
                return ins
            S.add("pe", mmz, reads=["a_aug", "wa2b"], writes=[("ps", 4)])
            S.add("act", lambda e: e.activation(out=e1[:], in_=ps[4][:, :], func=AF.Exp, scale=-1.0),
                  reads=[("ps", 4)], writes=["e1"])
            S.add("act", lambda e: e.activation(out=lbuf[:].rearrange("p s c -> p (s c)"), in_=e1[:], func=AF.Ln, bias=1.0),
                  reads=["e1"], writes=["lbuf"])

        def stA2(hp):
            par = hp % 2

            def mmrev(e):
                ins = None
                for s in range(NSUB):
                    ins = e.matmul(ps[5][:, s * 256:(s + 1) * 256], lhsT=ublk, rhs=lbuf[:, s, :], start=True, stop=True)
                return ins
            S.add("pe", mmrev, reads=["lbuf", "cstb"], writes=[("ps", 5)])
            S.add("act", lambda e: e.activation(out=erev[:].rearrange("p s c -> p (s c)"), in_=ps[5][:, :], func=AF.Exp,
                                                scale=-1.0 / 16.0),
                  reads=[("ps", 5)], writes=["erev"])

            def mmtot(e):
                ins = None
                for h in range(2):
                    for s in range(NSUB):
                        c0 = (h * NSUB + s) * 2
                        ins = e.matmul(ps[4][:, c0:c0 + 2], lhsT=lbuf[:, s, h * 128:(h + 1) * 128],
                                       rhs=cind[:, 0:2], start=True, stop=True)
                return ins
            S.add("pe", mmtot, reads=["lbuf", "cstb"], writes=[("ps", 4)])
            S.add("act", lambda e: e.activation(out=dTt[:, par * 8:par * 8 + 8], in_=ps[4][:, 0:8], func=AF.Exp, scale=-1.0 / 16.0),
                  reads=[("ps", 4)], writes=[("dTt", par)])

        def stB(hp):
            par = hp % 2
            accs, akeys = proj(w_in_v, OFF_K + hp * 256, (hT, "hT"), "tm")
            for s in range(NSUB):
                S.add("dve", lambda e, s=s, a=accs[s]: e.tensor_tensor(out=kdec[par][:, s, :], in0=a, in1=erev[:, s, :], op=ALU.mult),
                      reads=[akeys[s], "erev"], writes=[("kdec", par, s)])
            for h in range(2):
                accs, akeys = proj(w_in_v, OFF_V + (hp * 2 + h) * 256, (hT, "hT"), "tm")
                for s in range(NSUB):
                    mcol = ti * NSUB + s
                    S.add("act", lambda e, s=s, h=h, a=accs[s], mcol=mcol: e.activation(
                        out=vbuf[par][h][:, s, :], in_=a, func=AF.Identity, scale=mk[:, mcol:mcol + 1]),
                        reads=[akeys[s], "mk"], writes=[("vbuf", par, h, s)])
            if own:
                accs, akeys = proj(w_in_v, OFF_Q + hp * 256, (hT, "hT"), "fm")
                for h in range(2):
                    S.add("dve", lambda e, h=h, a=accs[h]: e.tensor_scalar(out=qTb[par][:, h, :], in0=a, scalar1=128.0 ** -0.5,
                                                                         scalar2=None, op0=ALU.mult),
                          reads=[akeys[h]], writes=[("qTb", par, h)])

        def stC(hp):
            par = hp % 2
            order = [(h, c) for h in range(2) for c in range(4)] if own else [(h, c) for c in range(4) for h in range(2)]
            for (h, c) in order:
                s, hf = c // 2, c % 2
                rows = slice(hf * 64, hf * 64 + 64)
                hg = hp * 2 + h
                kvb = 6 if (own or h == 0) else 7
                kvp = ps[kvb][:, 0:256]
                S.add("pe", lambda e, h=h, s=s, rows=rows, kvp=kvp: e.matmul(
                    kvp, lhsT=kdec[par][rows, s, h * 128:(h + 1) * 128], rhs=vbuf[par][h][rows, s, :], start=True, stop=True),
                    reads=[("kdec", par, s), ("vbuf", par, h, s)], writes=[("ps", kvb)])
                S.add("dve", lambda e, h=h, hg=hg, c=c, kvp=kvp: e.scalar_tensor_tensor(
                    out=Sst[:, hg, :], in0=Sst[:, hg, :], scalar=dTt[:, par * 8 + h * 4 + c:par * 8 + h * 4 + c + 1], in1=kvp,
                    op0=ALU.mult, op1=ALU.add),
                    reads=[("S", hg), ("dTt", par), ("ps", kvb)], writes=[("S", hg)])
                if own:
                    S.add("act", lambda e, h=h, hg=hg: e.activation(out=Sb[:, h, :], in_=Sst[:, hg, :], func=AF.Identity),
                          reads=[("S", hg)], writes=[("Sb", h)])

                    def mmo(e, h=h, c=c):
                        ins = None
                        for dvs in range(2):
                            ins = e.matmul(ps[7][:, dvs * 256 + c * 64: dvs * 256 + c * 64 + 64],
                                           lhsT=Sb[:, h, dvs * 128:(dvs + 1) * 128], rhs=qTb[par][:, h, c * 64:(c + 1) * 64],
                                           start=True, stop=True)
                        return ins
                    S.add("pe", mmo, reads=[("Sb", h), ("qTb", par, h)], writes=[("ps", 7)])
                if own and c == 3:
                    S.add("act", lambda e, h=h: e.activation(out=sqb[:].rearrange("p a t -> p (a t)"), in_=ps[7][:, :],
                                                            func=AF.Square), reads=[("ps", 7)], writes=["sqb"])

                    def mms(e):
                        ins = None
                        for dvs in range(2):
                            ins = e.matmul(ps[5][:, 256:512], lhsT=onesb, rhs=sqb[:, dvs, :], start=(dvs == 0), stop=(dvs == 1))
                        return ins
                    S.add("pe", mms, reads=["sqb", "cstb"], writes=[("ps", 5)])
                    S.add("dve", lambda e: e.tensor_scalar(out=lnv[:], in0=ps[5][:, 256:512], scalar1=1.0 / 256.0,
                                                           scalar2=RMS_EPS, op0=ALU.mult, op1=ALU.add),
                          reads=[("ps", 5)], writes=["lnv"])
                    S.add("act", lambda e: e.activation(out=lnv[:], in_=lnv[:], func=AF.Ln), reads=["lnv"], writes=["lnv"])
                    S.add("act", lambda e: e.activation(out=rstd[:], in_=lnv[:], func=AF.Exp, scale=-0.5),
                          reads=["lnv"], writes=["rstd"])
                    accs, akeys = proj(w_in_v, OFF_G + hg * 256, (hT, "hT"), "fm")
                    for dvs in range(2):
                        S.add("act", lambda e, dvs=dvs, a=accs[dvs]: e.activation(out=tmpa[dvs][:], in_=a, func=AF.Silu),
                              reads=[akeys[dvs]], writes=[("tmpa", dvs)])
                        S.add("dve", lambda e, h=h, dvs=dvs: e.scalar_tensor_tensor(
                            out=tmpb[dvs][:], in0=ps[7][:, dvs * 256:(dvs + 1) * 256], scalar=gn[:, dvs:dvs + 1],
                            in1=rstd[:], op0=ALU.mult, op1=ALU.mult),
                            reads=[("ps", 7), "gn", "rstd"], writes=[("tmpb", dvs)])
                        S.add("dve", lambda e, hg=hg, dvs=dvs: e.tensor_tensor(
                            out=B2[:, hg * 2 + dvs, :], in0=tmpb[dvs][:], in1=tmpa[dvs][:], op=ALU.mult),
                            reads=[("tmpb", dvs), ("tmpa", dvs)], writes=["B2"])

        for hp in range(9):
            if hp < 8:
                stA1(hp)
            if hp >= 1:
                stC(hp - 1)
            if hp < 8:
                stA2(hp)
                stB(hp)
                if (not own) and (not last_pre) and pc_list:
                    precache(*pc_list.pop(0))

        if stage == 2:
            if own:
                for kc in (0, 1, 31):
                    S.add("dve", lambda e, kc=kc: e.tensor_copy(out=tmpa[0][:], in_=B2[:, kc, :]), reads=["B2"], writes=[("tmpa", 0)])
                    dump(tmpa[0][:], ("tmpa", 0), T)
            continue
        if not own and not last_pre:
            continue

        if own:
            for d2 in range(16):
                accy, ky = proj(wgo_v, d2 * 256, (B2, "B2"), "fm", "wgo")
                accg, kg = proj(w_in_v, OFF_GATE + d2 * 256, (hT, "hT"), "fm")
                for a in range(2):
                    dc = d2 * 2 + a
                    S.add("act", lambda e, a=a, ag=accg[a]: e.activation(out=tmpa[a][:], in_=ag, func=AF.Sigmoid),
                          reads=[kg[a]], writes=[("tmpa", a)])
                    S.add("dve", lambda e, a=a, dc=dc, ay=accy[a]: e.tensor_tensor(out=B3[:, dc, :], in0=ay, in1=tmpa[a][:],
                                                                                 op=ALU.mult),
                          reads=[ky[a], ("tmpa", a)], writes=["B3"])
        for cp in range(16):
            acca, ka = proj(w_in_v, OFF_C + cp * 256, (hT, "hT"), "fm")
            for a in range(2):
                S.add("act", lambda e, a=a, aa=acca[a]: e.activation(out=tmpa[a][:], in_=aa, func=AF.Identity),
                      reads=[ka[a]], writes=[("tmpa", a)])
            accb, kb_ = proj(w_in_v, OFF_C + D + cp * 256, (hT, "hT"), "fm")
            for a in range(2):
                kc = cp * 2 + a
                S.add("act", lambda e, a=a, ab=accb[a]: e.activation(out=tmpb[a][:], in_=ab, func=AF.Sigmoid),
                      reads=[kb_[a]], writes=[("tmpb", a)])
                S.add("dve", lambda e, a=a, kc=kc: e.tensor_copy(out=cgb[a][:, 0:32], in_=halo[:, kc, :]),
                      reads=["halo"], writes=[("cgb", a)])
                S.add("dve", lambda e, a=a: e.tensor_tensor(out=cgb[a][:, 32:32 + T], in0=tmpa[a][:], in1=tmpb[a][:], op=ALU.mult),
                      reads=[("tmpa", a), ("tmpb", a)], writes=[("cgb", a)])
                if last_pre:
                    S.add("dve", lambda e, a=a: e.tensor_tensor(out=cgb[a][:, 32:32 + T], in0=cgb[a][:, 32:32 + T], in1=mkrep[:],
                                                                op=ALU.mult), reads=[("cgb", a), "mkrep"], writes=[("cgb", a)])
                S.add("act", lambda e, a=a, kc=kc: e.activation(out=halo[:, kc, :], in_=cgb[a][:, T:T + 32], func=AF.Identity),
                      reads=[("cgb", a)], writes=["halo"])
            if own:
                for k in range(31):
                    for a in range(2):
                        kc = cp * 2 + a
                        if k == 0:
                            S.add("dve", lambda e, a=a, kc=kc: e.tensor_scalar(
                                out=B4[:, kc, :], in0=cgb[a][:, 2:2 + T], scalar1=cw[:, kc, 0:1], scalar2=vec[:, VCB, kc:kc + 1],
                                op0=ALU.mult, op1=ALU.add), reads=[("cgb", a), "cw", "vec"], writes=[("B4", kc)])
                        else:
                            S.add("dve", lambda e, a=a, kc=kc, k=k: e.scalar_tensor_tensor(
                                out=B4[:, kc, :], in0=cgb[a][:, 2 + k:2 + k + T], scalar=cw[:, kc, k:k + 1], in1=B4[:, kc, :],
                                op0=ALU.mult, op1=ALU.add), reads=[("cgb", a), "cw", ("B4", kc)], writes=[("B4", kc)])
        if not own:
            continue
        b4keys = [("B4", kc) for kc in range(KC)]
        S.add("dve", lambda e: e.engine_nop(), reads=b4keys, writes=["B4"])
        ln_stats(B4, "B4")
        for kc in range(KC):
            ln_apply(B4, "B4", kc, B2[:, kc, :], "B2", VCG, VCBB, func=AF.Silu)
        for d2 in range(16):
            accy, ky = proj(wco_v, d2 * 256, (B2, "B2"), "fm", "wco")
            accg, kg = proj(w_in_v, OFF_GATE + D + d2 * 256, (hT, "hT"), "fm")
            for a in range(2):
                dc = d2 * 2 + a
                S.add("act", lambda e, a=a, ag=accg[a]: e.activation(out=tmpa[a][:], in_=ag, func=AF.Sigmoid),
                      reads=[kg[a]], writes=[("tmpa", a)])
                S.add("dve", lambda e, a=a, dc=dc, ay=accy[a]: e.scalar_tensor_tensor(
                    out=tmpb[a][:], in0=ay, scalar=vec[:, VBCO, dc:dc + 1], in1=tmpa[a][:], op0=ALU.add, op1=ALU.mult),
                    reads=[ky[a], ("tmpa", a), "vec"], writes=[("tmpb", a)])
                S.add("dve", lambda e, a=a, dc=dc: e.tensor_tensor(out=B3[:, dc, :], in0=B3[:, dc, :], in1=tmpb[a][:], op=ALU.add),
                      reads=["B3", ("tmpb", a)], writes=["B3"])
        for d2 in range(16):
            accy, ky = proj(wout_v, d2 * 256, (B3, "B3"), "fm", "wout")
            for a in range(2):
                dc = d2 * 2 + a
                S.add("dve", lambda e, a=a, dc=dc, ay=accy[a]: e.scalar_tensor_tensor(
                    out=B4[:, dc, :], in0=hT[:, dc, :], scalar=ALPHA, in1=ay, op0=ALU.mult, op1=ALU.add),
                    reads=["hT", ky[a]], writes=["B4"])
        ln_stats(B4, "B4")
        for kc in range(KC):
            ln_apply(B4, "B4", kc, hT[:, kc, :], "hT", VG1, VB1)

        if stage == 3:
            for kc in (0, 31):
                S.add("dve", lambda e, kc=kc: e.tensor_copy(out=tmpa[0][:], in_=hT[:, kc, :]), reads=["hT"], writes=[("tmpa", 0)])
                dump(tmpa[0][:], ("tmpa", 0), T)
            continue
        for p in range(8):
            accq, kq = proj(wq_v, p * 256, (hT, "hT"), "fm", "wq")
            pq = pqT[p % 2]
            for a in range(2):
                S.add(evac_engine() if False else "act", lambda e, a=a, aq=accq[a], pq=pq: e.activation(out=pq[:, a, :], in_=aq, func=AF.Identity),
                      reads=[kq[a]], writes=[("pqT", p % 2, a)])

            def mmsc(e, p=p, pq=pq):
                ins = None
                for s in range(NSUB):
                    for hfq in range(2):
                        c0 = (s * 2 + hfq) * 128
                        ins = e.matmul(ps[4][:, c0:c0 + 128], lhsT=pq[:, hfq, s * 128:(s + 1) * 128],
                                       rhs=keyb[:, p * 2 + hfq, :], start=True, stop=True)
                return ins
            S.add("pe", mmsc, reads=[("pqT", p % 2, 0), ("pqT", p % 2, 1), "keyb"], writes=[("ps", 4)])
            for s in range(NSUB):
                S.add("act" if s == 0 else "dve", (lambda e, s=s, p=p: e.activation(
                    out=sc[:, s, :, p, :], in_=ps[4][:, s * 256:(s + 1) * 256].rearrange("q (a n) -> q a n", a=2), func=AF.Identity))
                    if s == 0 else (lambda e, s=s, p=p: e.tensor_copy(
                        out=sc[:, s, :, p, :], in_=ps[4][:, s * 256:(s + 1) * 256].rearrange("q (a n) -> q a n", a=2))),
                    reads=[("ps", 4)], writes=["xbuf"])
        for s in range(NSUB):
            def top_a(e, s=s):
                ins = None
                for hfq in range(2):
                    for p in range(8):
                        ins = e.max(out=v16[:, hfq, p, 0:8], in_=sc[:, s, hfq, p, :])
                return ins
            S.add("dve", top_a, reads=["xbuf"], writes=["v16a"])

            def top_b(e, s=s):
                ins = None
                for hfq in range(2):
                    for p in range(8):
                        ins = e.match_replace(out=scw[:, hfq * 8 + p, :], in_to_replace=v16[:, hfq, p, 0:8],
                                              in_values=sc[:, s, hfq, p, :], imm_value=NEG)
                return ins
            S.add("dve", top_b, reads=["xbuf", "v16a"], writes=["B2"])

            def top_c(e):
                ins = None
                for hfq in range(2):
                    for p in range(8):
                        ins = e.max(out=v16[:, hfq, p, 8:16], in_=scw[:, hfq * 8 + p, :])
                return ins
            S.add("dve", top_c, reads=["B2"], writes=["v16b"])

            def mkcand(e):
                ins = None
                for p in range(8):
                    ins = e.tensor_tensor(out=cand[:, p, :].rearrange("q (a b) -> q a b", a=16),
                                          in0=v16[:, 0, p, :].unsqueeze(2).broadcast_to([128, 16, 16]),
                                          in1=v16[:, 1, p, :].unsqueeze(1).broadcast_to([128, 16, 16]), op=ALU.add)
                return ins
            S.add("dve", mkcand, reads=["v16a", "v16b"], writes=["B2"])

            def ctop_a(e):
                ins = None
                for p in range(8):
                    ins = e.max(out=t16[:, p, 0:8], in_=cand[:, p, :])
                return ins
            S.add("dve", ctop_a, reads=["B2"], writes=["t16a"])

            def ctop_b(e):
                ins = None
                for p in range(8):
                    ins = e.match_replace(out=cand[:, p, :], in_to_replace=t16[:, p, 0:8], in_values=cand[:, p, :], imm_value=NEG)
                return ins
            S.add("dve", ctop_b, reads=["B2", "t16a"], writes=["B2"])

            def ctop_c(e):
                ins = None
                for p in range(8):
                    ins = e.max(out=t16[:, p, 8:16], in_=cand[:, p, :])
                return ins
            S.add("dve", ctop_c, reads=["B2"], writes=["t16b"])
            S.add("dve", lambda e: e.tensor_tensor(out=e16[:], in0=t16[:], in1=t16[:, :, 0:1].broadcast_to([128, 8, 16]),
                                                   op=ALU.subtract), reads=["t16a", "t16b"], writes=["e16"])
            S.add("act", lambda e: e.activation(out=e16[:], in_=e16[:], func=AF.Exp), reads=["e16"], writes=["e16"])
            S.add("dve", lambda e: e.tensor_reduce(out=zz[:], in_=e16[:], axis=AX.X, op=ALU.add), reads=["e16"], writes=["zz"])
            S.add("act", lambda e: e.activation(out=zz[:], in_=zz[:], func=AF.Ln), reads=["zz"], writes=["zz"])
            S.add("dve", lambda e: e.tensor_tensor(out=mz[:], in0=zz[:], in1=t16[:, :, 0], op=ALU.add),
                  reads=["zz", "t16a"], writes=["mz"])
            S.add("dve", lambda e: e.tensor_scalar(out=taue[:], in0=t16[:, :, 15], scalar1=-TOPK_EPS, scalar2=None, op0=ALU.add),
                  reads=["t16b"], writes=["taue"])
            S.add("dve", lambda e: e.tensor_tensor(out=taue[:], in0=taue[:], in1=mz[:], op=ALU.subtract),
                  reads=["taue", "mz"], writes=["taue"])
            S.add("dve", lambda e, s=s: e.tensor_tensor(out=thr[:, s, :, :], in0=taue[:].unsqueeze(2).broadcast_to([128, 8, 128]),
                                                        in1=sc[:, s, 0, :, :], op=ALU.subtract),
                  reads=["taue", "xbuf"], writes=[("thr", s)])
            S.add("dve", lambda e, s=s: e.tensor_tensor(out=sc[:, s, 1, :, :], in0=sc[:, s, 1, :, :],
                                                        in1=mz[:].unsqueeze(2).broadcast_to([128, 8, 128]), op=ALU.subtract),
                  reads=["mz", "xbuf"], writes=["xbuf"])
        ei = 0
        for qtr in range(4):
            pend = proj(uT_v, (qtr * 16) * 256, (hT, "hT"), "fm", "uT")
            for i2 in range(16):
                accu, ku = pend
                if i2 + 1 < 16:
                    pend = proj(uT_v, (qtr * 16 + i2 + 1) * 256, (hT, "hT"), "fm", "uT")
                for a in range(2):
                    i = (qtr * 16 + i2) * 2 + a
                    il = i2 * 2 + a
                    wtb = 6 + (i % 2)
                    for s in range(NSUB):
                        wsel = (i * NSUB + s) % 4
                        wb = wpb[wsel]
                        wk = ("wpb", wsel)
                        for p in range(8):
                            et = etmp[ei % 4]
                            ek = ("etmp", ei % 4)
                            ei += 1
                            S.add("act", lambda e, et=et, s=s, p=p, i=i: e.activation(
                                out=et[:], in_=sc[:, s, 1, p, :], func=AF.Exp, bias=sc[:, s, 0, p, i:i + 1], scale=1.0),
                                reads=["xbuf"], writes=[ek])
                            S.add("dve", lambda e, et=et, wb=wb, s=s, p=p, i=i: e.scalar_tensor_tensor(
                                out=wb[:, p, :], in0=sc[:, s, 1, p, :], scalar=thr[:, s, p, i:i + 1], in1=et[:],
                                op0=ALU.is_ge, op1=ALU.mult), reads=["xbuf", ("thr", s), ek], writes=[wk])

                        def mmw(e, wb=wb, s=s, wtb=wtb):
                            ins = None
                            for p in range(8):
                                ins = e.matmul(ps[wtb][:, s * 128:(s + 1) * 128], lhsT=wb[:, p, :], rhs=identb,
                                               start=(p == 0), stop=(p == 7))
                            return ins
                        S.add("pe", mmw, reads=[wk, "cstb"], writes=[("ps", wtb)])
                    S.add("act", lambda e, a=a, au=accu[a]: e.activation(out=tmpa[a][:], in_=au, func=AF.Gelu),
                          reads=[ku[a]], writes=[("tmpa", a)])
                    S.add("dve", lambda e, a=a, il=il, wtb=wtb: e.tensor_tensor(out=B3[:, il, :], in0=tmpa[a][:], in1=ps[wtb][:, 0:T], op=ALU.mult),
                          reads=[("tmpa", a), ("ps", wtb)], writes=["B3"])
            for d2 in range(16):
                accv, kv_ = proj(vt_v[qtr], d2 * 256, (B3, "B3"), "fm", ("v", qtr))
                for a in range(2):
                    dc = d2 * 2 + a
                    if qtr == 0:
                        S.add("dve", lambda e, dc=dc, av=accv[a]: e.scalar_tensor_tensor(
                            out=B4[:, dc, :], in0=hT[:, dc, :], scalar=ALPHA, in1=av, op0=ALU.mult, op1=ALU.add),
                            reads=["hT", kv_[a]], writes=["B4"])
                    else:
                        S.add("dve", lambda e, dc=dc, av=accv[a]: e.tensor_tensor(out=B4[:, dc, :], in0=B4[:, dc, :], in1=av, op=ALU.add),
                              reads=["B4", kv_[a]], writes=["B4"])
        ln_stats(B4, "B4")
        for kc in range(KC):
            ln_apply(B4, "B4", kc, B4[:, kc, :], "B4", VG2, VB2)
        for s in range(NSUB):
            for g4 in range(8):
                bank = 4 + (g4 % 2)

                def tr2(e, g4=g4, bank=bank, s=s):
                    ins = None
                    for q in range(4):
                        kc = g4 * 4 + q
                        ins = e.transpose(out=ps[bank][:, q * 128:(q + 1) * 128], in_=B4[:, kc, s * 128:(s + 1) * 128],
                                          identity=identf)
                    return ins
                S.add("pe", tr2, reads=["B4", "cst"], writes=[("ps", bank)])
                if g4 % 2 == 0:
                    S.add("act", lambda e, g4=g4, bank=bank: e.activation(out=xbuf[:, g4 * 512:(g4 + 1) * 512], in_=ps[bank][:, :], func=AF.Identity),
                          reads=[("ps", bank)], writes=["xbuf"])
                else:
                    S.add("dve", lambda e, g4=g4, bank=bank: e.tensor_copy(out=xbuf[:, g4 * 512:(g4 + 1) * 512], in_=ps[bank][:, :]),
                          reads=[("ps", bank)], writes=["xbuf"])
            r0 = (ti - n_pre) * T + s * 128
            S.add("sp", lambda e, r0=r0: e.dma_start(out=out[r0:r0 + 128, :], in_=xbuf[:]), reads=["xbuf"], dma=("out", 0))

    S.emit(nc, es)
    es.close()
    return nc


def _consts():
    c = np.zeros((128, 4, 128), np.float32)
    c[:, 0, :] = np.eye(128, dtype=np.float32)
    c[:, 1, :] = 1.0
    j = np.arange(128)[:, None]
    cc = np.arange(128)[None, :]
    c[:, 2, :] = ((j // 64 == cc // 64) & (j > cc)).astype(np.float32)
    c[:, 3, 0] = (np.arange(128) < 64)
    c[:, 3, 1] = (np.arange(128) >= 64)
    return c


def _chunkvec(v):
    return np.ascontiguousarray(np.asarray(v, np.float32).reshape(KC, 128).T)


def prepare(inputs, n_pre=NPRE_TILES, n_own=NOWN_TILES, cores=range(8)):
    x = np.asarray(inputs["x"], np.float32)
    meta = np.asarray(inputs["meta"], np.float32)
    f = lambda k: np.asarray(inputs[k], np.float32)
    shared = {
        "w_in": f("w_in")[0],
        "wa2": np.ascontiguousarray(np.concatenate([f("w_a2")[0], f("b_a")[0][None, :]], axis=0)),
        "gng": np.ascontiguousarray(f("gla_norm_g")[0].reshape(2, 128).T),
        "w_gla_o": f("w_gla_o")[0],
        "conv_w": np.ascontiguousarray(f("conv_w")[0].T.reshape(KC, 128, 31).transpose(1, 0, 2)),
        "w_conv_o": f("w_conv_o")[0],
        "w_out": f("w_out")[0],
        "wq": f("peer_wq")[0],
        "keysT": np.ascontiguousarray(f("peer_keys")[0].reshape(16, 128, 128).transpose(2, 0, 1)),
        "uT": np.ascontiguousarray(f("peer_u")[0].T),
        "vtab": f("peer_v")[0],
        "consts": _consts(),
    }
    vecs = np.zeros((128, 11, KC), np.float32)
    for i, v in enumerate([f("ln0_g"), f("ln0_b"), f("conv_b")[0], f("conv_ln_g")[0], f("conv_ln_b")[0], f("b_conv_o")[0],
                           f("ln1_g")[0], f("ln1_b")[0], f("ln2_g")[0], f("ln2_b")[0]]):
        vecs[:, i, :] = _chunkvec(v)
    shared["vecs"] = vecs
    npre_tok = n_pre * T
    ntok = (n_pre + n_own) * T
    maps = []
    for c in cores:
        b, j = c // 4, c % 4
        xs = np.zeros((ntok, D), np.float32)
        mask = np.zeros((ntok,), np.float32)
        nvalid = j * SEG
        own0 = npre_tok
        if nvalid > 0:
            xs[own0 - nvalid:own0] = x[b, 0:nvalid]
            mask[own0 - nvalid:own0] = 1.0
        m0 = own0 - nvalid - 16
        xs[m0:m0 + 16] = meta
        mask[m0:m0 + 16] = 1.0
        xs[own0:own0 + n_own * T] = x[b, j * SEG:j * SEG + n_own * T]
        mask[own0:] = 1.0
        mt = np.ascontiguousarray(mask.reshape(-1, 128).T)
        lp = n_pre - 1
        mrep = np.ascontiguousarray(np.broadcast_to(mask[lp * T:(lp + 1) * T][None, :], (128, T)))
        d = dict(shared)
        d.update({"xs": xs, "maskt": mt, "maskrep": mrep})
        maps.append(d)
    return maps


def kernel(**inputs):
    nc = build()
    maps = prepare(inputs)
    res = run_bass_kernel_spmd(nc, maps, core_ids=list(range(8)))
    outp = np.zeros((2, 8192, D), np.float32)
    for c in range(8):
        b, j = c // 4, c % 4
        outp[b, j * SEG:(j + 1) * SEG] = res.results[c]["out"]
    return outp
```

```python
import numpy as np
from contextlib import ExitStack
import concourse.bass as bass
import concourse.mybir as mybir
from concourse.bass_utils import run_bass_kernel_spmd

F32 = mybir.dt.float32
BF16 = mybir.dt.bfloat16
AF = mybir.ActivationFunctionType
ALU = mybir.AluOpType
AX = mybir.AxisListType

D = 4096
KC = 32
T = 256
NSUB = 2
SEG = 2048
NPRE_TILES = 25
NOWN_TILES = 8
OFF_Q, OFF_K, OFF_V, OFF_G, OFF_A, OFF_C, OFF_GATE = 0, 2048, 4096, 8192, 12288, 12304, 20496
IN_WIDTH = 28688
ALPHA = 2.0 ** 0.25
LN_EPS = 1e-5
RMS_EPS = 1e-6
NEXP = 16384
TOPK_EPS = 2e-5
NB = 2
NEG = -1.0e30

ENGS = ["pe", "act", "dve", "pool", "sp"]


class Op:
    __slots__ = ("eng", "fn", "deps", "dma", "idx", "sig", "sigcnt", "dcnt")

    def __init__(self, eng, fn, dma):
        self.eng, self.fn, self.dma = eng, fn, dma
        self.deps = []
        self.sig = False
        self.sigcnt = 0
        self.dcnt = 0


class Sched:
    def __init__(self):
        self.ops = {e: [] for e in ENGS}
        self.lastw = {}
        self.readers = {}
        self.dma_keys = {}

    def add(self, eng, fn, reads=(), writes=(), dma=None):
        op = Op(eng, fn, dma)
        deps = {}
        for r in reads:
            w = self.lastw.get(r)
            if w is not None:
                deps[id(w)] = w
        for wr in writes:
            w = self.lastw.get(wr)
            if w is not None:
                deps[id(w)] = w
            for rd in self.readers.get(wr, ()):
                deps[id(rd)] = rd
        deps.pop(id(op), None)
        op.deps = [d for d in deps.values()
                   if not (d.eng == "pe" and eng == "pe" and d.dma is None)]
        for r in reads:
            self.readers.setdefault(r, []).append(op)
        for wr in writes:
            self.lastw[wr] = op
            self.readers[wr] = []
        if dma is not None:
            self.dma_keys[dma] = self.dma_keys.get(dma, 0) + 1
            op.dcnt = self.dma_keys[dma]
        op.idx = len(self.ops[eng])
        self.ops[eng].append(op)
        return op

    def emit(self, nc, es, final_waits_eng="sp"):
        for e in ENGS:
            for op in self.ops[e]:
                for d in op.deps:
                    if d.dma is None:
                        d.sig = True
        esem = {e: es.enter_context(nc.semaphore("sem_" + e)) for e in ENGS}
        dsem = {k: es.enter_context(nc.semaphore("dsem_%d" % i)) for i, k in enumerate(self.dma_keys)}
        for e in ENGS:
            c = 0
            for op in self.ops[e]:
                if op.sig:
                    c += 1
                op.sigcnt = c
        block = es.enter_context(nc.Block())
        ops = self.ops
        final = [(dsem[k], 16 * n) for k, n in self.dma_keys.items() if isinstance(k, tuple) and k[0] == "out"]

        def body_for(e):
            def body(eng):
                waited = {}
                for op in ops[e]:
                    for d in op.deps:
                        if d.dma is not None:
                            sem, val = dsem[d.dma], 16 * d.dcnt
                        else:
                            sem, val = esem[d.eng], d.sigcnt
                        key = id(sem)
                        if waited.get(key, 0) >= val:
                            continue
                        waited[key] = val
                        eng.wait_ge(sem, val)
                    ins = op.fn(eng)
                    if op.dma is not None:
                        ins.then_inc(dsem[op.dma], 16)
                    elif op.sig:
                        ins.then_inc(esem[e], 1)
                if e == final_waits_eng:
                    for sem, val in final:
                        eng.wait_ge(sem, val)
            return body

        block.tensor(body_for("pe"))
        block.scalar(body_for("act"))
        block.vector(body_for("dve"))
        block.gpsimd(body_for("pool"))
        block.sync(body_for("sp"))


def build(n_pre=NPRE_TILES, n_own=NOWN_TILES, dbg=None, stage=9):
    nc = bass.Bass("TRN2", target_bir_lowering=False)
    ntiles = n_pre + n_own
    S = Sched()
    es = ExitStack()

    def dram(name, shape, dt=F32, kind="ExternalInput"):
        return nc.dram_tensor(name, list(shape), dt, kind=kind).ap()

    xs = dram("xs", [ntiles * T, D])
    maskt = dram("maskt", [128, ntiles * NSUB])
    maskrep = dram("maskrep", [128, T])
    w_in = dram("w_in", [D, IN_WIDTH])
    wa2 = dram("wa2", [17, 2048])
    gng = dram("gng", [128, 2])
    w_gla_o = dram("w_gla_o", [D, D])
    conv_w = dram("conv_w", [128, KC, 31])
    w_conv_o = dram("w_conv_o", [D, D])
    w_out = dram("w_out", [D, D])
    wq = dram("wq", [D, 2048])
    keysT = dram("keysT", [128, 16, 128])
    uT = dram("uT", [D, NEXP])
    vtab = dram("vtab", [NEXP, D])
    vecs = dram("vecs", [128, 11, KC])
    consts = dram("consts", [128, 4, 128])
    out = dram("out", [n_own * T, D], F32, kind="ExternalOutput")
    scrs = [nc.dram_tensor("scr%d" % i, [200, 128, 4096], BF16, kind="Internal").ap() for i in range(3)]

    class _Scr:
        def __getitem__(self, bid):
            return scrs[bid // 200][bid % 200]
    scr = _Scr()
    scr_ids = {}
    dbg_out = None
    if dbg:
        dbg_out = dram("dbg", [128, dbg["cols"]], F32, kind="ExternalOutput")

    def sb(name, shape, dt=F32):
        return es.enter_context(nc.sbuf_tensor(name, list(shape), dt))

    hT = sb("hT", [128, KC, T], BF16)
    B2f = sb("B2f", [128, D], F32)
    B2 = B2f[:].bitcast(BF16).rearrange("p (k t) -> p k t", k=KC)
    B3 = sb("B3", [128, KC, T], BF16)
    B4 = sb("B4", [128, KC, T], F32)
    xbuf = sb("xbuf", [128, D], F32)
    NSLOT = 3
    wsl = [sb("wsl%d" % i, [128, 16, 256], BF16) for i in range(NSLOT)]
    Sst = sb("Sst", [128, 16, 256], F32)
    Sb = sb("Sb", [128, 2, 256], BF16)
    vec = sb("vec", [128, 11, KC], F32)
    cst = sb("cst", [128, 4, 128], F32)
    identf = cst[:, 0, :]
    cstb = sb("cstb", [128, 4, 128], BF16)
    identb, onesb, ublk, cind = cstb[:, 0, :], cstb[:, 1, :], cstb[:, 2, :], cstb[:, 3, 0:4]
    onesf = sb("onesf", [128, 128], F32)
    mk = sb("mk", [128, ntiles * NSUB], F32)
    mkrep = sb("mkrep", [128, T], F32)
    wA = sb("wA", [128, KC, 16], BF16)
    wa2b = sb("wa2b", [32, 2048], BF16)
    gn = sb("gn", [128, 2], F32)
    cw = sb("cw", [128, KC, 31], F32)
    keyb = sb("keyb", [128, 16, 128], BF16)
    halo = sb("halo", [128, KC, 32], F32)
    stats = sb("stats", [128, 8, 6], F32)
    mv = sb("mv", [128, 2], F32)
    rs = sb("rs", [128, 2], F32)
    a_aug = sb("a_aug", [32, T], BF16)
    e1 = sb("e1", [128, 512], F32)
    lbuf = sb("lbuf", [128, NSUB, 256], BF16)
    erev = sb("erev", [128, NSUB, 256], F32)
    dTt = sb("dTt", [128, 16], F32)
    kdec = [sb("kdec%d" % i, [128, NSUB, 256], BF16) for i in range(2)]
    vbuf = [[sb("vbuf%d_%d" % (i, h), [128, NSUB, 256], BF16) for h in range(2)] for i in range(2)]
    qTb = [sb("qTb%d" % i, [128, 2, T], BF16) for i in range(2)]
    sqb = sb("sqb", [128, 2, T], BF16)
    lnv = sb("lnv", [128, T], F32)
    rstd = sb("rstd", [128, T], F32)
    mean = sb("mean", [128, T], F32)
    msq = sb("msq", [128, T], F32)
    tmpa = [sb("tmpa%d" % i, [128, T], F32) for i in range(2)]
    tmpb = [sb("tmpb%d" % i, [128, T], F32) for i in range(2)]
    cgb = [sb("cgb%d" % i, [128, 32 + T], F32) for i in range(2)]
    sc = xbuf[:].rearrange("p (s a h n) -> p s a h n", s=NSUB, a=2, h=8)
    scw = B2f[:, 0:2048].rearrange("p (a n) -> p a n", a=16)
    v16 = sb("v16", [128, 2, 8, 16], F32)
    cand = B2f[:, 2048:4096].rearrange("p (a n) -> p a n", a=8)
    t16 = sb("t16", [128, 8, 16], F32)
    e16 = sb("e16", [128, 8, 16], F32)
    zz = sb("zz", [128, 8], F32)
    mz = sb("mz", [128, 8], F32)
    taue = sb("taue", [128, 8], F32)
    thr = sb("thr", [128, NSUB, 8, 128], F32)
    etmp = [sb("etmp%d" % i, [128, 128], F32) for i in range(4)]
    wpb = [sb("wpb%d" % i, [128, 8, 128], BF16) for i in range(4)]
    pqT = [sb("pqT%d" % i, [128, 2, T], BF16) for i in range(2)]

    ps = [es.enter_context(nc.psum_tensor("ps%d" % i, [128, 512], F32)) for i in range(8)]

    def wview(w):
        return w.rearrange("(kc p) n -> p kc n", p=128)
    w_in_v, wgo_v, wco_v, wout_v, wq_v, uT_v = (wview(w) for w in (w_in, w_gla_o, w_conv_o, w_out, wq, uT))
    vt_v = vtab.rearrange("(q kc p) n -> q p kc n", p=128, kc=32)

    cnt = {"slot": 0, "acc": 0, "x": 0}

    def proj(wv, col0, rhs_t, mode, wname="w_in"):
        rhs_ap, rhs_key = rhs_t
        half = cnt["acc"] % 2
        cnt["acc"] += 1
        accs = [ps[2 * half][:, 0:256], ps[2 * half + 1][:, 0:256]]
        akeys = [("ps", 2 * half), ("ps", 2 * half + 1)]
        for kb in range(2):
            slot = cnt["slot"] % NSLOT
            cnt["slot"] += 1
            wt = wsl[slot]
            bkey = (wname, col0, kb)
            if bkey in scr_ids:
                bid = scr_ids[bkey]
                src = scr[bid].rearrange("p (k n) -> p k n", k=16)
                S.add("pool", (lambda eng, wt=wt, src=src: eng.dma_start(out=wt[:], in_=src)),
                      reads=[("scr", bid)], writes=[("wsl", slot)], dma=("wsl", slot))
            else:
                bid = len(scr_ids)
                scr_ids[bkey] = bid
                src = wv[:, kb * 16:(kb + 1) * 16, col0:col0 + 256]
                S.add("pool", (lambda eng, wt=wt, src=src: eng.dma_start(out=wt[:], in_=src)),
                      writes=[("wsl", slot)], dma=("wsl", slot))
                dst = scr[bid].rearrange("p (k n) -> p k n", k=16)
                S.add("sp", (lambda eng, wt=wt, dst=dst: eng.dma_start(out=dst, in_=wt[:])),
                      reads=[("wsl", slot)], writes=[("scr", bid)], dma=("scrst", bid % 16))

            def mm(eng, wt=wt, kb=kb):
                ins = None
                for a in range(2):
                    for k in range(16):
                        kc = kb * 16 + k
                        st = (kb == 0 and k == 0)
                        sp_ = (kb == 1 and k == 15)
                        if mode == "fm":
                            ins = eng.matmul(accs[a], lhsT=wt[:, k, a * 128:(a + 1) * 128], rhs=rhs_ap[:, kc, :],
                                             start=st, stop=sp_)
                        else:
                            ins = eng.matmul(accs[a], lhsT=rhs_ap[:, kc, a * 128:(a + 1) * 128], rhs=wt[:, k, :],
                                             start=st, stop=sp_)
                return ins
            S.add("pe", mm, reads=[("wsl", slot), rhs_key], writes=akeys)
        return accs, akeys

    dbg_col = [0]

    def dump(ap_f32, key, ncol):
        if dbg_out is None:
            return
        c0 = dbg_col[0]
        dbg_col[0] += ncol
        S.add("sp", (lambda eng: eng.dma_start(out=dbg_out[:, c0:c0 + ncol], in_=ap_f32)),
              reads=[key], dma=("out", "dbg"))

    S.add("sp", lambda e: e.dma_start(out=cst[:], in_=consts), writes=["cst"], dma=("ld", 0))
    S.add("sp", lambda e: e.dma_start(out=vec[:], in_=vecs), writes=["vec"], dma=("ld", 1))
    S.add("sp", lambda e: e.dma_start(out=mk[:], in_=maskt), writes=["mk"], dma=("ld", 2))
    S.add("sp", lambda e: e.dma_start(out=mkrep[:], in_=maskrep), writes=["mkrep"], dma=("ld", 3))
    S.add("sp", lambda e: e.dma_start(out=gn[:], in_=gng), writes=["gn"], dma=("ld", 4))
    S.add("sp", lambda e: e.dma_start(out=cw[:], in_=conv_w), writes=["cw"], dma=("ld", 5))
    S.add("pool", lambda e: e.dma_start(out=wA[:], in_=w_in_v[:, :, OFF_A:OFF_A + 16]), writes=["wA"], dma=("ld", 6))
    S.add("pool", lambda e: e.dma_start(out=wa2b[0:17, :], in_=wa2), writes=["wa2b"], dma=("ld", 7))
    S.add("pool", lambda e: e.dma_start(out=keyb[:], in_=keysT), writes=["keyb"], dma=("ld", 8))
    S.add("pool", lambda e: e.dma_start(out=cstb[:], in_=consts), writes=["cstb"], dma=("ld", 9))
    S.add("dve", lambda e: e.memset(Sst[:], 0.0), writes=[("S", h) for h in range(16)])
    S.add("dve", lambda e: e.memset(halo[:], 0.0), writes=["halo"])
    S.add("dve", lambda e: e.memset(a_aug[:], 1.0), writes=["a_aug"])
    S.add("dve", lambda e: e.memset(onesf[:], 1.0), writes=["onesf"])

    VG0, VB0, VCB, VCG, VCBB, VBCO, VG1, VB1, VG2, VB2 = range(10)
    alt = [0]

    def evac_engine():
        alt[0] ^= 1
        return "act" if alt[0] else "dve"

    def affine_evac(eng_name, out_ap, in_ap, scale_ap, bias_ap, reads, writes, func=None):
        if eng_name == "act" or func is not None:
            f = func if func is not None else AF.Identity
            S.add("act", lambda e: e.activation(out=out_ap, in_=in_ap, func=f, bias=bias_ap, scale=scale_ap),
                  reads=reads, writes=writes)
        else:
            S.add("dve", lambda e: e.tensor_scalar(out=out_ap, in0=in_ap, scalar1=scale_ap, scalar2=bias_ap,
                                                   op0=ALU.mult, op1=ALU.add), reads=reads, writes=writes)

    def ln_stats(src, src_key):
        def mm1(eng):
            ins = None
            for kc in range(KC):
                ins = eng.matmul(ps[5][:, 0:T], lhsT=onesf[:], rhs=src[:, kc, :], start=(kc == 0), stop=(kc == KC - 1))
            return ins
        S.add("pe", mm1, reads=[src_key, "onesf"], writes=[("ps", 5)])
        for kc in range(KC):
            tb = tmpa[kc % 2]
            S.add("act", lambda e, tb=tb, kc=kc: e.activation(out=tb[:], in_=src[:, kc, :], func=AF.Square),
                  reads=[src_key], writes=[("tmpa", kc % 2)])
            S.add("pe", lambda e, tb=tb, kc=kc: e.matmul(ps[5][:, T:2 * T], lhsT=onesf[:], rhs=tb[:],
                                                         start=(kc == 0), stop=(kc == KC - 1)),
                  reads=[("tmpa", kc % 2), "onesf"], writes=[("ps", 5)])
        S.add("dve", lambda e: e.tensor_scalar(out=mean[:], in0=ps[5][:, 0:T], scalar1=1.0 / D, scalar2=None,
                                               op0=ALU.mult), reads=[("ps", 5)], writes=["mean"])
        S.add("dve", lambda e: e.tensor_tensor(out=msq[:], in0=mean[:], in1=mean[:], op=ALU.mult),
              reads=["mean"], writes=["msq"])
        S.add("dve", lambda e: e.scalar_tensor_tensor(out=lnv[:], in0=ps[5][:, T:2 * T], scalar=1.0 / D, in1=msq[:],
                                                      op0=ALU.mult, op1=ALU.subtract),
              reads=[("ps", 5), "msq"], writes=["lnv"])
        S.add("dve", lambda e: e.tensor_scalar(out=lnv[:], in0=lnv[:], scalar1=LN_EPS, scalar2=None, op0=ALU.add),
              reads=["lnv"], writes=["lnv"])
        S.add("act", lambda e: e.activation(out=lnv[:], in_=lnv[:], func=AF.Ln), reads=["lnv"], writes=["lnv"])
        S.add("act", lambda e: e.activation(out=rstd[:], in_=lnv[:], func=AF.Exp, scale=-0.5),
              reads=["lnv"], writes=["rstd"])

    def ln_apply(src, src_key, kc, out_ap, out_key, gi, bi, func=None):
        tb = tmpb[kc % 2]
        S.add("dve", lambda e: e.tensor_tensor(out=tb[:], in0=src[:, kc, :], in1=mean[:], op=ALU.subtract),
              reads=[src_key, "mean"], writes=[("tmpb", kc % 2)])
        S.add("dve", lambda e: e.tensor_tensor(out=tb[:], in0=tb[:], in1=rstd[:], op=ALU.mult),
              reads=[("tmpb", kc % 2), "rstd"], writes=[("tmpb", kc % 2)])
        affine_evac("act", out_ap, tb[:], vec[:, gi, kc:kc + 1], vec[:, bi, kc:kc + 1],
                    reads=[("tmpb", kc % 2), "vec"], writes=[out_key], func=func)

    for ti in range(ntiles):
        own = ti >= n_pre
        last_pre = (ti == n_pre - 1)
        for s in range(NSUB):
            r0 = ti * T + s * 128
            xb, xk = (xbuf, "xbuf") if s == 0 else (B2f, "B2")
            S.add("sp", lambda e, r0=r0, xb=xb: e.dma_start(out=xb[:], in_=xs[r0:r0 + 128, :]),
                  writes=[xk], dma=("xbuf", s))

            def bst(e, xb=xb):
                ins = None
                for j in range(8):
                    ins = e.bn_stats(out=stats[:, j, :], in_=xb[:, j * 512:(j + 1) * 512])
                return ins
            S.add("dve", bst, reads=[xk], writes=["stats"])
            S.add("dve", lambda e: e.bn_aggr(out=mv[:], in_=stats[:].rearrange("p a b -> p (a b)")),
                  reads=["stats"], writes=["mv"])
            S.add("dve", lambda e: e.tensor_scalar(out=rs[:, 0:1], in0=mv[:, 1:2], scalar1=LN_EPS, scalar2=None,
                                                   op0=ALU.add), reads=["mv"], writes=["rs0"])
            S.add("act", lambda e: e.activation(out=rs[:, 0:1], in_=rs[:, 0:1], func=AF.Sqrt),
                  reads=["rs0"], writes=["rs0"])
            S.add("dve", lambda e: e.reciprocal(out=rs[:, 1:2], in_=rs[:, 0:1]), reads=["rs0"], writes=["rs1"])
            S.add("dve", lambda e, xb=xb: e.tensor_scalar(out=xb[:], in0=xb[:], scalar1=mv[:, 0:1], scalar2=rs[:, 1:2],
                                                   op0=ALU.subtract, op1=ALU.mult),
                  reads=[xk, "mv", "rs1"], writes=[xk])
            for g4 in range(8):
                bank = 4 + (g4 % 2)

                def tr(e, g4=g4, bank=bank, xb=xb):
                    ins = None
                    for q in range(4):
                        kc = g4 * 4 + q
                        ins = e.transpose(out=ps[bank][:, q * 128:(q + 1) * 128], in_=xb[:, kc * 128:(kc + 1) * 128],
                                          identity=identf)
                    return ins
                S.add("pe", tr, reads=[xk, "cst"], writes=[("ps", bank)])
                for q in range(4):
                    kc = g4 * 4 + q
                    affine_evac(evac_engine(), hT[:, kc, s * 128:(s + 1) * 128], ps[bank][:, q * 128:(q + 1) * 128],
                                vec[:, VG0, kc:kc + 1], vec[:, VB0, kc:kc + 1],
                                reads=[("ps", bank), "vec"], writes=["hT"])

        if stage == 1:
            if own:
                for kc in (0, 31):
                    S.add("dve", lambda e, kc=kc: e.tensor_copy(out=tmpa[0][:], in_=hT[:, kc, :]), reads=["hT"], writes=[("tmpa", 0)])
                    dump(tmpa[0][:], ("tmpa", 0), T)
            continue
        def mma(e):
            ins = None
            for kc in range(KC):
                ins = e.matmul(ps[4][0:16, 0:T], lhsT=wA[:, kc, :], rhs=hT[:, kc, :], start=(kc == 0), stop=(kc == KC - 1))
            return ins
        S.add("pe", mma, reads=["wA", "hT"], writes=[("ps", 4)])
        S.add("act", lambda e: e.activation(out=a_aug[0:16, :], in_=ps[4][0:16, 0:T], func=AF.Identity),
              reads=[("ps", 4)], writes=["a_aug"])

        def stA1(hp):
            def mmz(e, hp=hp):
                ins = None
                for s in range(NSUB):
                    ins = e.matmul(ps[4][:, s * 256:(s + 1) * 256], lhsT=a_aug[0:17, s * 128:(s + 1) * 128],
                                   rhs=wa2b[0:17, hp * 256:(hp + 1) * 256], start=True, stop=True)
                return ins
            S.add("pe", mmz, reads=["a_aug", "wa2b"], writes=[("ps", 4)])
            S.add("act", lambda e: e.activation(out=e1[:], in_=ps[4][:, :], func=AF.Exp, scale=-1.0),
                  reads=[("ps", 4)], writes=["e1"])
            S.add("act", lambda e: e.activation(out=lbuf[:].rearrange("p s c -> p (s c)"), in_=e1[:], func=AF.Ln, bias=1.0),
                  reads=["e1"], writes=["lbuf"])

        def stA2(hp):
            par = hp % 2

            def mmrev(e):
                ins = None
                for s in range(NSUB):
                    ins = e.matmul(ps[5][:, s * 256:(s + 1) * 256], lhsT=ublk, rhs=lbuf[:, s, :], start=True, stop=True)
                return ins
            S.add("pe", mmrev, reads=["lbuf", "cstb"], writes=[("ps", 5)])
            S.add("act", lambda e: e.activation(out=erev[:].rearrange("p s c -> p (s c)"), in_=ps[5][:, :], func=AF.Exp,
                                                scale=-1.0 / 16.0),
                  reads=[("ps", 5)], writes=["erev"])

            def mmtot(e):
                ins = None
                for h in range(2):
                    for s in range(NSUB):
                        c0 = (h * NSUB + s) * 2
                        ins = e.matmul(ps[4][:, c0:c0 + 2], lhsT=lbuf[:, s, h * 128:(h + 1) * 128],
                                       rhs=cind[:, 0:2], start=True, stop=True)
                return ins
            S.add("pe", mmtot, reads=["lbuf", "cstb"], writes=[("ps", 4)])
            S.add("act", lambda e: e.activation(out=dTt[:, par * 8:par * 8 + 8], in_=ps[4][:, 0:8], func=AF.Exp, scale=-1.0 / 16.0),
                  reads=[("ps", 4)], writes=[("dTt", par)])

        def stB(hp):
            par = hp % 2
            accs, akeys = proj(w_in_v, OFF_K + hp * 256, (hT, "hT"), "tm")
            for s in range(NSUB):
                S.add("dve", lambda e, s=s, a=accs[s]: e.tensor_tensor(out=kdec[par][:, s, :], in0=a, in1=erev[:, s, :], op=ALU.mult),
                      reads=[akeys[s], "erev"], writes=[("kdec", par, s)])
            for h in range(2):
                accs, akeys = proj(w_in_v, OFF_V + (hp * 2 + h) * 256, (hT, "hT"), "tm")
                for s in range(NSUB):
                    mcol = ti * NSUB + s
                    S.add("act", lambda e, s=s, h=h, a=accs[s], mcol=mcol: e.activation(
                        out=vbuf[par][h][:, s, :], in_=a, func=AF.Identity, scale=mk[:, mcol:mcol + 1]),
                        reads=[akeys[s], "mk"], writes=[("vbuf", par, h, s)])
            if own:
                accs, akeys = proj(w_in_v, OFF_Q + hp * 256, (hT, "hT"), "fm")
                for h in range(2):
                    S.add("dve", lambda e, h=h, a=accs[h]: e.tensor_scalar(out=qTb[par][:, h, :], in0=a, scalar1=128.0 ** -0.5,
                                                                         scalar2=None, op0=ALU.mult),
                          reads=[akeys[h]], writes=[("qTb", par, h)])

        def stC(hp):
            par = hp % 2
            order = [(h, c) for h in range(2) for c in range(4)] if own else [(h, c) for c in range(4) for h in range(2)]
            for (h, c) in order:
                s, hf = c // 2, c % 2
                rows = slice(hf * 64, hf * 64 + 64)
                hg = hp * 2 + h
                kvb = 6 if (own or h == 0) else 7
                kvp = ps[kvb][:, 0:256]
                S.add("pe", lambda e, h=h, s=s, rows=rows, kvp=kvp: e.matmul(
                    kvp, lhsT=kdec[par][rows, s, h * 128:(h + 1) * 128], rhs=vbuf[par][h][rows, s, :], start=True, stop=True),
                    reads=[("kdec", par, s), ("vbuf", par, h, s)], writes=[("ps", kvb)])
                S.add("dve", lambda e, h=h, hg=hg, c=c, kvp=kvp: e.scalar_tensor_tensor(
                    out=Sst[:, hg, :], in0=Sst[:, hg, :], scalar=dTt[:, par * 8 + h * 4 + c:par * 8 + h * 4 + c + 1], in1=kvp,
                    op0=ALU.mult, op1=ALU.add),
                    reads=[("S", hg), ("dTt", par), ("ps", kvb)], writes=[("S", hg)])
                if own:
                    S.add("act", lambda e, h=h, hg=hg: e.activation(out=Sb[:, h, :], in_=Sst[:, hg, :], func=AF.Identity),
                          reads=[("S", hg)], writes=[("Sb", h)])

                    def mmo(e, h=h, c=c):
                        ins = None
                        for dvs in range(2):
                            ins = e.matmul(ps[7][:, dvs * 256 + c * 64: dvs * 256 + c * 64 + 64],
                                           lhsT=Sb[:, h, dvs * 128:(dvs + 1) * 128], rhs=qTb[par][:, h, c * 64:(c + 1) * 64],
                                           start=True, stop=True)
                        return ins
                    S.add("pe", mmo, reads=[("Sb", h), ("qTb", par, h)], writes=[("ps", 7)])
                if own and c == 3:
                    S.add("act", lambda e, h=h: e.activation(out=sqb[:].rearrange("p a t -> p (a t)"), in_=ps[7][:, :],
                                                            func=AF.Square), reads=[("ps", 7)], writes=["sqb"])

                    def mms(e):
                        ins = None
                        for dvs in range(2):
                            ins = e.matmul(ps[5][:, 256:512], lhsT=onesb, rhs=sqb[:, dvs, :], start=(dvs == 0), stop=(dvs == 1))
                        return ins
                    S.add("pe", mms, reads=["sqb", "cstb"], writes=[("ps", 5)])
                    S.add("dve", lambda e: e.tensor_scalar(out=lnv[:], in0=ps[5][:, 256:512], scalar1=1.0 / 256.0,
                                                           scalar2=RMS_EPS, op0=ALU.mult, op1=ALU.add),
                          reads=[("ps", 5)], writes=["lnv"])
                    S.add("act", lambda e: e.activation(out=lnv[:], in_=lnv[:], func=AF.Ln), reads=["lnv"], writes=["lnv"])
                    S.add("act", lambda e: e.activation(out=rstd[:], in_=lnv[:], func=AF.Exp, scale=-0.5),
                          reads=["lnv"], writes=["rstd"])
                    accs, akeys = proj(w_in_v, OFF_G + hg * 256, (hT, "hT"), "fm")
                    for dvs in range(2):
                        S.add("act", lambda e, dvs=dvs, a=accs[dvs]: e.activation(out=tmpa[dvs][:], in_=a, func=AF.Silu),
                              reads=[akeys[dvs]], writes=[("tmpa", dvs)])
                        S.add("dve", lambda e, h=h, dvs=dvs: e.scalar_tensor_tensor(
                            out=tmpb[dvs][:], in0=ps[7][:, dvs * 256:(dvs + 1) * 256], scalar=gn[:, dvs:dvs + 1],
                            in1=rstd[:], op0=ALU.mult, op1=ALU.mult),
                            reads=[("ps", 7), "gn", "rstd"], writes=[("tmpb", dvs)])
                        S.add("dve", lambda e, hg=hg, dvs=dvs: e.tensor_tensor(
                            out=B2[:, hg * 2 + dvs, :], in0=tmpb[dvs][:], in1=tmpa[dvs][:], op=ALU.mult),
                            reads=[("tmpb", dvs), ("tmpa", dvs)], writes=["B2"])

        for hp in range(9):
            if hp < 8:
                stA1(hp)
            if hp >= 1:
                stC(hp - 1)
            if hp < 8:
                stA2(hp)
                stB(hp)

        if stage == 2:
            if own:
                for kc in (0, 1, 31):
                    S.add("dve", lambda e, kc=kc: e.tensor_copy(out=tmpa[0][:], in_=B2[:, kc, :]), reads=["B2"], writes=[("tmpa", 0)])
                    dump(tmpa[0][:], ("tmpa", 0), T)
            continue
        if not own and not last_pre:
            continue

        if own:
            for d2 in range(16):
                accy, ky = proj(wgo_v, d2 * 256, (B2, "B2"), "fm", "wgo")
                accg, kg = proj(w_in_v, OFF_GATE + d2 * 256, (hT, "hT"), "fm")
                for a in range(2):
                    dc = d2 * 2 + a
                    S.add("act", lambda e, a=a, ag=accg[a]: e.activation(out=tmpa[a][:], in_=ag, func=AF.Sigmoid),
                          reads=[kg[a]], writes=[("tmpa", a)])
                    S.add("dve", lambda e, a=a, dc=dc, ay=accy[a]: e.tensor_tensor(out=B3[:, dc, :], in0=ay, in1=tmpa[a][:],
                                                                                 op=ALU.mult),
                          reads=[ky[a], ("tmpa", a)], writes=["B3"])
        for cp in range(16):
            acca, ka = proj(w_in_v, OFF_C + cp * 256, (hT, "hT"), "fm")
            for a in range(2):
                S.add("act", lambda e, a=a, aa=acca[a]: e.activation(out=tmpa[a][:], in_=aa, func=AF.Identity),
                      reads=[ka[a]], writes=[("tmpa", a)])
            accb, kb_ = proj(w_in_v, OFF_C + D + cp * 256, (hT, "hT"), "fm")
            for a in range(2):
                kc = cp * 2 + a
                S.add("act", lambda e, a=a, ab=accb[a]: e.activation(out=tmpb[a][:], in_=ab, func=AF.Sigmoid),
                      reads=[kb_[a]], writes=[("tmpb", a)])
                S.add("dve", lambda e, a=a, kc=kc: e.tensor_copy(out=cgb[a][:, 0:32], in_=halo[:, kc, :]),
                      reads=["halo"], writes=[("cgb", a)])
                S.add("dve", lambda e, a=a: e.tensor_tensor(out=cgb[a][:, 32:32 + T], in0=tmpa[a][:], in1=tmpb[a][:], op=ALU.mult),
                      reads=[("tmpa", a), ("tmpb", a)], writes=[("cgb", a)])
                if last_pre:
                    S.add("dve", lambda e, a=a: e.tensor_tensor(out=cgb[a][:, 32:32 + T], in0=cgb[a][:, 32:32 + T], in1=mkrep[:],
                                                                op=ALU.mult), reads=[("cgb", a), "mkrep"], writes=[("cgb", a)])
                S.add("act", lambda e, a=a, kc=kc: e.activation(out=halo[:, kc, :], in_=cgb[a][:, T:T + 32], func=AF.Identity),
                      reads=[("cgb", a)], writes=["halo"])
            if own:
                for k in range(31):
                    for a in range(2):
                        kc = cp * 2 + a
                        if k == 0:
                            S.add("dve", lambda e, a=a, kc=kc: e.tensor_scalar(
                                out=B4[:, kc, :], in0=cgb[a][:, 2:2 + T], scalar1=cw[:, kc, 0:1], scalar2=vec[:, VCB, kc:kc + 1],
                                op0=ALU.mult, op1=ALU.add), reads=[("cgb", a), "cw", "vec"], writes=[("B4", kc)])
                        else:
                            S.add("dve", lambda e, a=a, kc=kc, k=k: e.scalar_tensor_tensor(
                                out=B4[:, kc, :], in0=cgb[a][:, 2 + k:2 + k + T], scalar=cw[:, kc, k:k + 1], in1=B4[:, kc, :],
                                op0=ALU.mult, op1=ALU.add), reads=[("cgb", a), "cw", ("B4", kc)], writes=[("B4", kc)])
        if not own:
            continue
        b4keys = [("B4", kc) for kc in range(KC)]
        S.add("dve", lambda e: e.engine_nop(), reads=b4keys, writes=["B4"])
        ln_stats(B4, "B4")
        for kc in range(KC):
            ln_apply(B4, "B4", kc, B2[:, kc, :], "B2", VCG, VCBB, func=AF.Silu)
        for d2 in range(16):
            accy, ky = proj(wco_v, d2 * 256, (B2, "B2"), "fm", "wco")
            accg, kg = proj(w_in_v, OFF_GATE + D + d2 * 256, (hT, "hT"), "fm")
            for a in range(2):
                dc = d2 * 2 + a
                S.add("act", lambda e, a=a, ag=accg[a]: e.activation(out=tmpa[a][:], in_=ag, func=AF.Sigmoid),
                      reads=[kg[a]], writes=[("tmpa", a)])
                S.add("dve", lambda e, a=a, dc=dc, ay=accy[a]: e.scalar_tensor_tensor(
                    out=tmpb[a][:], in0=ay, scalar=vec[:, VBCO, dc:dc + 1], in1=tmpa[a][:], op0=ALU.add, op1=ALU.mult),
                    reads=[ky[a], ("tmpa", a), "vec"], writes=[("tmpb", a)])
                S.add("dve", lambda e, a=a, dc=dc: e.tensor_tensor(out=B3[:, dc, :], in0=B3[:, dc, :], in1=tmpb[a][:], op=ALU.add),
                      reads=["B3", ("tmpb", a)], writes=["B3"])
        for d2 in range(16):
            accy, ky = proj(wout_v, d2 * 256, (B3, "B3"), "fm", "wout")
            for a in range(2):
                dc = d2 * 2 + a
                S.add("dve", lambda e, a=a, dc=dc, ay=accy[a]: e.scalar_tensor_tensor(
                    out=B4[:, dc, :], in0=hT[:, dc, :], scalar=ALPHA, in1=ay, op0=ALU.mult, op1=ALU.add),
                    reads=["hT", ky[a]], writes=["B4"])
        ln_stats(B4, "B4")
        for kc in range(KC):
            ln_apply(B4, "B4", kc, hT[:, kc, :], "hT", VG1, VB1)

        if stage == 3:
            for kc in (0, 31):
                S.add("dve", lambda e, kc=kc: e.tensor_copy(out=tmpa[0][:], in_=hT[:, kc, :]), reads=["hT"], writes=[("tmpa", 0)])
                dump(tmpa[0][:], ("tmpa", 0), T)
            continue
        for p in range(8):
            accq, kq = proj(wq_v, p * 256, (hT, "hT"), "fm", "wq")
            pq = pqT[p % 2]
            for a in range(2):
                S.add(evac_engine() if False else "act", lambda e, a=a, aq=accq[a], pq=pq: e.activation(out=pq[:, a, :], in_=aq, func=AF.Identity),
                      reads=[kq[a]], writes=[("pqT", p % 2, a)])

            def mmsc(e, p=p, pq=pq):
                ins = None
                for s in range(NSUB):
                    for hfq in range(2):
                        c0 = (s * 2 + hfq) * 128
                        ins = e.matmul(ps[4][:, c0:c0 + 128], lhsT=pq[:, hfq, s * 128:(s + 1) * 128],
                                       rhs=keyb[:, p * 2 + hfq, :], start=True, stop=True)
                return ins
            S.add("pe", mmsc, reads=[("pqT", p % 2, 0), ("pqT", p % 2, 1), "keyb"], writes=[("ps", 4)])
            for s in range(NSUB):
                S.add("act" if s == 0 else "dve", (lambda e, s=s, p=p: e.activation(
                    out=sc[:, s, :, p, :], in_=ps[4][:, s * 256:(s + 1) * 256].rearrange("q (a n) -> q a n", a=2), func=AF.Identity))
                    if s == 0 else (lambda e, s=s, p=p: e.tensor_copy(
                        out=sc[:, s, :, p, :], in_=ps[4][:, s * 256:(s + 1) * 256].rearrange("q (a n) -> q a n", a=2))),
                    reads=[("ps", 4)], writes=["xbuf"])
        for s in range(NSUB):
            def top_a(e, s=s):
                ins = None
                for hfq in range(2):
                    for p in range(8):
                        ins = e.max(out=v16[:, hfq, p, 0:8], in_=sc[:, s, hfq, p, :])
                return ins
            S.add("dve", top_a, reads=["xbuf"], writes=["v16a"])

            def top_b(e, s=s):
                ins = None
                for hfq in range(2):
                    for p in range(8):
                        ins = e.match_replace(out=scw[:, hfq * 8 + p, :], in_to_replace=v16[:, hfq, p, 0:8],
                                              in_values=sc[:, s, hfq, p, :], imm_value=NEG)
                return ins
            S.add("dve", top_b, reads=["xbuf", "v16a"], writes=["B2"])

            def top_c(e):
                ins = None
                for hfq in range(2):
                    for p in range(8):
                        ins = e.max(out=v16[:, hfq, p, 8:16], in_=scw[:, hfq * 8 + p, :])
                return ins
            S.add("dve", top_c, reads=["B2"], writes=["v16b"])

            def mkcand(e):
                ins = None
                for p in range(8):
                    ins = e.tensor_tensor(out=cand[:, p, :].rearrange("q (a b) -> q a b", a=16),
                                          in0=v16[:, 0, p, :].unsqueeze(2).broadcast_to([128, 16, 16]),
                                          in1=v16[:, 1, p, :].unsqueeze(1).broadcast_to([128, 16, 16]), op=ALU.add)
                return ins
            S.add("dve", mkcand, reads=["v16a", "v16b"], writes=["B2"])

            def ctop_a(e):
                ins = None
                for p in range(8):
                    ins = e.max(out=t16[:, p, 0:8], in_=cand[:, p, :])
                return ins
            S.add("dve", ctop_a, reads=["B2"], writes=["t16a"])

            def ctop_b(e):
                ins = None
                for p in range(8):
                    ins = e.match_replace(out=cand[:, p, :], in_to_replace=t16[:, p, 0:8], in_values=cand[:, p, :], imm_value=NEG)
                return ins
            S.add("dve", ctop_b, reads=["B2", "t16a"], writes=["B2"])

            def ctop_c(e):
                ins = None
                for p in range(8):
                    ins = e.max(out=t16[:, p, 8:16], in_=cand[:, p, :])
                return ins
            S.add("dve", ctop_c, reads=["B2"], writes=["t16b"])
            S.add("dve", lambda e: e.tensor_tensor(out=e16[:], in0=t16[:], in1=t16[:, :, 0:1].broadcast_to([128, 8, 16]),
                                                   op=ALU.subtract), reads=["t16a", "t16b"], writes=["e16"])
            S.add("act", lambda e: e.activation(out=e16[:], in_=e16[:], func=AF.Exp), reads=["e16"], writes=["e16"])
            S.add("dve", lambda e: e.tensor_reduce(out=zz[:], in_=e16[:], axis=AX.X, op=ALU.add), reads=["e16"], writes=["zz"])
            S.add("act", lambda e: e.activation(out=zz[:], in_=zz[:], func=AF.Ln), reads=["zz"], writes=["zz"])
            S.add("dve", lambda e: e.tensor_tensor(out=mz[:], in0=zz[:], in1=t16[:, :, 0], op=ALU.add),
                  reads=["zz", "t16a"], writes=["mz"])
            S.add("dve", lambda e: e.tensor_scalar(out=taue[:], in0=t16[:, :, 15], scalar1=-TOPK_EPS, scalar2=None, op0=ALU.add),
                  reads=["t16b"], writes=["taue"])
            S.add("dve", lambda e: e.tensor_tensor(out=taue[:], in0=taue[:], in1=mz[:], op=ALU.subtract),
                  reads=["taue", "mz"], writes=["taue"])
            S.add("dve", lambda e, s=s: e.tensor_tensor(out=thr[:, s, :, :], in0=taue[:].unsqueeze(2).broadcast_to([128, 8, 128]),
                                                        in1=sc[:, s, 0, :, :], op=ALU.subtract),
                  reads=["taue", "xbuf"], writes=[("thr", s)])
            S.add("dve", lambda e, s=s: e.tensor_tensor(out=sc[:, s, 1, :, :], in0=sc[:, s, 1, :, :],
                                                        in1=mz[:].unsqueeze(2).broadcast_to([128, 8, 128]), op=ALU.subtract),
                  reads=["mz", "xbuf"], writes=["xbuf"])
            S.add("act", lambda e, s=s: e.activation(out=thr[:, s, 0:NB, :], in_=thr[:, s, 0:NB, :], func=AF.Exp),
                  reads=[("thr", s)], writes=[("thr", s)])
            S.add("act", lambda e, s=s: e.activation(out=sc[:, s, 1, 0:NB, :], in_=sc[:, s, 1, 0:NB, :], func=AF.Exp),
                  reads=["xbuf"], writes=["xbuf"])
            S.add("act", lambda e, s=s: e.activation(out=sc[:, s, 0, 0:NB, :], in_=sc[:, s, 0, 0:NB, :], func=AF.Exp),
                  reads=["xbuf"], writes=["xbuf"])
        ei = 0
        for qtr in range(4):
            pend = proj(uT_v, (qtr * 16) * 256, (hT, "hT"), "fm", "uT")
            for i2 in range(16):
                accu, ku = pend
                if i2 + 1 < 16:
                    pend = proj(uT_v, (qtr * 16 + i2 + 1) * 256, (hT, "hT"), "fm", "uT")
                for a in range(2):
                    i = (qtr * 16 + i2) * 2 + a
                    il = i2 * 2 + a
                    wtb = 6 + (i % 2)
                    for s in range(NSUB):
                        wsel = (i * NSUB + s) % 4
                        wb = wpb[wsel]
                        wkeys = [("wpb", wsel, p) for p in range(8)]
                        for p in range(NB):
                            S.add("dve", lambda e, wb=wb, s=s, p=p, i=i: e.scalar_tensor_tensor(
                                out=wb[:, p, :], in0=sc[:, s, 1, p, :], scalar=thr[:, s, p, i:i + 1], in1=sc[:, s, 1, p, :],
                                op0=ALU.is_ge, op1=ALU.mult), reads=["xbuf", ("thr", s)], writes=[wkeys[p]])
                        for p in range(NB, 8):
                            et = etmp[ei % 4]
                            ek = ("etmp", ei % 4)
                            ei += 1
                            S.add("act", lambda e, et=et, s=s, p=p, i=i: e.activation(
                                out=et[:], in_=sc[:, s, 1, p, :], func=AF.Exp, bias=sc[:, s, 0, p, i:i + 1], scale=1.0),
                                reads=["xbuf"], writes=[ek])
                            S.add("dve", lambda e, et=et, wb=wb, s=s, p=p, i=i: e.scalar_tensor_tensor(
                                out=wb[:, p, :], in0=sc[:, s, 1, p, :], scalar=thr[:, s, p, i:i + 1], in1=et[:],
                                op0=ALU.is_ge, op1=ALU.mult), reads=["xbuf", ("thr", s), ek], writes=[wkeys[p]])
                        for p in range(NB):
                            S.add("dve", lambda e, wb=wb, s=s, p=p, i=i: e.tensor_scalar(
                                out=wb[:, p, :], in0=wb[:, p, :], scalar1=sc[:, s, 0, p, i:i + 1], scalar2=None, op0=ALU.mult),
                                reads=["xbuf", wkeys[p]], writes=[wkeys[p]])

                        def mmw(e, wb=wb, s=s, wtb=wtb):
                            ins = None
                            for p in range(8):
                                ins = e.matmul(ps[wtb][:, s * 128:(s + 1) * 128], lhsT=wb[:, p, :], rhs=identb,
                                               start=(p == 0), stop=(p == 7))
                            return ins
                        S.add("pe", mmw, reads=wkeys + ["cstb"], writes=[("ps", wtb)])
                    S.add("act", lambda e, a=a, au=accu[a]: e.activation(out=tmpa[a][:], in_=au, func=AF.Gelu),
                          reads=[ku[a]], writes=[("tmpa", a)])
                    S.add("dve", lambda e, a=a, il=il, wtb=wtb: e.tensor_tensor(out=B3[:, il, :], in0=tmpa[a][:], in1=ps[wtb][:, 0:T], op=ALU.mult),
                          reads=[("tmpa", a), ("ps", wtb)], writes=["B3"])
            for d2 in range(16):
                accv, kv_ = proj(vt_v[qtr], d2 * 256, (B3, "B3"), "fm", ("v", qtr))
                for a in range(2):
                    dc = d2 * 2 + a
                    if qtr == 0:
                        S.add("dve", lambda e, dc=dc, av=accv[a]: e.scalar_tensor_tensor(
                            out=B4[:, dc, :], in0=hT[:, dc, :], scalar=ALPHA, in1=av, op0=ALU.mult, op1=ALU.add),
                            reads=["hT", kv_[a]], writes=["B4"])
                    else:
                        S.add("dve", lambda e, dc=dc, av=accv[a]: e.tensor_tensor(out=B4[:, dc, :], in0=B4[:, dc, :], in1=av, op=ALU.add),
                              reads=["B4", kv_[a]], writes=["B4"])
        ln_stats(B4, "B4")
        for kc in range(KC):
            ln_apply(B4, "B4", kc, B4[:, kc, :], "B4", VG2, VB2)
        for s in range(NSUB):
            for g4 in range(8):
                bank = 4 + (g4 % 2)

                def tr2(e, g4=g4, bank=bank, s=s):
                    ins = None
                    for q in range(4):
                        kc = g4 * 4 + q
                        ins = e.transpose(out=ps[bank][:, q * 128:(q + 1) * 128], in_=B4[:, kc, s * 128:(s + 1) * 128],
                                          identity=identf)
                    return ins
                S.add("pe", tr2, reads=["B4", "cst"], writes=[("ps", bank)])
                if g4 % 2 == 0:
                    S.add("act", lambda e, g4=g4, bank=bank: e.activation(out=xbuf[:, g4 * 512:(g4 + 1) * 512], in_=ps[bank][:, :], func=AF.Identity),
                          reads=[("ps", bank)], writes=["xbuf"])
                else:
                    S.add("dve", lambda e, g4=g4, bank=bank: e.tensor_copy(out=xbuf[:, g4 * 512:(g4 + 1) * 512], in_=ps[bank][:, :]),
                          reads=[("ps", bank)], writes=["xbuf"])
            r0 = (ti - n_pre) * T + s * 128
            S.add("sp", lambda e, r0=r0: e.dma_start(out=out[r0:r0 + 128, :], in_=xbuf[:]), reads=["xbuf"], dma=("out", 0))

    S.emit(nc, es)
    es.close()
    return nc


def _consts():
    c = np.zeros((128, 4, 128), np.float32)
    c[:, 0, :] = np.eye(128, dtype=np.float32)
    c[:, 1, :] = 1.0
    j = np.arange(128)[:, None]
    cc = np.arange(128)[None, :]
    c[:, 2, :] = ((j // 64 == cc // 64) & (j > cc)).astype(np.float32)
    c[:, 3, 0] = (np.arange(128) < 64)
    c[:, 3, 1] = (np.arange(128) >= 64)
    return c


def _chunkvec(v):
    return np.ascontiguousarray(np.asarray(v, np.float32).reshape(KC, 128).T)


def prepare(inputs, n_pre=NPRE_TILES, n_own=NOWN_TILES, cores=range(8)):
    x = np.asarray(inputs["x"], np.float32)
    meta = np.asarray(inputs["meta"], np.float32)
    f = lambda k: np.asarray(inputs[k], np.float32)
    shared = {
        "w_in": f("w_in")[0],
        "wa2": np.ascontiguousarray(np.concatenate([f("w_a2")[0], f("b_a")[0][None, :]], axis=0)),
        "gng": np.ascontiguousarray(f("gla_norm_g")[0].reshape(2, 128).T),
        "w_gla_o": f("w_gla_o")[0],
        "conv_w": np.ascontiguousarray(f("conv_w")[0].T.reshape(KC, 128, 31).transpose(1, 0, 2)),
        "w_conv_o": f("w_conv_o")[0],
        "w_out": f("w_out")[0],
        "wq": f("peer_wq")[0],
        "keysT": np.ascontiguousarray(f("peer_keys")[0].reshape(16, 128, 128).transpose(2, 0, 1)),
        "uT": np.ascontiguousarray(f("peer_u")[0].T),
        "vtab": f("peer_v")[0],
        "consts": _consts(),
    }
    vecs = np.zeros((128, 11, KC), np.float32)
    for i, v in enumerate([f("ln0_g"), f("ln0_b"), f("conv_b")[0], f("conv_ln_g")[0], f("conv_ln_b")[0], f("b_conv_o")[0],
                           f("ln1_g")[0], f("ln1_b")[0], f("ln2_g")[0], f("ln2_b")[0]]):
        vecs[:, i, :] = _chunkvec(v)
    shared["vecs"] = vecs
    npre_tok = n_pre * T
    ntok = (n_pre + n_own) * T
    maps = []
    for c in cores:
        b, j = c // 4, c % 4
        xs = np.zeros((ntok, D), np.float32)
        mask = np.zeros((ntok,), np.float32)
        nvalid = j * SEG
        own0 = npre_tok
        if nvalid > 0:
            xs[own0 - nvalid:own0] = x[b, 0:nvalid]
            mask[own0 - nvalid:own0] = 1.0
        m0 = own0 - nvalid - 16
        xs[m0:m0 + 16] = meta
        mask[m0:m0 + 16] = 1.0
        xs[own0:own0 + n_own * T] = x[b, j * SEG:j * SEG + n_own * T]
        mask[own0:] = 1.0
        mt = np.ascontiguousarray(mask.reshape(-1, 128).T)
        lp = n_pre - 1
        mrep = np.ascontiguousarray(np.broadcast_to(mask[lp * T:(lp + 1) * T][None, :], (128, T)))
        d = dict(shared)
        d.update({"xs": xs, "maskt": mt, "maskrep": mrep})
        maps.append(d)
    return maps


def kernel(**inputs):
    nc = build()
    maps = prepare(inputs)
    res = run_bass_kernel_spmd(nc, maps, core_ids=list(range(8)))
    outp = np.zeros((2, 8192, D), np.float32)
    for c in range(8):
        b, j = c // 4, c % 4
        outp[b, j * SEG:(j + 1) * SEG] = res.results[c]["out"]
    return outp
```
